# Optimizing a Trainium2 kernel written in Bass

```python
import math
import jax
import jax.numpy as jnp
from jax import lax
import numpy as np

D_MODEL = 2048
BATCH = 16
SEQ = 256
DEPTH = 2
DEC_BATCH = 4
DEC_SEQ = 4096
PAST_LEN = 256

GRID_W = 64
ROPE_DIM = 64
ROPE_BASE = 10000.0
RMS_EPS = 1e-6
ATTN_BLOCK = 128
ML_HEADS = 4
ML_DH = 256
ML_W = ML_HEADS * ML_DH
ML_CHUNK = 64
MLA_HEADS = 8
MLA_NOPE = 128
MLA_ROPE = ROPE_DIM
MLA_V = 128
MLA_KV_RANK = 512
MLA_W = MLA_HEADS * MLA_V
S5_GROUP = 16
S5_GROUPS = 64
S5_W = S5_GROUPS * S5_GROUP
S5_P = 64
S5_CHUNK = 128
DF_HEADS = 8
DF_DQK = ROPE_DIM
DF_DV = 2 * DF_DQK
DF_W = DF_HEADS * DF_DV
N_BRANCH = 4
BRANCH_W = 1024
FFN_HIDDEN = (8 * D_MODEL + 3 * 256 - 1) // (3 * 256) * 256
IN_SIZES = (ML_W, ML_W, ML_W, ML_W, 4 * ML_HEADS,
            MLA_HEADS * (MLA_NOPE + MLA_ROPE), MLA_KV_RANK + MLA_ROPE,
            S5_W,
            DF_HEADS * 2 * DF_DQK, DF_HEADS * 2 * DF_DQK, DF_HEADS * DF_DV,
            N_BRANCH * D_MODEL)
IN_SPLITS = tuple(int(s) for s in np.cumsum(IN_SIZES)[:-1])
IN_COLS = int(sum(IN_SIZES))

kernel_name = 'hybrid_mlstm_mla_s5_diffattn_prefix_dit_step'


def _rmsnorm(x, g):
    xf = x.astype(jnp.float32)
    y = xf * lax.rsqrt(jnp.mean(xf * xf, axis=-1, keepdims=True) + RMS_EPS)
    return (y * g.astype(jnp.float32)).astype(x.dtype)


def _map_query_blocks(fn, q):
    bsz, n = q.shape[0], q.shape[1]
    nb = n // ATTN_BLOCK
    qb = jnp.moveaxis(q.reshape((bsz, nb, ATTN_BLOCK) + q.shape[2:]), 1, 0)
    out = jnp.moveaxis(lax.map(fn, qb), 0, 1)
    return out.reshape((bsz, n) + out.shape[3:])


def _softmax_attend(q, k, v, scale):
    def blk(qb):
        s = jnp.einsum('bqhd,bkhd->bhqk', qb, k).astype(jnp.float32) * scale
        p = jax.nn.softmax(s, axis=-1).astype(v.dtype)
        return jnp.einsum('bhqk,bkhd->bqhd', p, v)
    return _map_query_blocks(blk, q)


def _diff_attend(q, k, v, lam, scale):
    def blk(qb):
        s = jnp.einsum('bqhcd,bkhcd->bchqk', qb, k).astype(jnp.float32) * scale
        p = jax.nn.softmax(s, axis=-1)
        pd = (p[:, 0] - lam * p[:, 1]).astype(v.dtype)
        return jnp.einsum('bhqk,bkhd->bqhd', pd, v)
    return _map_query_blocks(blk, q)


def _axial_rope_tables(n_tok):
    grid_rows = n_tok // GRID_W
    rows, cols = jnp.meshgrid(jnp.arange(grid_rows, dtype=jnp.float32),
                              jnp.arange(GRID_W, dtype=jnp.float32), indexing='ij')
    quarter = ROPE_DIM // 4
    inv = ROPE_BASE ** (-jnp.arange(quarter, dtype=jnp.float32) / quarter)
    ang_r = rows.reshape(-1, 1) * inv
    ang_c = cols.reshape(-1, 1) * inv
    return (jnp.cos(ang_r), jnp.sin(ang_r), jnp.cos(ang_c), jnp.sin(ang_c))


def _rotate(z, cos, sin):
    z1, z2 = jnp.split(z, 2, axis=-1)
    return jnp.concatenate([z1 * cos - z2 * sin, z2 * cos + z1 * sin], axis=-1)


def _apply_axial_rope(x, tables):
    cos_r, sin_r, cos_c, sin_c = (t[:, None, :] for t in tables)
    x_row, x_col = jnp.split(x.astype(jnp.float32), 2, axis=-1)
    out = jnp.concatenate([_rotate(x_row, cos_r, sin_r), _rotate(x_col, cos_c, sin_c)], axis=-1)
    return out.astype(x.dtype)


def _mlstm_dir(q, k, v, log_i, log_f, c0, n0, m0):
    bsz, n = q.shape[0], q.shape[1]
    nc = n // ML_CHUNK
    tril = jnp.tril(jnp.ones((ML_CHUNK, ML_CHUNK), dtype=bool))

    def chunks(a):
        return jnp.moveaxis(a.reshape((bsz, nc, ML_CHUNK) + a.shape[2:]), 1, 0)

    def body(carry, xs):
        c_st, n_st, m_st = carry
        qc, kc, vc, li, lf = xs
        b = jnp.cumsum(lf, axis=1)
        log_d = b[:, :, None, :] - b[:, None, :, :] + li[:, None, :, :]
        log_d = jnp.where(tril[None, :, :, None], log_d, -jnp.inf)
        log_inter = b + m_st[:, None, :]
        m_t = jnp.maximum(log_inter, jnp.max(log_d, axis=2))
        w_d = jnp.exp(log_d - m_t[:, :, None, :])
        w_inter = jnp.exp(log_inter - m_t)
        s = jnp.einsum('bthd,bshd->btsh', qc, kc) * w_d
        num = (jnp.einsum('btsh,bshd->bthd', s, vc)
               + w_inter[..., None] * jnp.einsum('bhvk,bthk->bthv', c_st, qc))
        den = jnp.sum(s, axis=2) + w_inter * jnp.einsum('bhk,bthk->bth', n_st, qc)
        h = num / jnp.maximum(jnp.abs(den), jnp.exp(-m_t))[..., None]
        b_last = b[:, -1, :]
        log_w = b_last[:, None, :] - b + li
        m_new = jnp.maximum(b_last + m_st, jnp.max(log_w, axis=1))
        w_s = jnp.exp(log_w - m_new[:, None, :])
        w_c = jnp.exp(b_last + m_st - m_new)
        c_new = w_c[..., None, None] * c_st + jnp.einsum('bshv,bshk->bhvk', vc * w_s[..., None], kc)
        n_new = w_c[..., None] * n_st + jnp.einsum('bsh,bshk->bhk', w_s, kc)
        return (c_new, n_new, m_new), h

    (c_f, n_f, m_f), h = lax.scan(body, (c0, n0, m0),
                                  (chunks(q), chunks(k), chunks(v), chunks(log_i), chunks(log_f)))
    h = jnp.moveaxis(h, 0, 1).reshape(q.shape)
    return h, c_f, n_f, m_f


def _mlstm_bidir(q, k, v, log_i, log_f, c0, n0, m0):
    h_f, c_f, n_f, m_f = _mlstm_dir(q, k, v, log_i[:, :, 0], log_f[:, :, 0], c0[:, 0], n0[:, 0], m0[:, 0])
    h_b, c_b, n_b, m_b = _mlstm_dir(jnp.flip(q, 1), jnp.flip(k, 1), jnp.flip(v, 1),
                                    jnp.flip(log_i[:, :, 1], 1), jnp.flip(log_f[:, :, 1], 1),
                                    c0[:, 1], n0[:, 1], m0[:, 1])
    return (h_f + jnp.flip(h_b, 1), jnp.stack([c_f, c_b], axis=1),
            jnp.stack([n_f, n_b], axis=1), jnp.stack([m_f, m_b], axis=1))


def _complex_affine_combine(e1, e2):
    a1r, a1i, b1r, b1i = e1
    a2r, a2i, b2r, b2i = e2
    return (a2r * a1r - a2i * a1i, a2r * a1i + a2i * a1r,
            a2r * b1r - a2i * b1i + b2r, a2r * b1i + a2i * b1r + b2i)


def _s5_discretise(a_re, a_im, log_dt, b_re, b_im):
    lr = jnp.minimum(a_re.astype(jnp.float32), -1e-4)
    li = a_im.astype(jnp.float32)
    dt = jnp.exp(log_dt.astype(jnp.float32))[:, None]
    mag = jnp.exp(dt * lr)
    ab_re, ab_im = mag * jnp.cos(dt * li), mag * jnp.sin(dt * li)
    den = lr * lr + li * li
    nr, ni = ab_re - 1.0, ab_im
    qr = (nr * lr + ni * li) / den
    qi = (ni * lr - nr * li) / den
    b_re, b_im = b_re.astype(jnp.float32), b_im.astype(jnp.float32)
    bb_re = qr[..., None] * b_re - qi[..., None] * b_im
    bb_im = qr[..., None] * b_im + qi[..., None] * b_re
    return ab_re, ab_im, bb_re, bb_im


def _s5_dir(u, ab_re, ab_im, bb_re, bb_im, c_re, c_im, h_re, h_im):
    bsz, n = u.shape[0], u.shape[1]
    nc = n // S5_CHUNK
    uc = jnp.moveaxis(u.reshape((bsz, nc, S5_CHUNK) + u.shape[2:]), 1, 0)

    def body(carry, u_blk):
        hr, hi = carry
        bu_re = jnp.einsum('gpi,blgi->blgp', bb_re, u_blk)
        bu_im = jnp.einsum('gpi,blgi->blgp', bb_im, u_blk)
        bu_re = bu_re.at[:, 0].add(ab_re * hr - ab_im * hi)
        bu_im = bu_im.at[:, 0].add(ab_re * hi + ab_im * hr)
        a_re = jnp.broadcast_to(ab_re, bu_re.shape)
        a_im = jnp.broadcast_to(ab_im, bu_im.shape)
        _, _, sr, si = lax.associative_scan(_complex_affine_combine, (a_re, a_im, bu_re, bu_im), axis=1)
        y = jnp.einsum('gip,blgp->blgi', c_re, sr) - jnp.einsum('gip,blgp->blgi', c_im, si)
        return (sr[:, -1], si[:, -1]), y

    (hr_f, hi_f), y = lax.scan(body, (h_re, h_im), uc)
    return jnp.moveaxis(y, 0, 1).reshape(u.shape), hr_f, hi_f


def _s5_bidir(u, a_re, a_im, log_dt, b_re, b_im, c_re, c_im, h0_re, h0_im):
    ab_re, ab_im, bb_re, bb_im = _s5_discretise(a_re[0], a_im[0], log_dt[0], b_re[0], b_im[0])
    y_f, hr_f, hi_f = _s5_dir(u, ab_re, ab_im, bb_re, bb_im, c_re[0].astype(jnp.float32),
                              c_im[0].astype(jnp.float32), h0_re[:, 0], h0_im[:, 0])
    ab_re, ab_im, bb_re, bb_im = _s5_discretise(a_re[1], a_im[1], log_dt[1], b_re[1], b_im[1])
    y_b, hr_b, hi_b = _s5_dir(jnp.flip(u, 1), ab_re, ab_im, bb_re, bb_im, c_re[1].astype(jnp.float32),
                              c_im[1].astype(jnp.float32), h0_re[:, 1], h0_im[:, 1])
    return (y_f + jnp.flip(y_b, 1), jnp.stack([hr_f, hr_b], axis=1), jnp.stack([hi_f, hi_b], axis=1))


def _mixer(h, lp, rope, ctx):
    f32 = jnp.float32
    bsz, n, _ = h.shape
    (ml_q, ml_k, ml_v, ml_o, ml_if, mla_q, mla_kva, s5_u,
     df_q, df_k, df_v, gate_pre) = jnp.split(h @ lp['w_in'], IN_SPLITS, axis=-1)
    latent = ctx is not None
    if latent:
        (ckv_c, krope_c, dk_c, dv_c, ml_c0, ml_n0, ml_m0, s5_h0r, s5_h0i) = ctx
        ml_c0, ml_n0, ml_m0 = ml_c0.astype(f32), ml_n0.astype(f32), ml_m0.astype(f32)
        s5_h0r, s5_h0i = s5_h0r.astype(f32), s5_h0i.astype(f32)
    else:
        ml_c0 = jnp.zeros((bsz, 2, ML_HEADS, ML_DH, ML_DH), f32)
        ml_n0 = jnp.zeros((bsz, 2, ML_HEADS, ML_DH), f32)
        ml_m0 = jnp.zeros((bsz, 2, ML_HEADS), f32)
        s5_h0r = jnp.zeros((bsz, 2, S5_GROUPS, S5_P), f32)
        s5_h0i = jnp.zeros((bsz, 2, S5_GROUPS, S5_P), f32)

    q = ml_q.reshape(bsz, n, ML_HEADS, ML_DH).astype(f32)
    k = ml_k.reshape(bsz, n, ML_HEADS, ML_DH).astype(f32) * (ML_DH ** -0.5)
    v = ml_v.reshape(bsz, n, ML_HEADS, ML_DH).astype(f32)
    gpre = ml_if.reshape(bsz, n, 2, 2, ML_HEADS).astype(f32) + lp['ml_if_bias'].astype(f32)
    log_i = gpre[:, :, :, 0]
    log_f = jax.nn.log_sigmoid(gpre[:, :, :, 1])
    h_ml, ml_c, ml_n, ml_m = _mlstm_bidir(q, k, v, log_i, log_f, ml_c0, ml_n0, ml_m0)
    h_ml = _rmsnorm(h_ml, lp['ml_norm'].reshape(ML_HEADS, ML_DH)).astype(h.dtype)
    y_ml = (h_ml * jax.nn.sigmoid(ml_o.reshape(bsz, n, ML_HEADS, ML_DH))).reshape(bsz, n, ML_W)

    qm = mla_q.reshape(bsz, n, MLA_HEADS, MLA_NOPE + MLA_ROPE)
    q_nope, q_rope = qm[..., :MLA_NOPE], qm[..., MLA_NOPE:]
    ckv = _rmsnorm(mla_kva[..., :MLA_KV_RANK], lp['mla_kv_norm'])
    krope = mla_kva[..., MLA_KV_RANK:]
    if latent:
        q_rope = _apply_axial_rope(q_rope, rope)
        krope_lat = _apply_axial_rope(krope[:, :, None, :], rope)[:, :, 0, :]
        ckv_all = jnp.concatenate([ckv, ckv_c.astype(ckv.dtype)], axis=1)
        krope_all = jnp.concatenate([krope_lat, krope_c.astype(krope.dtype)], axis=1)
    else:
        ckv_all, krope_all = ckv, krope
    kv = (ckv_all @ lp['mla_w_kvb']).reshape(bsz, -1, MLA_HEADS, MLA_NOPE + MLA_V)
    k_nope, v_mla = kv[..., :MLA_NOPE], kv[..., MLA_NOPE:]
    k_mla = jnp.concatenate(
        [k_nope, jnp.broadcast_to(krope_all[:, :, None, :], k_nope.shape[:3] + (MLA_ROPE,))], axis=-1)
    q_mla = jnp.concatenate([q_nope, q_rope], axis=-1)
    y_mla = _softmax_attend(q_mla, k_mla, v_mla, (MLA_NOPE + MLA_ROPE) ** -0.5).reshape(bsz, n, MLA_W)

    u = s5_u.reshape(bsz, n, S5_GROUPS, S5_GROUP).astype(f32)
    y_ss, s5_hr, s5_hi = _s5_bidir(u, lp['s5_a_re'], lp['s5_a_im'], lp['s5_log_dt'], lp['s5_b_re'],
                                   lp['s5_b_im'], lp['s5_c_re'], lp['s5_c_im'], s5_h0r, s5_h0i)
    y_ss = y_ss.reshape(bsz, n, S5_W).astype(h.dtype) + lp['s5_d'] * s5_u
    g_ss = jax.nn.gelu(y_ss)
    y_s5 = g_ss * jax.nn.sigmoid(g_ss @ lp['s5_w_glu'])

    dq = df_q.reshape(bsz, n, 2 * DF_HEADS, DF_DQK)
    dk = df_k.reshape(bsz, n, 2 * DF_HEADS, DF_DQK)
    dv = df_v.reshape(bsz, n, DF_HEADS, DF_DV)
    dk_ctx_layout = dk.reshape(bsz, n, DF_HEADS, 2 * DF_DQK)
    if latent:
        dq = _apply_axial_rope(dq, rope)
        dk_lat = _apply_axial_rope(dk, rope).reshape(bsz, n, DF_HEADS, 2, DF_DQK)
        dk_all = jnp.concatenate(
            [dk_lat, dk_c.astype(dk.dtype).reshape(bsz, -1, DF_HEADS, 2, DF_DQK)], axis=1)
        dv_all = jnp.concatenate([dv, dv_c.astype(dv.dtype)], axis=1)
    else:
        dk_all = dk.reshape(bsz, n, DF_HEADS, 2, DF_DQK)
        dv_all = dv
    lam_p = lp['df_lambda'].astype(f32)
    lam = (jnp.exp(jnp.sum(lam_p[0] * lam_p[1])) - jnp.exp(jnp.sum(lam_p[2] * lam_p[3]))
           + lp['lam_init'])
    o = _diff_attend(dq.reshape(bsz, n, DF_HEADS, 2, DF_DQK), dk_all, dv_all, lam, DF_DQK ** -0.5)
    y_df = (_rmsnorm(o, lp['df_norm']) * (1.0 - lp['lam_init'])).reshape(bsz, n, DF_W)

    gates = jax.nn.sigmoid(gate_pre.reshape(bsz, n, N_BRANCH, D_MODEL))
    wb = lp['w_branch']
    merged = (gates[:, :, 0] * (y_ml @ wb[0]) + gates[:, :, 1] * (y_mla @ wb[1])
              + gates[:, :, 2] * (y_s5 @ wb[2]) + gates[:, :, 3] * (y_df @ wb[3]))
    out = merged @ lp['w_o']
    new_ctx = None if latent else (ckv, krope, dk_ctx_layout, dv, ml_c, ml_n, ml_m, s5_hr, s5_hi)
    return out, new_ctx


def _layer(x, cond, lp, rope, ctx):
    mod = jax.nn.silu(cond) @ lp['w_ada'] + lp['b_ada']
    sh1, sc1, g1, sh2, sc2, g2 = jnp.split(mod[:, None, :], 6, axis=-1)
    h = _rmsnorm(x, lp['norm_mix']) * (1.0 + sc1) + sh1
    mix, new_ctx = _mixer(h, lp, rope, ctx)
    x = x + g1 * mix
    h = _rmsnorm(x, lp['norm_ffn']) * (1.0 + sc2) + sh2
    a, b = jnp.split(h @ lp['w_ffn_in'], 2, axis=-1)
    x = x + g2 * ((jax.nn.silu(a) * b) @ lp['w_ffn_out'])
    return x, new_ctx


def setup_inputs(seed: int = 0) -> dict:
    key = jax.random.key(seed)
    ks = iter(jax.random.split(key, 48))
    f32 = jnp.float32

    def nrm(shape, scale):
        return scale * jax.random.normal(next(ks), shape, f32)

    def gain(shape):
        return 1.0 + nrm(shape, 0.01)

    inp = {}
    inp['x_prompt'] = nrm((BATCH, SEQ, D_MODEL), 1.0)
    inp['x_sample'] = nrm((DEC_BATCH, DEC_SEQ, D_MODEL), 1.0)
    inp['cache_mla_ckv'] = nrm((DEC_BATCH, DEPTH, PAST_LEN, MLA_KV_RANK), 1.0)
    inp['cache_mla_krope'] = nrm((DEC_BATCH, DEPTH, PAST_LEN, MLA_ROPE), 1.0)
    inp['cache_diff_k'] = nrm((DEC_BATCH, DEPTH, PAST_LEN, DF_HEADS, 2 * DF_DQK), 1.0)
    inp['cache_diff_v'] = nrm((DEC_BATCH, DEPTH, PAST_LEN, DF_HEADS, DF_DV), 1.0)
    inp['state_mlstm_c'] = nrm((DEC_BATCH, DEPTH, 2, ML_HEADS, ML_DH, ML_DH), 0.1)
    inp['state_mlstm_n'] = nrm((DEC_BATCH, DEPTH, 2, ML_HEADS, ML_DH), 0.1)
    inp['state_mlstm_m'] = nrm((DEC_BATCH, DEPTH, 2, ML_HEADS), 1.0)
    inp['state_s5_re'] = nrm((DEC_BATCH, DEPTH, 2, S5_GROUPS, S5_P), 0.1)
    inp['state_s5_im'] = nrm((DEC_BATCH, DEPTH, 2, S5_GROUPS, S5_P), 0.1)
    inp['c'] = nrm((DEC_BATCH, D_MODEL), 1.0)
    inp['c_ctx'] = nrm((D_MODEL,), 1.0)
    inp['w_ada'] = nrm((DEPTH, D_MODEL, 6 * D_MODEL), 0.5 * D_MODEL ** -0.5)
    inp['b_ada'] = nrm((DEPTH, 6 * D_MODEL), 0.01)
    inp['norm_mix'] = gain((DEPTH, D_MODEL))
    inp['norm_ffn'] = gain((DEPTH, D_MODEL))
    inp['w_in'] = nrm((DEPTH, D_MODEL, IN_COLS), D_MODEL ** -0.5)
    i_bias = nrm((DEPTH, 2, ML_HEADS), 0.1)
    f_bias = jnp.linspace(3.0, 6.0, ML_HEADS, dtype=f32) + nrm((DEPTH, 2, ML_HEADS), 0.1)
    inp['ml_if_bias'] = jnp.stack([i_bias, f_bias], axis=2)
    inp['ml_norm'] = gain((DEPTH, ML_W))
    inp['mla_kv_norm'] = gain((DEPTH, MLA_KV_RANK))
    inp['mla_w_kvb'] = nrm((DEPTH, MLA_KV_RANK, MLA_HEADS * (MLA_NOPE + MLA_V)), MLA_KV_RANK ** -0.5)
    n_idx = jnp.arange(S5_P, dtype=f32)
    inp['s5_a_re'] = -0.5 + nrm((DEPTH, 2, S5_GROUPS, S5_P), 0.01)
    inp['s5_a_im'] = math.pi * n_idx + nrm((DEPTH, 2, S5_GROUPS, S5_P), 0.01)
    inp['s5_log_dt'] = jax.random.uniform(next(ks), (DEPTH, 2, S5_GROUPS), f32,
                                          math.log(1e-3), math.log(1e-1))
    inp['s5_b_re'] = nrm((DEPTH, 2, S5_GROUPS, S5_P, S5_GROUP), (2 * S5_GROUP) ** -0.5)
    inp['s5_b_im'] = nrm((DEPTH, 2, S5_GROUPS, S5_P, S5_GROUP), (2 * S5_GROUP) ** -0.5)
    inp['s5_c_re'] = nrm((DEPTH, 2, S5_GROUPS, S5_GROUP, S5_P), S5_P ** -0.5)
    inp['s5_c_im'] = nrm((DEPTH, 2, S5_GROUPS, S5_GROUP, S5_P), S5_P ** -0.5)
    inp['s5_d'] = nrm((DEPTH, S5_W), 1.0)
    inp['s5_w_glu'] = nrm((DEPTH, S5_W, S5_W), S5_W ** -0.5)
    inp['df_lambda'] = nrm((DEPTH, 4, DF_DQK), 0.1)
    inp['df_norm'] = gain((DEPTH, DF_DV))
    inp['w_branch'] = nrm((DEPTH, N_BRANCH, BRANCH_W, D_MODEL), BRANCH_W ** -0.5)
    inp['w_o'] = nrm((DEPTH, D_MODEL, D_MODEL), D_MODEL ** -0.5)
    inp['w_ffn_in'] = nrm((DEPTH, D_MODEL, 2 * FFN_HIDDEN), D_MODEL ** -0.5)
    inp['w_ffn_out'] = nrm((DEPTH, FFN_HIDDEN, D_MODEL), FFN_HIDDEN ** -0.5)
    inp['final_norm'] = gain((D_MODEL,))
    return inp


def reference(x_prompt, x_sample, cache_mla_ckv, cache_mla_krope, cache_diff_k, cache_diff_v,
              state_mlstm_c, state_mlstm_n, state_mlstm_m, state_s5_re, state_s5_im, c,
              c_ctx, w_ada, b_ada, norm_mix, norm_ffn, w_in, ml_if_bias, ml_norm, mla_kv_norm,
              mla_w_kvb, s5_a_re, s5_a_im, s5_log_dt, s5_b_re, s5_b_im, s5_c_re, s5_c_im, s5_d,
              s5_w_glu, df_lambda, df_norm, w_branch, w_o, w_ffn_in, w_ffn_out, final_norm):
    rope = _axial_rope_tables(x_sample.shape[1])
    cond_ctx = jnp.broadcast_to(c_ctx, (x_prompt.shape[0], D_MODEL))
    xp, xs = x_prompt, x_sample
    ctx_out = []
    for l in range(DEPTH):
        lp = {'w_ada': w_ada[l], 'b_ada': b_ada[l], 'norm_mix': norm_mix[l], 'norm_ffn': norm_ffn[l],
              'w_in': w_in[l], 'ml_if_bias': ml_if_bias[l], 'ml_norm': ml_norm[l],
              'mla_kv_norm': mla_kv_norm[l], 'mla_w_kvb': mla_w_kvb[l],
              's5_a_re': s5_a_re[l], 's5_a_im': s5_a_im[l], 's5_log_dt': s5_log_dt[l],
              's5_b_re': s5_b_re[l], 's5_b_im': s5_b_im[l], 's5_c_re': s5_c_re[l], 's5_c_im': s5_c_im[l],
              's5_d': s5_d[l], 's5_w_glu': s5_w_glu[l], 'df_lambda': df_lambda[l], 'df_norm': df_norm[l],
              'lam_init': 0.8 - 0.6 * math.exp(-0.3 * l),
              'w_branch': w_branch[l], 'w_o': w_o[l], 'w_ffn_in': w_ffn_in[l], 'w_ffn_out': w_ffn_out[l]}
        xp, new_ctx = _layer(xp, cond_ctx, lp, None, None)
        ctx_out.append(new_ctx)
        cached = (cache_mla_ckv[:, l], cache_mla_krope[:, l], cache_diff_k[:, l], cache_diff_v[:, l],
                  state_mlstm_c[:, l], state_mlstm_n[:, l], state_mlstm_m[:, l],
                  state_s5_re[:, l], state_s5_im[:, l])
        xs, _ = _layer(xs, c, lp, rope, cached)
    y_prompt = _rmsnorm(xp, final_norm)
    y_sample = _rmsnorm(xs, final_norm)
    new_mla_ckv = jnp.stack([t[0] for t in ctx_out], axis=1)
    new_mla_krope = jnp.stack([t[1] for t in ctx_out], axis=1)
    new_diff_k = jnp.stack([t[2] for t in ctx_out], axis=1)
    new_diff_v = jnp.stack([t[3] for t in ctx_out], axis=1)
    new_mlstm_c = jnp.stack([t[4] for t in ctx_out], axis=1)
    new_mlstm_n = jnp.stack([t[5] for t in ctx_out], axis=1)
    new_mlstm_m = jnp.stack([t[6] for t in ctx_out], axis=1)
    new_s5_re = jnp.stack([t[7] for t in ctx_out], axis=1)
    new_s5_im = jnp.stack([t[8] for t in ctx_out], axis=1)
    return (y_prompt, y_sample, new_mla_ckv, new_mla_krope, new_diff_k, new_diff_v,
            new_mlstm_c, new_mlstm_n, new_mlstm_m, new_s5_re, new_s5_im)
```

```python
import contextlib
import math
import numpy as np
import ml_dtypes
import concourse.bass as bass
import concourse.mybir as mybir
from concourse.bass_utils import run_bass_kernel_spmd

F32 = mybir.dt.float32
BF16 = mybir.dt.bfloat16
AF = mybir.ActivationFunctionType
ALU = mybir.AluOpType
AX = mybir.AxisListType

ENGS = ("pe", "act", "dve", "pool", "sp")
EPOCH = 12000
DMA_EPOCH = 700
N_DMA_SEMS = 24


class Dep:
    __slots__ = ("writers", "readers", "ps")

    def __init__(self):
        self.writers = []
        self.readers = []
        self.ps = False


class Prog:
    def __init__(self, nc):
        self.nc = nc
        self.ops = {e: [] for e in ENGS}
        self.count = {e: 0 for e in ENGS}
        self.clock = {e: {} for e in ENGS}
        self.pre = {e: [] for e in ENGS}
        self.last_ev = {e: None for e in ENGS}
        self.dma_i = 0
        self.dma_last = {}
        self.sems = {}
        self.semstack = contextlib.ExitStack()

    def _eng_event(self, eng):
        i = self.count[eng]
        self.count[eng] += 1
        return (("e" + eng, i // EPOCH), i % EPOCH + 1)

    def _need(self, eng, ev, waits):
        key, val = ev
        if self.clock[eng].get(key, 0) >= val:
            return
        self.clock[eng][key] = val
        waits.append(ev)

    def op(self, eng, fn, reads=(), writes=(), dma=False, acc=False):
        waits = []
        for ev in self.pre[eng]:
            self._need(eng, ev, waits)
        self.pre[eng] = []
        me = "e" + eng
        writes = list(writes)
        for d in reads:
            if d.ps and d not in writes:
                writes.append(d)
        for d in reads:
            for ev in d.writers:
                if eng == "pe" and not dma and ev[0][0] == "epe":
                    continue
                self._need(eng, ev, waits)
        if not acc:
            for d in writes:
                for ev in d.writers:
                    if not dma and ev[0][0] == me:
                        continue
                    self._need(eng, ev, waits)
                for ev in d.readers:
                    if not dma and ev[0][0] == me:
                        continue
                    self._need(eng, ev, waits)
        if dma:
            slot = self.dma_i % N_DMA_SEMS
            n = self.dma_i // N_DMA_SEMS
            self.dma_i += 1
            prev = self.dma_last.get(slot)
            if prev is not None:
                self._need(eng, prev, waits)
            ev = (("d%d" % slot, n // DMA_EPOCH), 16 * (n % DMA_EPOCH + 1))
            self.dma_last[slot] = ev
        else:
            ev = self._eng_event(eng)
            self.last_ev[eng] = ev
        self.ops[eng].append((fn, waits, ev, dma))
        for d in reads:
            d.readers.append(ev)
        for d in writes:
            d.writers = [ev]
            d.readers = []
        return ev

    def barrier(self):
        evs = [ev for ev in self.last_ev.values() if ev is not None] + list(self.dma_last.values())
        for e in ENGS:
            self.pre[e] = list(evs)

    def _sem(self, key):
        if key not in self.sems:
            self.sems[key] = self.semstack.enter_context(self.nc.semaphore("s_%s_%d" % key))
        return self.sems[key]

    def emit(self):
        nc = self.nc
        self.barrier()
        final = self.pre["sp"]
        for e in ENGS:
            for (fn, waits, ev, dma) in self.ops[e]:
                self._sem(ev[0])
        names = {"pe": "tensor", "act": "scalar", "dve": "vector", "pool": "gpsimd", "sp": "sync"}
        with nc.Block() as block:
            for e in ENGS:
                def section(engine, ops=self.ops[e], last=(e == "sp")):
                    for (fn, waits, ev, dma) in ops:
                        for (k, v) in waits:
                            engine.wait_ge(self.sems[k], v)
                        fn(engine).then_inc(self.sems[ev[0]], 16 if dma else 1)
                    if last:
                        fin = {}
                        for (k, v) in final:
                            fin[k] = max(fin.get(k, 0), v)
                        for k, v in fin.items():
                            engine.wait_ge(self.sems[k], v)
                getattr(block, names[e])(section)


D = 2048
DEPTH = 2
GRID_W = 64
ML_H, ML_DH, ML_CH = 4, 256, 64
MLA_H, MLA_NOPE, MLA_ROPE, MLA_V, MLA_RANK = 8, 128, 64, 128, 512
S5_G, S5_I, S5_P = 64, 16, 64
DF_H, DF_DQK, DF_DV = 8, 64, 128
FFN_H = 5632
RMS_EPS = 1e-6
IN_SIZES = (1024, 1024, 1024, 1024, 16, 1536, 576, 1024, 1024, 1024, 1024, 8192)
IN_OFF = [0]
for _s in IN_SIZES:
    IN_OFF.append(IN_OFF[-1] + _s)
(O_MLQ, O_MLK, O_MLV, O_MLO, O_MLIF, O_MLAQ, O_KVA, O_S5U, O_DFQ, O_DFK, O_DFV, O_GATE, IN_COLS) = IN_OFF
NEG = -1.0e30


class Cfg:
    def __init__(self, ts=4096, tp=256, npr=2, past=256, g=1024, gw=2048):
        self.TS, self.TP, self.NPR, self.PAST, self.G = ts, tp, npr, past, g
        self.GW = min(gw, ts)


def _d(x):
    return x.d if hasattr(x, "d") else x


class Tile:
    def __init__(self, h):
        self.h = h
        self.d = Dep()

    def __getitem__(self, k):
        return self.h[k]


class Pool:
    def __init__(self, b):
        self.b = b
        self.es = contextlib.ExitStack()

    def __enter__(self):
        return self

    def __exit__(self, *a):
        self.b.P.barrier()
        self.es.close()

    def t(self, shape, dt=F32):
        self.b.uid += 1
        return Tile(self.es.enter_context(self.b.nc.sbuf_tensor("t%d" % self.b.uid, list(shape), dt)))


class Builder:
    def __init__(self, cfg):
        self.cfg = cfg
        self.nc = bass.Bass("TRN2", target_bir_lowering=False)
        self.P = Prog(self.nc)
        self.uid = 0
        self.din = {}
        self.dout = {}
        self.rr = 0

    def inp(self, name, shape, dt=F32):
        a = self.nc.dram_tensor(name, list(shape), dt, kind="ExternalInput").ap()
        self.din[name] = (list(shape), dt)
        return a

    def outp(self, name, shape):
        a = self.nc.dram_tensor(name, list(shape), F32, kind="ExternalOutput").ap()
        self.dout[name] = list(shape)
        return a

    def scr(self, name, shape, dt):
        return self.nc.dram_tensor(name, list(shape), dt, kind="Internal").ap()

    def mm(self, out, lhsT, rhs, start, stop, reads, w):
        self.P.op("pe", lambda e: e.matmul(out, lhsT=lhsT, rhs=rhs, start=start, stop=stop),
                  reads=[_d(r) for r in reads], writes=[_d(w)], acc=not start)

    def tr(self, out, in_, ident, reads, w, first):
        self.P.op("pe", lambda e: e.transpose(out=out, in_=in_, identity=ident),
                  reads=[_d(r) for r in reads], writes=[_d(w)], acc=not first)

    def act(self, out, in_, func, reads, w, scale=1.0, bias=0.0, accum=None):
        ws = [_d(x) for x in (w if isinstance(w, (list, tuple)) else [w])]
        if accum is None:
            f = lambda e: e.activation(out=out, in_=in_, func=func, scale=scale, bias=bias)
        else:
            f = lambda e: e.activation(out=out, in_=in_, func=func, scale=scale, bias=bias, accum_out=accum)
        self.P.op("act", f, reads=[_d(r) for r in reads], writes=ws)

    def tt(self, out, in0, in1, op, reads, w, eng="dve"):
        self.P.op(eng, lambda e: e.tensor_tensor(out=out, in0=in0, in1=in1, op=op),
                  reads=[_d(r) for r in reads], writes=[_d(w)])

    def ts(self, out, in0, s1, op0, reads, w, s2=None, op1=None, eng="dve"):
        if op1 is None:
            f = lambda e: e.tensor_scalar(out=out, in0=in0, scalar1=s1, scalar2=None, op0=op0)
        else:
            f = lambda e: e.tensor_scalar(out=out, in0=in0, scalar1=s1, scalar2=s2, op0=op0, op1=op1)
        self.P.op(eng, f, reads=[_d(r) for r in reads], writes=[_d(w)])

    def stt(self, out, in0, scalar, in1, op0, op1, reads, w):
        self.P.op("dve", lambda e: e.scalar_tensor_tensor(out=out, in0=in0, scalar=scalar, in1=in1, op0=op0, op1=op1),
                  reads=[_d(r) for r in reads], writes=[_d(w)])

    def scan(self, out, d0, d1, init, op0, op1, reads, w):
        self.P.op("dve", lambda e: e.tensor_tensor_scan(out=out, data0=d0, data1=d1, initial=init, op0=op0, op1=op1),
                  reads=[_d(r) for r in reads], writes=[_d(w)])

    def cp(self, out, in_, reads, w, eng=None):
        if eng is None:
            self.rr += 1
            eng = "act" if self.rr % 2 else "dve"
        if eng == "act":
            f = lambda e: e.copy(out=out, in_=in_)
        else:
            f = lambda e: e.tensor_copy(out=out, in_=in_)
        self.P.op(eng, f, reads=[_d(r) for r in reads], writes=[_d(w)])

    def recip(self, out, in_, reads, w):
        self.P.op("dve", lambda e: e.reciprocal(out=out, in_=in_), reads=[_d(r) for r in reads], writes=[_d(w)])

    def memset(self, ap, val, w, eng="pool"):
        self.P.op(eng, lambda e: e.memset(ap, val), writes=[_d(w)])

    def dma(self, out, in_, reads=(), w=None, eng="sp", slow=False):
        ws = [] if w is None else [_d(x) for x in (w if isinstance(w, (list, tuple)) else [w])]
        if slow:
            f = lambda e: e.dma_start(out=out, in_=in_, allow_slow_non_contiguous=True)
        else:
            f = lambda e: e.dma_start(out=out, in_=in_)
        self.P.op(eng, f, reads=[_d(r) for r in reads], writes=ws, dma=True)

    def pool(self):
        return Pool(self)

    def setup(self):
        c = self.cfg
        L = DEPTH
        TS, TP, NPR, PAST = c.TS, c.TP, c.NPR, c.PAST
        self.TPT = NPR * TP
        TM = max(TS, self.TPT)
        self.TM = TM
        I = self.inp
        self.xs = I("xs", [TS, D]); self.xp = I("xp", [self.TPT, D])
        self.condT = I("condT", [128, 16, 2])
        self.ckv_c = I("ckv_c", [L, PAST, 512]); self.kr_c = I("kr_c", [L, PAST, 64])
        self.dk_c = I("dk_c", [L, PAST, 1024]); self.dv_c = I("dv_c", [L, PAST, 1024])
        self.mlc0T = I("mlc0T", [L, 2, 4, 256, 256]); self.mln0 = I("mln0", [L, 2, 4, 128, 2])
        self.mlm0 = I("mlm0", [L, 2, 4, 1])
        self.s5h0r = I("s5h0r", [L, 2, 128, 32]); self.s5h0i = I("s5h0i", [L, 2, 128, 32])
        self.w_ada = I("w_ada", [L, D, 6 * D]); self.b_adaT = I("b_adaT", [L, 128, 96])
        self.nmixT = I("nmixT", [L, 128, 16]); self.nffnT = I("nffnT", [L, 128, 16])
        self.w_in = I("w_in", [L, D, IN_COLS]); self.mlifb = I("mlifb", [L, 4, 4])
        self.mlnormT = I("mlnormT", [L, 128, 8]); self.kvnorm = I("kvnorm", [L, 1, 512])
        self.w_kvb = I("w_kvb", [L, 512, 2048])
        self.s5_are = I("s5_are", [L, 2, 1, 4096]); self.s5_aim = I("s5_aim", [L, 2, 1, 4096])
        self.s5_ldt = I("s5_ldt", [L, 2, 1, 4096])
        self.s5_areS = I("s5_areS", [L, 2, 128, 32]); self.s5_aimS = I("s5_aimS", [L, 2, 128, 32])
        self.s5_ldtS = I("s5_ldtS", [L, 2, 128, 32])
        self.s5_breT = I("s5_breT", [L, 2, 32, 4096]); self.s5_bimT = I("s5_bimT", [L, 2, 32, 4096])
        self.s5_creT = I("s5_creT", [L, 2, 128, 32, 32]); self.s5_cimT = I("s5_cimT", [L, 2, 128, 32, 32])
        self.s5_dT = I("s5_dT", [L, 128, 8]); self.w_glu = I("w_glu", [L, 1024, 1024])
        self.df_lam = I("df_lam", [L, 1, 256]); self.dfnormT = I("dfnormT", [L, 128, 1])
        self.w_br = I("w_br", [L, 4, 1024, D]); self.w_o = I("w_o", [L, D, D])
        self.w_f1 = I("w_f1", [L, D, 2 * FFN_H]); self.w_f2 = I("w_f2", [L, FFN_H, D])
        self.fnorm = I("fnorm", [1, D])
        self.c_ident = I("c_ident", [128, 128]); self.c_ones = I("c_ones", [128, 128])
        self.c_maskL = I("c_maskL", [64, 64]); self.c_maskU = I("c_maskU", [64, 64])
        self.c_sel = I("c_sel", [4, 4, 128]); self.c_ropeR = I("c_ropeR", [128, 128])
        self.c_ropeC = I("c_ropeC", [128, TS]); self.c_ropeS = I("c_ropeS", [128, TS])
        O = self.outp
        self.o_ys = O("o_ys", [TS, D]); self.o_yp = O("o_yp", [self.TPT, D])
        self.o_ckv = O("o_ckv", [L, self.TPT, 512]); self.o_kr = O("o_kr", [L, self.TPT, 64])
        self.o_dk = O("o_dk", [L, self.TPT, 1024]); self.o_dv = O("o_dv", [L, self.TPT, 1024])
        self.o_mlc = O("o_mlc", [NPR, L, 2, 4, 256, 256]); self.o_mln = O("o_mln", [NPR, L, 2, 4, 128, 2])
        self.o_mlm = O("o_mlm", [NPR, L, 2, 4, 1])
        self.o_s5r = O("o_s5r", [NPR, L, 2, 128, 32]); self.o_s5i = O("o_s5i", [NPR, L, 2, 128, 32])
        S = self.scr
        self.xcs = S("xcs", [TS, D], F32); self.xcp = S("xcp", [self.TPT, D], F32)
        self.gsc = S("gsc", [2, 2, D], F32)
        self.q_mlT = S("q_mlT", [1024, TM], BF16); self.k_mlT = S("k_mlT", [1024, TM], BF16)
        self.k_ml = S("k_ml", [TM, 1024], BF16); self.v_ml = S("v_ml", [TM, 1024], BF16)
        self.o_mlT = S("o_mlT", [1024, TM], BF16); self.gates = S("gates", [4, 4, TM], F32)
        self.qnT = S("qnT", [1024, TM], BF16); self.qrT = S("qrT", [512, TM], BF16)
        self.ckvT = S("ckvT", [512, TM + PAST], BF16); self.krT = S("krT", [128, TM + PAST], BF16)
        self.s5uT = S("s5uT", [1024, TM], BF16)
        self.dqT = S("dqT", [1024, TM], BF16); self.dkT = S("dkT", [1024, TM + PAST], BF16)
        self.dvv = S("dvv", [TM + PAST, 1024], BF16)
        self.gateT = S("gateT", [8192, TM], BF16); self.yT = S("yT", [4, 1024, TM], BF16)
        self.hfb = S("hfb", [2, TM, 1024], F32); self.ys5 = S("ys5", [2, 1024, TM], F32)
        self.gp = Pool(self)
        g = self.gp
        self.ps = []
        for i in range(8):
            self.uid += 1
            self.ps.append(Tile(g.es.enter_context(self.nc.psum_tensor("ps%d" % i, [128, 512], F32))))
            self.ps[-1].d.ps = True
        self.identf = g.t([128, 128]); self.onesf = g.t([128, 128])
        self.identb = g.t([128, 128], BF16); self.onesb = g.t([128, 128], BF16)
        self.maskL = g.t([64, 64]); self.maskU = g.t([64, 64]); self.sel = g.t([4, 4, 128])
        self.ropeRb = g.t([128, 128], BF16)
        self.modT = g.t([128, 96, 2]); self.A1 = g.t([128, 2, 16]); self.A2 = g.t([128, 2, 16])
        self.eps = g.t([128, 1])
        self.dma(self.identf[:], self.c_ident, w=self.identf); self.dma(self.onesf[:], self.c_ones, w=self.onesf)
        self.dma(self.maskL[:], self.c_maskL, w=self.maskL); self.dma(self.maskU[:], self.c_maskU, w=self.maskU)
        self.dma(self.sel[:], self.c_sel, w=self.sel)
        self.dma(self.ropeRb[:], self.c_ropeR, w=self.ropeRb, eng="pool")
        self.cp(self.identb[:], self.identf[:], [self.identf], self.identb, eng="dve")
        self.cp(self.onesb[:], self.onesf[:], [self.onesf], self.onesb, eng="dve")
        self.memset(self.eps[:], RMS_EPS, self.eps)
        self.psi = 0

    def bank(self, lo=0, hi=4):
        b = lo + self.psi % (hi - lo)
        self.psi += 1
        return self.ps[b]

    def phase_ada(self, l):
        with self.pool() as A:
            scT = A.t([128, 16, 2]); bT = A.t([128, 96]); nm = A.t([128, 16]); nf = A.t([128, 16])
            self.dma(scT[:], self.condT, w=scT)
            self.dma(bT[:], self.b_adaT[l], w=bT); self.dma(nm[:], self.nmixT[l], w=nm); self.dma(nf[:], self.nffnT[l], w=nf)
            self.act(scT[:], scT[:], AF.Silu, [scT], scT)
            wst = [A.t([128, 16, 128]) for _ in range(3)]
            wv = self.w_ada[l].rearrange("(k p) c -> p k c", p=128)
            ps = self.ps[0]
            for cb in range(96):
                w = wst[cb % 3]
                self.dma(w[:], wv[:, :, cb * 128:(cb + 1) * 128], w=w, eng=("sp" if cb % 2 else "act"))
                for kc in range(16):
                    self.mm(ps[:, cb * 2:cb * 2 + 2], w[:, kc, :], scT[:, kc, :], kc == 0, kc == 15, [w, scT], ps)
            m = self.modT
            psv = ps[:, 0:192].rearrange("p (c t) -> p c t", t=2)
            for cnd in range(2):
                self.tt(m[:, :, cnd], psv[:, :, cnd], bT[:], ALU.add, [ps, bT], m)
            g4 = A.t([128, 16]); g5 = A.t([16, 128])
            for cnd in range(2):
                self.stt(self.A1[:, cnd, :], m[:, 16:32, cnd], 1.0, nm[:], ALU.add, ALU.mult, [m, nm], self.A1)
                self.stt(self.A2[:, cnd, :], m[:, 64:80, cnd], 1.0, nf[:], ALU.add, ALU.mult, [m, nf], self.A2)
                for gi, off in enumerate((32, 80)):
                    self.cp(g4[:], m[:, off:off + 16, cnd], [m], g4, eng="dve")
                    p2 = self.ps[1]
                    self.mm(p2[0:16, 0:128], g4[:], self.identf[:], True, True, [g4, self.identf], p2)
                    self.cp(g5[:], p2[0:16, 0:128], [p2], g5, eng="dve")
                    self.dma(self.gsc[gi, cnd].rearrange("(j p) -> j p", p=128), g5[:], [g5])

    def norm_T(self, A, xsrc, tok0, ntile, hT, acol, bcol_fn):
        xt = [A.t([128, D]) for _ in range(2)]
        xn = [A.t([128, D], BF16) for _ in range(2)]
        junk = A.t([128, D], BF16)
        ss = A.t([128, ntile])
        for t in range(ntile):
            x = xt[t % 2]; n = xn[t % 2]
            self.dma(x[:], xsrc[tok0 + t * 128: tok0 + (t + 1) * 128, :], w=x)
            self.act(junk[:], x[:], AF.Square, [x], [junk, ss], accum=ss[:, t:t + 1])
            self.act(ss[:, t:t + 1], ss[:, t:t + 1], AF.Sqrt, [ss], ss, scale=1.0 / D, bias=self.eps[:, 0:1])
            self.recip(ss[:, t:t + 1], ss[:, t:t + 1], [ss], ss)
            self.ts(n[:], x[:], ss[:, t:t + 1], ALU.mult, [x, ss], n)
            for half in range(2):
                pb = self.bank()
                pv = pb[:].bitcast(BF16)
                for j in range(8):
                    kc = half * 8 + j
                    self.tr(pv[:, j * 128:(j + 1) * 128], n[:, kc * 128:(kc + 1) * 128], self.identb[:],
                            [n, self.identb], pb, j == 0)
                self.cp(hT[:, half * 8:(half + 1) * 8, t * 128:(t + 1) * 128],
                        pv[:, 0:1024].rearrange("p (k t) -> p k t", k=8), [pb], hT)
        nt = ntile * 128
        for kc in range(16):
            self.ts(hT[:, kc, 0:nt], hT[:, kc, 0:nt], acol(kc), ALU.mult, [hT, self.A1, self.A2, self.modT], hT,
                    s2=bcol_fn(kc), op1=ALU.add)

    def load_w(self, wt, wsrc, pieces, kch=16):
        wv = wsrc.rearrange("(k p) c -> p k c", p=128)
        for (c0, n, off) in pieces:
            self.dma(wt[:, 0:kch, off:off + n], wv[:, :, c0:c0 + n], w=wt, eng="pool")

    def proj_F(self, A, hT, ntok, wsrc, blocks, kch=16):
        wts = [A.t([128, kch, 128], BF16) for _ in range(6)]
        nch = (ntok + 511) // 512
        for bi, (pieces, M, evac) in enumerate(blocks):
            wt = wts[bi % 6]
            self.load_w(wt, wsrc, pieces, kch)
            for j in range(nch):
                n = min(512, ntok - j * 512)
                pb = self.bank()
                for kc in range(kch):
                    self.mm(pb[0:M, 0:n], wt[:, kc, 0:M], hT[:, kc, j * 512:j * 512 + n], kc == 0, kc == kch - 1, [wt, hT], pb)
                evac(pb, j, M, n)

    def proj_T(self, A, hT, ntile, wsrc, blocks, kch=16):
        wts = [A.t([128, kch, 512], BF16) for _ in range(2)]
        for bi, (c0, ncols, evac) in enumerate(blocks):
            wt = wts[bi % 2]
            self.load_w(wt, wsrc, [(c0, ncols, 0)], kch)
            for t in range(ntile):
                pb = self.bank()
                for kc in range(kch):
                    self.mm(pb[:, 0:ncols], hT[:, kc, t * 128:(t + 1) * 128], wt[:, kc, 0:ncols], kc == 0, kc == kch - 1, [wt, hT], pb)
                evac(pb, t, ncols)

    def phase_win(self, l, xsrc, tok0, ntok, cnd, isS):
        ntile = ntok // 128
        wsrc = self.w_in[l]
        with self.pool() as A:
            hT = A.t([128, 16, ntok], BF16)
            with self.pool() as A0:
                self.norm_T(A0, xsrc, tok0, ntile, hT,
                            lambda kc: self.A1[:, cnd, kc:kc + 1], lambda kc: self.modT[:, kc, cnd:cnd + 1])
            stg = [A.t([128, 512], BF16) for _ in range(4)]
            stf = [A.t([128, 512]) for _ in range(3)]
            self.si = 0

            def stage():
                self.si += 1
                return stg[self.si % 4]

            def stagef():
                self.si += 1
                return stf[self.si % 3]

            if isS:
                rC = A.t([128, ntok]); rS = A.t([128, ntok])
                self.dma(rC[:], self.c_ropeC[:, tok0:tok0 + ntok], w=rC)
                self.dma(rS[:], self.c_ropeS[:, tok0:tok0 + ntok], w=rS)
            bia = A.t([4, 4])
            self.dma(bia[:], self.mlifb[l], w=bia)

            def ev_store(dst, row0, func=None, scale=1.0):
                def ev(pb, j, M, n):
                    s = stage()
                    if func is None and scale == 1.0:
                        self.cp(s[0:M, 0:n], pb[0:M, 0:n], [pb], s)
                    else:
                        self.act(s[0:M, 0:n], pb[0:M, 0:n], func or AF.Copy, [pb], s, scale=scale)
                    self.dma(dst[row0:row0 + M, tok0 + j * 512: tok0 + j * 512 + n], s[0:M, 0:n], [s])
                return ev

            def ev_rope(dst, row0):
                def ev(pb, j, M, n):
                    xb = stage()
                    self.cp(xb[:, 0:n], pb[:, 0:n], [pb], xb)
                    p2 = self.bank()
                    self.mm(p2[:, 0:n], self.ropeRb[:], xb[:, 0:n], True, True, [self.ropeRb, xb], p2)
                    t1 = stagef(); t2 = stagef(); s = stage()
                    self.tt(t1[:, 0:n], pb[:, 0:n], rC[:, j * 512:j * 512 + n], ALU.mult, [pb, rC], t1)
                    self.tt(t2[:, 0:n], p2[:, 0:n], rS[:, j * 512:j * 512 + n], ALU.mult, [p2, rS], t2)
                    self.tt(s[:, 0:n], t1[:, 0:n], t2[:, 0:n], ALU.add, [t1, t2], s, eng="pool")
                    self.dma(dst[row0:row0 + 128, tok0 + j * 512: tok0 + j * 512 + n], s[:, 0:n], [s])
                return ev

            def ev_gate(kind):
                def ev(pb, j, M, n):
                    s = stagef()
                    self.act(s[0:4, 0:n], pb[0:4, 0:n], AF.Identity, [pb, bia], s, bias=bia[:, kind:kind + 1])
                    self.dma(self.gates[kind, :, tok0 + j * 512: tok0 + j * 512 + n], s[0:4, 0:n], [s])
                return ev

            rp = ev_rope if isS else ev_store
            blocks = []
            for b in range(8):
                blocks.append(([(O_MLQ + b * 128, 128, 0)], 128, ev_store(self.q_mlT, b * 128)))
                blocks.append(([(O_MLK + b * 128, 128, 0)], 128, ev_store(self.k_mlT, b * 128, AF.Copy, 0.0625)))
                blocks.append(([(O_MLO + b * 128, 128, 0)], 128, ev_store(self.o_mlT, b * 128, AF.Sigmoid)))
                blocks.append(([(O_MLAQ + b * 192, 128, 0)], 128, ev_store(self.qnT, b * 128)))
                blocks.append(([(O_S5U + b * 128, 128, 0)], 128, ev_store(self.s5uT, b * 128)))
                blocks.append(([(O_DFQ + b * 128, 128, 0)], 128, rp(self.dqT, b * 128)))
                blocks.append(([(O_DFK + b * 128, 128, 0)], 128, rp(self.dkT, b * 128)))
            for k in range(4):
                blocks.append(([(O_MLIF + (k // 2) * 8 + (k % 2) * 4, 4, 0)], 4, ev_gate(k)))
            for p in range(4):
                blocks.append(([(O_MLAQ + (2 * p) * 192 + 128, 64, 0), (O_MLAQ + (2 * p + 1) * 192 + 128, 64, 64)], 128,
                               rp(self.qrT, p * 128)))
            blocks.append(([(O_KVA + 512, 64, 0), (O_KVA + 512, 64, 64)], 128, rp(self.krT, 0)))
            for b in range(64):
                blocks.append(([(O_GATE + b * 128, 128, 0)], 128, ev_store(self.gateT, b * 128, AF.Sigmoid)))
            import os
            sub = int(os.environ.get("MK_SUB", "9"))
            nblk = int(os.environ.get("MK_NBLK", "100000"))
            if sub >= 2:
                self.proj_F(A, hT, ntok, wsrc, blocks[:nblk])
            if sub < 3:
                return

            kvn = A.t([128, 512]); ssk = A.t([128, ntile])
            self.dma(kvn[:], self.kvnorm[l].partition_broadcast(128), w=kvn)
            junk = A.t([128, 512], BF16)

            def evT_store(dst, c0, scale=1.0, outf=None):
                def ev(pb, t, ncols):
                    s = stage()
                    r0 = tok0 + t * 128
                    if scale == 1.0:
                        self.cp(s[:, 0:ncols], pb[:, 0:ncols], [pb], s)
                    else:
                        self.act(s[:, 0:ncols], pb[:, 0:ncols], AF.Copy, [pb], s, scale=scale)
                    self.dma(dst[r0:r0 + 128, c0:c0 + ncols], s[:, 0:ncols], [s])
                    if outf is not None:
                        f = stagef()
                        self.cp(f[:, 0:ncols], pb[:, 0:ncols], [pb], f)
                        self.dma(outf[l, r0:r0 + 128, c0:c0 + ncols], f[:, 0:ncols], [f])
                return ev

            def evT_out(outf, c0):
                def ev(pb, t, ncols):
                    f = stagef()
                    r0 = tok0 + t * 128
                    self.cp(f[:, 0:ncols], pb[:, 0:ncols], [pb], f)
                    self.dma(outf[l, r0:r0 + 128, c0:c0 + ncols], f[:, 0:ncols], [f])
                return ev

            def evT_kva(pb, t, ncols):
                r0 = tok0 + t * 128
                self.act(junk[:], pb[:], AF.Square, [pb], [junk, ssk], accum=ssk[:, t:t + 1])
                self.act(ssk[:, t:t + 1], ssk[:, t:t + 1], AF.Sqrt, [ssk, self.eps], ssk, scale=1.0 / 512, bias=self.eps[:, 0:1])
                self.recip(ssk[:, t:t + 1], ssk[:, t:t + 1], [ssk], ssk)
                f = stagef()
                self.stt(f[:], pb[:], ssk[:, t:t + 1], kvn[:], ALU.mult, ALU.mult, [pb, ssk, kvn], f)
                if not isS:
                    self.dma(self.o_ckv[l, r0:r0 + 128, :], f[:], [f])
                s = stage()
                self.cp(s[:], f[:], [f], s)
                p2 = self.bank()
                pv = p2[:].bitcast(BF16)
                for kc in range(4):
                    self.tr(pv[:, kc * 128:(kc + 1) * 128], s[:, kc * 128:(kc + 1) * 128], self.identb[:], [s, self.identb], p2, kc == 0)
                s2 = stage()
                self.cp(s2[:], pv[:, 0:512], [p2], s2)
                self.dma(self.ckvT.rearrange("(k p) t -> p k t", p=128)[:, :, r0:r0 + 128],
                         s2[:].rearrange("p (k t) -> p k t", k=4), [s2])

            tb = []
            for hlf in range(2):
                tb.append((O_MLK + hlf * 512, 512, evT_store(self.k_ml, hlf * 512, 0.0625)))
                tb.append((O_MLV + hlf * 512, 512, evT_store(self.v_ml, hlf * 512)))
                tb.append((O_DFV + hlf * 512, 512, evT_store(self.dvv, hlf * 512, 1.0, None if (isS or os.environ.get("MK_VAR") == "noout") else self.o_dv)))
                if not isS:
                    tb.append((O_DFK + hlf * 512, 512, evT_out(self.o_dk, hlf * 512)))
            tb.append((O_KVA, 512, evT_kva))
            if not isS:
                tb.append((O_KVA + 512, 64, evT_out(self.o_kr, 0)))
            tb = tb[::-1][:int(os.environ.get("MK_NT", "1000"))]
            self.proj_T(A, hT, ntile, wsrc, tb)

    def phase_ctx(self, l):
        c = self.cfg
        TS, PAST = c.TS, c.PAST
        with self.pool() as A:
            for t in range(PAST // 128):
                r0 = TS + t * 128
                a = A.t([128, 512], BF16); kr = A.t([128, 64], BF16); dk = A.t([128, 1024], BF16); dv = A.t([128, 1024], BF16)
                self.dma(a[:], self.ckv_c[l, t * 128:(t + 1) * 128, :], w=a, eng="pool")
                self.dma(kr[:], self.kr_c[l, t * 128:(t + 1) * 128, :], w=kr, eng="pool")
                self.dma(dk[:], self.dk_c[l, t * 128:(t + 1) * 128, :], w=dk, eng="pool")
                self.dma(dv[:], self.dv_c[l, t * 128:(t + 1) * 128, :], w=dv, eng="pool")
                self.dma(self.dvv[r0:r0 + 128, :], dv[:], [dv])
                p2 = self.bank(); pv = p2[:].bitcast(BF16)
                for kc in range(4):
                    self.tr(pv[:, kc * 128:(kc + 1) * 128], a[:, kc * 128:(kc + 1) * 128], self.identb[:], [a, self.identb], p2, kc == 0)
                s2 = A.t([128, 512], BF16)
                self.cp(s2[:], pv[:, 0:512], [p2], s2)
                self.dma(self.ckvT.rearrange("(k p) t -> p k t", p=128)[:, :, r0:r0 + 128], s2[:].rearrange("p (k t) -> p k t", k=4), [s2])
                p3 = self.bank(); pv3 = p3[:].bitcast(BF16)
                self.tr(pv3[0:64, 0:128], kr[:, 0:64], self.identb[:], [kr, self.identb], p3, True)
                s3 = A.t([64, 128], BF16)
                self.cp(s3[:], pv3[0:64, 0:128], [p3], s3)
                self.dma(self.krT[0:64, r0:r0 + 128], s3[:], [s3]); self.dma(self.krT[64:128, r0:r0 + 128], s3[:], [s3])
                for hh in range(2):
                    p4 = self.bank(); pv4 = p4[:].bitcast(BF16)
                    for j in range(4):
                        b = hh * 4 + j
                        self.tr(pv4[:, j * 128:(j + 1) * 128], dk[:, b * 128:(b + 1) * 128], self.identb[:], [dk, self.identb], p4, j == 0)
                    s4 = A.t([128, 512], BF16)
                    self.cp(s4[:], pv4[:, 0:512], [p4], s4)
                    self.dma(self.dkT.rearrange("(k p) t -> p k t", p=128)[:, hh * 4:(hh + 1) * 4, r0:r0 + 128],
                             s4[:].rearrange("p (k t) -> p k t", k=4), [s4])

    def key_ranges(self, tok0, T, isS):
        r = [(tok0 + i * 128) for i in range(T // 128)]
        if isS:
            r += [(self.cfg.TS + i * 128) for i in range(self.cfg.PAST // 128)]
        return r

    def phase_mla(self, l, tok0, T, isS):
        kts = self.key_ranges(tok0, T, isS)
        NKT = len(kts)
        QC = min(512, T)
        sc = (MLA_NOPE + MLA_ROPE) ** -0.5
        with self.pool() as A:
            ckv = A.t([128, 4, NKT * 128], BF16); kr = A.t([128, NKT * 128], BF16)
            ckvv = self.ckvT.rearrange("(k p) t -> p k t", p=128)
            self.dma(ckv[:, :, 0:T], ckvv[:, :, tok0:tok0 + T], w=ckv); self.dma(kr[:, 0:T], self.krT[:, tok0:tok0 + T], w=kr)
            if isS:
                TS, PA = self.cfg.TS, self.cfg.PAST
                self.dma(ckv[:, :, T:T + PA], ckvv[:, :, TS:TS + PA], w=ckv); self.dma(kr[:, T:T + PA], self.krT[:, TS:TS + PA], w=kr)
            wk = [A.t([128, 4, 128], BF16) for _ in range(2)]; wv = [A.t([128, 4, 128], BF16) for _ in range(2)]
            knT = [A.t([128, NKT * 128], BF16) for _ in range(2)]; vt = [A.t([128, NKT, 128], BF16) for _ in range(2)]
            qn = [A.t([128, T], BF16) for _ in range(2)]; qr = [A.t([128, T], BF16) for _ in range(2)]
            pts = [A.t([128, 512], BF16) for _ in range(3)]
            rc = A.t([128, 512]); ob = [A.t([128, 512], BF16) for _ in range(2)]
            pi = 0
            for h in range(MLA_H):
                b = h % 2
                self.load_w(wk[b], self.w_kvb[l], [(h * 256, 128, 0)], 4)
                self.load_w(wv[b], self.w_kvb[l], [(h * 256 + 128, 128, 0)], 4)
                self.dma(qn[b][:], self.qnT[h * 128:(h + 1) * 128, tok0:tok0 + T], w=qn[b])
                self.dma(qr[b][:], self.qrT[(h // 2) * 128:(h // 2 + 1) * 128, tok0:tok0 + T], w=qr[b])
                nk = NKT * 128
                for j in range((nk + 511) // 512):
                    n = min(512, nk - j * 512)
                    pb = self.bank(0, 3)
                    for kc in range(4):
                        self.mm(pb[:, 0:n], wk[b][:, kc, :], ckv[:, kc, j * 512:j * 512 + n], kc == 0, kc == 3, [wk[b], ckv], pb)
                    self.cp(knT[b][:, j * 512:j * 512 + n], pb[:, 0:n], [pb], knT[b])
                for g in range((NKT + 3) // 4):
                    m = min(4, NKT - g * 4)
                    pb = self.bank(0, 3)
                    for i in range(m):
                        kt = g * 4 + i
                        for kc in range(4):
                            self.mm(pb[:, i * 128:(i + 1) * 128], ckv[:, kc, kt * 128:(kt + 1) * 128], wv[b][:, kc, :],
                                    kc == 0, kc == 3, [wv[b], ckv], pb)
                    self.cp(vt[b][:, g * 4:g * 4 + m, :], pb[:, 0:m * 128].rearrange("p (k t) -> p k t", k=m), [pb], vt[b])
                hp = (h % 2) * 64
                for qc in range(T // QC):
                    q0 = qc * QC
                    ao = self.ps[4 + qc % 2]; ad = self.ps[6 + qc % 2]
                    for kt in range(NKT):
                        pb = self.bank(0, 3)
                        self.mm(pb[:, 0:QC], knT[b][:, kt * 128:(kt + 1) * 128], qn[b][:, q0:q0 + QC], True, False, [knT[b], qn[b]], pb)
                        self.mm(pb[:, 0:QC], kr[hp:hp + 64, kt * 128:(kt + 1) * 128], qr[b][hp:hp + 64, q0:q0 + QC], False, True, [kr, qr[b]], pb)
                        pt = pts[pi % 3]; pi += 1
                        self.act(pt[:, 0:QC], pb[:, 0:QC], AF.Exp, [pb], pt, scale=sc)
                        self.mm(ao[:, 0:QC], vt[b][:, kt, :], pt[:, 0:QC], kt == 0, kt == NKT - 1, [vt[b], pt], ao)
                        self.mm(ad[:, 0:QC], self.onesb[:], pt[:, 0:QC], kt == 0, kt == NKT - 1, [self.onesb, pt], ad)
                    self.recip(rc[:, 0:QC], ad[:, 0:QC], [ad], rc)
                    o = ob[qc % 2]
                    self.tt(o[:, 0:QC], ao[:, 0:QC], rc[:, 0:QC], ALU.mult, [ao, rc], o)
                    self.dma(self.yT[1, h * 128:(h + 1) * 128, tok0 + q0:tok0 + q0 + QC], o[:, 0:QC], [o])

    def phase_diff(self, l, tok0, T, isS):
        kts = self.key_ranges(tok0, T, isS)
        NKT = len(kts)
        QC = min(512, T)
        lam_init = 0.8 - 0.6 * math.exp(-0.3 * l)
        with self.pool() as A:
            lb = A.t([128, 256]); pr = A.t([128, 128]); sm = A.t([128, 4]); dfs = A.t([128, 1])
            self.dma(lb[:], self.df_lam[l].partition_broadcast(128), w=lb)
            self.dma(dfs[:], self.dfnormT[l], w=dfs)
            self.tt(pr[:, 0:64], lb[:, 0:64], lb[:, 64:128], ALU.mult, [lb], pr)
            self.tt(pr[:, 64:128], lb[:, 128:192], lb[:, 192:256], ALU.mult, [lb], pr)
            self.P.op("dve", lambda e: e.reduce_sum(out=sm[:, 0:1], in_=pr[:, 0:64], axis=AX.X), reads=[pr.d], writes=[sm.d])
            self.P.op("dve", lambda e: e.reduce_sum(out=sm[:, 1:2], in_=pr[:, 64:128], axis=AX.X), reads=[pr.d], writes=[sm.d])
            self.act(sm[:, 0:2], sm[:, 0:2], AF.Exp, [sm], sm)
            self.tt(sm[:, 2:3], sm[:, 1:2], sm[:, 0:1], ALU.subtract, [sm], sm)
            self.ts(sm[:, 3:4], sm[:, 2:3], -lam_init, ALU.add, [sm], sm)
            self.ts(dfs[:], dfs[:], 1.0 - lam_init, ALU.mult, [dfs], dfs)
            kT = [A.t([128, NKT * 128], BF16) for _ in range(2)]; vt = [A.t([128, NKT, 128], BF16) for _ in range(2)]
            q = [A.t([128, T], BF16) for _ in range(2)]
            pts = [A.t([128, 512], BF16) for _ in range(4)]
            rc = A.t([128, 512]); o0 = A.t([128, 512]); o1 = A.t([128, 512]); sq = A.t([128, 512]); ob = [A.t([128, 512], BF16) for _ in range(2)]
            pi = 0
            dvr = self.dvv.rearrange("(k p) c -> p k c", p=128)
            for h in range(DF_H):
                b = h % 2
                self.dma(q[b][:], self.dqT[h * 128:(h + 1) * 128, tok0:tok0 + T], w=q[b])
                self.dma(kT[b][:, 0:T], self.dkT[h * 128:(h + 1) * 128, tok0:tok0 + T], w=kT[b])
                self.dma(vt[b][:, 0:T // 128, :], dvr[:, tok0 // 128:(tok0 + T) // 128, h * 128:(h + 1) * 128], w=vt[b])
                if isS:
                    TS, PA = self.cfg.TS, self.cfg.PAST
                    self.dma(kT[b][:, T:T + PA], self.dkT[h * 128:(h + 1) * 128, TS:TS + PA], w=kT[b])
                    self.dma(vt[b][:, T // 128:NKT, :], dvr[:, TS // 128:(TS + PA) // 128, h * 128:(h + 1) * 128], w=vt[b])
                for qc in range(T // QC):
                    q0 = qc * QC
                    ao = [self.ps[4], self.ps[5]]; ad = [self.ps[6], self.ps[7]]
                    for kt in range(NKT):
                        for cc in range(2):
                            pb = self.bank(0, 4)
                            lo = cc * 64
                            self.mm(pb[:, 0:QC], kT[b][lo:lo + 64, kt * 128:(kt + 1) * 128], q[b][lo:lo + 64, q0:q0 + QC], True, True, [kT[b], q[b]], pb)
                            pt = pts[pi % 4]; pi += 1
                            self.act(pt[:, 0:QC], pb[:, 0:QC], AF.Exp, [pb], pt, scale=DF_DQK ** -0.5)
                            self.mm(ao[cc][:, 0:QC], vt[b][:, kt, :], pt[:, 0:QC], kt == 0, kt == NKT - 1, [vt[b], pt], ao[cc])
                            self.mm(ad[cc][:, 0:QC], self.onesb[:], pt[:, 0:QC], kt == 0, kt == NKT - 1, [self.onesb, pt], ad[cc])
                    self.recip(rc[:, 0:QC], ad[0][:, 0:QC], [ad[0]], rc)
                    self.tt(o0[:, 0:QC], ao[0][:, 0:QC], rc[:, 0:QC], ALU.mult, [ao[0], rc], o0)
                    self.recip(rc[:, 0:QC], ad[1][:, 0:QC], [ad[1]], rc)
                    self.tt(o1[:, 0:QC], ao[1][:, 0:QC], rc[:, 0:QC], ALU.mult, [ao[1], rc], o1)
                    self.stt(o0[:, 0:QC], o1[:, 0:QC], sm[:, 3:4], o0[:, 0:QC], ALU.mult, ALU.add, [o1, sm, o0], o0)
                    self.act(sq[:, 0:QC], o0[:, 0:QC], AF.Square, [o0], sq)
                    pn = self.bank(0, 4)
                    self.mm(pn[:, 0:QC], self.onesf[:], sq[:, 0:QC], True, True, [self.onesf, sq], pn)
                    self.act(sq[:, 0:QC], pn[:, 0:QC], AF.Sqrt, [pn, self.eps], sq, scale=1.0 / DF_DV, bias=self.eps[:, 0:1])
                    self.recip(sq[:, 0:QC], sq[:, 0:QC], [sq], sq)
                    o = ob[qc % 2]
                    self.stt(o[:, 0:QC], o0[:, 0:QC], dfs[:, 0:1], sq[:, 0:QC], ALU.mult, ALU.mult, [o0, dfs, sq], o)
                    self.dma(self.yT[3, h * 128:(h + 1) * 128, tok0 + q0:tok0 + q0 + QC], o[:, 0:QC], [o])

    def phase_mlstm(self, l, tok0, T, isS, seq):
        NCH = T // 64
        for dr in range(2):
            rev = (dr == 1)

            def V(ap):
                return ap[:, ::-1] if rev else ap
            with self.pool() as O:
                rrow = O.t([4, T]); XT = O.t([64, NCH, 16]); wcb = O.t([128, 4, NCH])
                with self.pool() as A:
                    X = [A.t([4, T]) for _ in range(5)]
                    mk = A.t([4, T], BF16); pen = A.t([4, T], BF16)
                    X16 = A.t([16, T])
                    ac = A.t([4, NCH]); cme = A.t([4, NCH]); cmx = A.t([4, NCH]); msq = A.t([4, NCH]); mpv = A.t([4, NCH])
                    cc = A.t([4, NCH]); wc = A.t([4, NCH]); m0 = A.t([4, 1])
                    self.memset(mk[:], 1.0, mk); self.memset(pen[:], 0.0, pen)
                    self.memset(mk[:].rearrange("p (c l) -> p c l", l=64)[:, :, 0:1], 0.0, mk)
                    self.memset(pen[:].rearrange("p (c l) -> p c l", l=64)[:, :, 0:1], NEG, pen)
                    self.dma(X[0][:], self.gates[2 * dr, :, tok0:tok0 + T], w=X[0])
                    self.dma(X[1][:], self.gates[2 * dr + 1, :, tok0:tok0 + T], w=X[1])
                    if isS:
                        self.dma(m0[:], self.mlm0[l, dr], w=m0)
                    else:
                        self.memset(m0[:], 0.0, m0)
                    self.act(X[2][:], X[1][:], AF.Abs, [X[1]], X[2])
                    self.act(X[2][:], X[2][:], AF.Exp, [X[2]], X[2], scale=-1.0)
                    self.act(X[2][:], X[2][:], AF.Ln, [X[2]], X[2], bias=1.0)
                    self.ts(X[3][:], X[1][:], 0.0, ALU.min, [X[1]], X[3])
                    self.tt(X[1][:], X[3][:], X[2][:], ALU.subtract, [X[3], X[2]], X[1])
                    self.scan(V(X[2][:]), mk[:], V(X[1][:]), 0.0, ALU.mult, ALU.add, [mk, X[1]], X[2])
                    self.tt(X[0][:], X[0][:], X[2][:], ALU.subtract, [X[0], X[2]], X[0])
                    self.scan(V(X[1][:]), pen[:], V(X[0][:]), NEG, ALU.add, ALU.max, [pen, X[0]], X[1])
                    e = 0 if rev else 63
                    b3 = X[2][:].rearrange("p (c l) -> p c l", l=64); c3 = X[1][:].rearrange("p (c l) -> p c l", l=64)
                    self.cp(ac[:].unsqueeze(2), b3[:, :, e:e + 1], [X[2]], ac, eng="dve")
                    self.cp(cme[:].unsqueeze(2), c3[:, :, e:e + 1], [X[1]], cme, eng="dve")
                    self.tt(cmx[:], ac[:], cme[:], ALU.add, [ac, cme], cmx)

                    def Vc(ap):
                        return ap[:, ::-1] if rev else ap
                    self.scan(Vc(msq[:]), Vc(ac[:]), Vc(cmx[:]), m0[:, 0:1], ALU.add, ALU.max, [ac, cmx, m0], msq)
                    if NCH > 1:
                        if rev:
                            self.cp(mpv[:, 0:NCH - 1], msq[:, 1:NCH], [msq], mpv, eng="dve")
                        else:
                            self.cp(mpv[:, 1:NCH], msq[:, 0:NCH - 1], [msq], mpv, eng="dve")
                    pe_ = NCH - 1 if rev else 0
                    self.cp(mpv[:, pe_:pe_ + 1], m0[:, 0:1], [m0], mpv, eng="dve")
                    if not isS:
                        fe = 0 if rev else NCH - 1
                        self.dma(self.o_mlm[seq, l, dr], msq[:, fe:fe + 1], [msq])
                    mpb = mpv[:].unsqueeze(2).to_broadcast([4, NCH, 64])
                    r3 = rrow[:].rearrange("p (c l) -> p c l", l=64)
                    self.tt(c3, c3, mpb, ALU.max, [X[1], mpv], X[1])
                    self.ts(rrow[:], X[1][:], -1.0, ALU.mult, [X[1]], rrow)
                    x3 = X[3][:].rearrange("p (c l) -> p c l", l=64)
                    self.tt(x3, r3, mpb, ALU.add, [rrow, mpv], X[3])
                    self.act(X[3][:], X[3][:], AF.Exp, [X[3]], X[3])
                    self.tt(X[2][:], rrow[:], X[2][:], ALU.subtract, [rrow, X[2]], X[2])
                    self.act(X[2][:], X[2][:], AF.Exp, [X[2]], X[2])
                    self.tt(cc[:], ac[:], msq[:], ALU.subtract, [ac, msq], cc)
                    x4 = X[4][:].rearrange("p (c l) -> p c l", l=64)
                    self.tt(x4, X[0][:].rearrange("p (c l) -> p c l", l=64), cc[:].unsqueeze(2).to_broadcast([4, NCH, 64]),
                            ALU.add, [X[0], cc], X[4])
                    self.act(X[4][:], X[4][:], AF.Exp, [X[4]], X[4])
                    self.tt(wc[:], cc[:], mpv[:], ALU.add, [cc, mpv], wc)
                    self.act(wc[:], wc[:], AF.Exp, [wc], wc)
                    for hh in range(4):
                        pb = self.bank()
                        self.mm(pb[:, 0:NCH], self.sel[:, hh, :], wc[:], True, True, [self.sel, wc], pb)
                        self.cp(wcb[:, hh, :], pb[:, 0:NCH], [pb], wcb)
                    for qi, src in enumerate((X[0], X[3], X[2], X[4])):
                        self.dma(X16[4 * qi:4 * qi + 4, :], src[:], [src], w=X16)
                    for g in range((NCH + 31) // 32):
                        m = min(32, NCH - g * 32)
                        pb = self.bank()
                        for i in range(m):
                            c = g * 32 + i
                            self.mm(pb[0:64, i * 16:(i + 1) * 16], X16[:, c * 64:(c + 1) * 64], self.identf[0:16, 0:16], True, True,
                                    [X16, self.identf], pb)
                        self.cp(XT[:, g * 32:g * 32 + m, :], pb[0:64, 0:m * 16].rearrange("p (c q) -> p c q", q=16), [pb], XT)
                with self.pool() as A:
                    CT = [A.t([128, 2, 257]) for _ in range(4)]; CTb = [A.t([128, 2, 257], BF16) for _ in range(4)]
                    for hh in range(4):
                        if isS:
                            self.dma(CT[hh][:, :, 0:256], self.mlc0T[l, dr, hh].rearrange("(k p) v -> p k v", p=128), w=CT[hh])
                            n0t = A.t([128, 2])
                            self.dma(n0t[:], self.mln0[l, dr, hh], w=n0t)
                            self.cp(CT[hh][:, :, 256], n0t[:], [n0t], CT[hh], eng="dve")
                        else:
                            self.memset(CT[hh][:], 0.0, CT[hh])
                        self.cp(CTb[hh][:], CT[hh][:], [CT[hh]], CTb[hh])
                    NB = 2
                    qc_ = [A.t([128, 8, 64], BF16) for _ in range(NB)]; kc_ = [A.t([128, 8, 64], BF16) for _ in range(NB)]
                    kt_ = [A.t([64, 1024], BF16) for _ in range(NB)]; va_ = [A.t([64, 4, 257], BF16) for _ in range(NB)]
                    for v in va_:
                        self.memset(v[:, :, 256:257], 1.0, v)
                    Dt = [A.t([64, 64]) for _ in range(4)]; SD = [A.t([64, 64], BF16) for _ in range(4)]
                    isb = [A.t([64, 257]) for _ in range(4)]; nd = [A.t([64, 257]) for _ in range(4)]
                    dn = [A.t([64, 2]) for _ in range(4)]; wv = [A.t([64, 257], BF16) for _ in range(4)]
                    hst = [A.t([64, 1024]) for _ in range(2)]
                    mask = self.maskU if rev else self.maskL
                    qv = self.q_mlT.rearrange("(j p) t -> p j t", p=128); kv = self.k_mlT.rearrange("(j p) t -> p j t", p=128)
                    order = range(NCH - 1, -1, -1) if rev else range(NCH)
                    for ci, c in enumerate(order):
                        bb = ci % NB
                        t0 = tok0 + c * 64
                        Q = qc_[bb]; K = kc_[bb]; KT = kt_[bb]; VA = va_[bb]; H = hst[ci % 2]
                        self.dma(Q[:], qv[:, :, t0:t0 + 64], w=Q); self.dma(K[:], kv[:, :, t0:t0 + 64], w=K)
                        self.dma(KT[:], self.k_ml[t0:t0 + 64, :], w=KT)
                        self.dma(VA[:, :, 0:256], self.v_ml[t0:t0 + 64, :].rearrange("s (h v) -> s h v", h=4), w=VA)
                        for hh in range(4):
                            pa = self.ps[2 * hh]; pbk = self.ps[2 * hh + 1]
                            for k2 in range(2):
                                self.mm(pa[0:64, 0:64], K[:, hh * 2 + k2, :], Q[:, hh * 2 + k2, :], k2 == 0, k2 == 1, [K, Q], pa)
                            self.mm(pbk[0:64, 0:64], self.sel[:, hh, 0:64], rrow[:, c * 64:(c + 1) * 64], True, False, [self.sel, rrow], pbk)
                            self.mm(pbk[0:64, 0:64], self.identf[0:64, 0:64], mask[:], False, True, [self.identf, mask], pbk)
                            self.act(Dt[hh][:], pbk[0:64, 0:64], AF.Exp, [pbk, XT], Dt[hh], bias=XT[:, c, hh:hh + 1])
                            self.tt(SD[hh][:], pa[0:64, 0:64], Dt[hh][:], ALU.mult, [pa, Dt[hh]], SD[hh])
                            self.mm(pbk[0:64, 0:257], SD[hh][:], VA[:, hh, :], True, True, [SD[hh], VA], pbk)
                            for k2 in range(2):
                                self.mm(pa[0:64, 0:257], Q[:, hh * 2 + k2, :], CTb[hh][:, k2, :], k2 == 0, k2 == 1, [Q, CTb[hh]], pa)
                            self.cp(isb[hh][:], pbk[0:64, 0:257], [pbk], isb[hh], eng="act")
                            self.stt(nd[hh][:], pa[0:64, 0:257], XT[:, c, 4 + hh:5 + hh], isb[hh][:], ALU.mult, ALU.add, [pa, XT, isb[hh]], nd[hh])
                            self.stt(dn[hh][:, 0:1], nd[hh][:, 256:257], -1.0, nd[hh][:, 256:257], ALU.mult, ALU.max, [nd[hh]], dn[hh])
                            self.ts(dn[hh][:, 0:1], dn[hh][:, 0:1], XT[:, c, 8 + hh:9 + hh], ALU.max, [dn[hh], XT], dn[hh])
                            self.recip(dn[hh][:, 1:2], dn[hh][:, 0:1], [dn[hh]], dn[hh])
                            self.ts(H[:, hh * 256:(hh + 1) * 256], nd[hh][:, 0:256], dn[hh][:, 1:2], ALU.mult, [nd[hh], dn[hh]], H)
                            self.act(wv[hh][:], VA[:, hh, :], AF.Copy, [VA, XT], wv[hh], scale=XT[:, c, 12 + hh:13 + hh])
                            for k2, pp in enumerate((pa, pbk)):
                                self.mm(pp[:, 0:257], KT[:, hh * 256 + k2 * 128: hh * 256 + (k2 + 1) * 128], wv[hh][:], True, True, [KT, wv[hh]], pp)
                                self.stt(CT[hh][:, k2, :], CT[hh][:, k2, :], wcb[:, hh, c:c + 1], pp[:, 0:257], ALU.mult, ALU.add,
                                         [CT[hh], wcb, pp], CT[hh])
                            self.cp(CTb[hh][:], CT[hh][:], [CT[hh]], CTb[hh], eng="pool")
                        self.dma(self.hfb[dr, t0:t0 + 64, :], H[:], [H])
                    if not isS:
                        for hh in range(4):
                            self.dma(self.o_mlc[seq, l, dr, hh].rearrange("(k p) v -> p k v", p=128), CT[hh][:, :, 0:256], [CT[hh]])
                            n1t = A.t([128, 2])
                            self.cp(n1t[:], CT[hh][:, :, 256], [CT[hh]], n1t, eng="dve")
                            self.dma(self.o_mln[seq, l, dr, hh], n1t[:], [n1t])

    def phase_mlstm_post(self, l, tok0, ntok):
        with self.pool() as A:
            mn = A.t([128, 8]); self.dma(mn[:], self.mlnormT[l], w=mn)
            hf = [A.t([128, 1024]) for _ in range(2)]; hb = [A.t([128, 1024]) for _ in range(2)]
            junk = A.t([128, 256], BF16); ss = A.t([128, 4]); hn = A.t([128, 4, 1024], BF16)
            ot = [A.t([128, 512], BF16) for _ in range(2)]; yo = [A.t([128, 512], BF16) for _ in range(2)]
            for g in range((ntok + 511) // 512):
                n = min(512, ntok - g * 512); nt = n // 128
                for t in range(nt):
                    r0 = tok0 + g * 512 + t * 128
                    a = hf[t % 2]; b = hb[t % 2]
                    self.dma(a[:], self.hfb[0, r0:r0 + 128, :], w=a); self.dma(b[:], self.hfb[1, r0:r0 + 128, :], w=b)
                    self.tt(a[:], a[:], b[:], ALU.add, [a, b], a, eng="pool")
                    for hh in range(4):
                        self.act(junk[:], a[:, hh * 256:(hh + 1) * 256], AF.Square, [a], [junk, ss], accum=ss[:, hh:hh + 1])
                    self.act(ss[:], ss[:], AF.Sqrt, [ss, self.eps], ss, scale=1.0 / 256, bias=self.eps[:, 0:1])
                    self.recip(ss[:], ss[:], [ss], ss)
                    for hh in range(4):
                        self.ts(hn[:, t, hh * 256:(hh + 1) * 256], a[:, hh * 256:(hh + 1) * 256], ss[:, hh:hh + 1], ALU.mult, [a, ss], hn)
                for j in range(8):
                    pb = self.bank(); pv = pb[:].bitcast(BF16)
                    for t in range(nt):
                        self.tr(pv[:, t * 128:(t + 1) * 128], hn[:, t, j * 128:(j + 1) * 128], self.identb[:], [hn, self.identb], pb, t == 0)
                    o = ot[j % 2]; y = yo[j % 2]
                    c0 = tok0 + g * 512
                    self.dma(o[:, 0:n], self.o_mlT[j * 128:(j + 1) * 128, c0:c0 + n], w=o)
                    self.stt(y[:, 0:n], pv[:, 0:n], mn[:, j:j + 1], o[:, 0:n], ALU.mult, ALU.mult, [pb, mn, o], y)
                    self.dma(self.yT[0, j * 128:(j + 1) * 128, c0:c0 + n], y[:, 0:n], [y])

    def phase_s5(self, l, tok0, T, isS, seq):
        TB = min(512, T)
        NBK = T // TB
        for dr in range(2):
            rev = (dr == 1)
            with self.pool() as O:
                bbr = O.t([32, 4096], BF16); bbi = O.t([32, 4096], BF16)
                cre = O.t([128, 32, 32], BF16); cim = O.t([128, 32, 32], BF16)
                mag = O.t([128, 32]); E1c = O.t([128, 32]); E1s = O.t([128, 32])
                ETc = O.t([128, 32]); ETs = O.t([128, 32])
                self.dma(cre[:], self.s5_creT[l, dr], w=cre, eng="pool"); self.dma(cim[:], self.s5_cimT[l, dr], w=cim, eng="pool")

                def disc(A, npart, nfree, src_re, src_im, src_dt, bc, c0=0):
                    t = {k: A.t([npart, nfree]) for k in ("lr", "li", "dt", "mag", "cs", "sn", "a", "b", "c", "qr", "qi")}
                    for k, s in (("lr", src_re), ("li", src_im), ("dt", src_dt)):
                        self.dma(t[k][:], s[:, c0:c0 + nfree].partition_broadcast(npart) if bc else s, w=t[k])
                    T_ = lambda k: t[k][:]
                    self.ts(T_("lr"), T_("lr"), -1e-4, ALU.min, [t["lr"]], t["lr"])
                    self.act(T_("dt"), T_("dt"), AF.Exp, [t["dt"]], t["dt"])
                    self.tt(T_("a"), T_("dt"), T_("lr"), ALU.mult, [t["dt"], t["lr"]], t["a"])
                    self.act(T_("mag"), T_("a"), AF.Exp, [t["a"]], t["mag"])
                    self.tt(T_("a"), T_("dt"), T_("li"), ALU.mult, [t["dt"], t["li"]], t["a"])
                    self.ts(T_("a"), T_("a"), 1.0 / 64, ALU.mult, [t["a"]], t["a"])
                    self.tt(T_("b"), T_("a"), T_("a"), ALU.mult, [t["a"]], t["b"])
                    self.ts(T_("c"), T_("b"), -1.0 / 42, ALU.mult, [t["b"]], t["c"], s2=1.0, op1=ALU.add)
                    self.tt(T_("c"), T_("c"), T_("b"), ALU.mult, [t["c"], t["b"]], t["c"])
                    self.ts(T_("c"), T_("c"), -1.0 / 20, ALU.mult, [t["c"]], t["c"], s2=1.0, op1=ALU.add)
                    self.tt(T_("c"), T_("c"), T_("b"), ALU.mult, [t["c"], t["b"]], t["c"])
                    self.ts(T_("c"), T_("c"), -1.0 / 6, ALU.mult, [t["c"]], t["c"], s2=1.0, op1=ALU.add)
                    self.tt(T_("sn"), T_("c"), T_("a"), ALU.mult, [t["c"], t["a"]], t["sn"])
                    self.ts(T_("c"), T_("b"), -1.0 / 56, ALU.mult, [t["b"]], t["c"], s2=1.0, op1=ALU.add)
                    self.tt(T_("c"), T_("c"), T_("b"), ALU.mult, [t["c"], t["b"]], t["c"])
                    self.ts(T_("c"), T_("c"), -1.0 / 30, ALU.mult, [t["c"]], t["c"], s2=1.0, op1=ALU.add)
                    self.tt(T_("c"), T_("c"), T_("b"), ALU.mult, [t["c"], t["b"]], t["c"])
                    self.ts(T_("c"), T_("c"), -1.0 / 12, ALU.mult, [t["c"]], t["c"], s2=1.0, op1=ALU.add)
                    self.tt(T_("c"), T_("c"), T_("b"), ALU.mult, [t["c"], t["b"]], t["c"])
                    self.ts(T_("cs"), T_("c"), -0.5, ALU.mult, [t["c"]], t["cs"], s2=1.0, op1=ALU.add)
                    for _ in range(6):
                        self.tt(T_("a"), T_("cs"), T_("cs"), ALU.mult, [t["cs"]], t["a"])
                        self.tt(T_("b"), T_("sn"), T_("sn"), ALU.mult, [t["sn"]], t["b"])
                        self.tt(T_("c"), T_("sn"), T_("cs"), ALU.mult, [t["sn"], t["cs"]], t["c"])
                        self.tt(T_("cs"), T_("a"), T_("b"), ALU.subtract, [t["a"], t["b"]], t["cs"])
                        self.ts(T_("sn"), T_("c"), 2.0, ALU.mult, [t["c"]], t["sn"])
                    return t

                for hc in range(2):
                  c0 = hc * 2048
                  with self.pool() as A:
                    t = disc(A, 32, 2048, self.s5_are[l, dr], self.s5_aim[l, dr], self.s5_ldt[l, dr], True, c0)
                    T_ = lambda k: t[k][:]
                    br = A.t([32, 2048]); bi = A.t([32, 2048])
                    self.dma(br[:], self.s5_breT[l, dr][:, c0:c0 + 2048], w=br); self.dma(bi[:], self.s5_bimT[l, dr][:, c0:c0 + 2048], w=bi)
                    self.tt(T_("cs"), T_("cs"), T_("mag"), ALU.mult, [t["cs"], t["mag"]], t["cs"])
                    self.tt(T_("sn"), T_("sn"), T_("mag"), ALU.mult, [t["sn"], t["mag"]], t["sn"])
                    self.ts(T_("cs"), T_("cs"), -1.0, ALU.add, [t["cs"]], t["cs"])
                    self.tt(T_("a"), T_("lr"), T_("lr"), ALU.mult, [t["lr"]], t["a"])
                    self.tt(T_("b"), T_("li"), T_("li"), ALU.mult, [t["li"]], t["b"])
                    self.tt(T_("a"), T_("a"), T_("b"), ALU.add, [t["a"], t["b"]], t["a"])
                    self.recip(T_("a"), T_("a"), [t["a"]], t["a"])
                    self.tt(T_("b"), T_("cs"), T_("lr"), ALU.mult, [t["cs"], t["lr"]], t["b"])
                    self.tt(T_("c"), T_("sn"), T_("li"), ALU.mult, [t["sn"], t["li"]], t["c"])
                    self.tt(T_("b"), T_("b"), T_("c"), ALU.add, [t["b"], t["c"]], t["b"])
                    self.tt(T_("qr"), T_("b"), T_("a"), ALU.mult, [t["b"], t["a"]], t["qr"])
                    self.tt(T_("b"), T_("sn"), T_("lr"), ALU.mult, [t["sn"], t["lr"]], t["b"])
                    self.tt(T_("c"), T_("cs"), T_("li"), ALU.mult, [t["cs"], t["li"]], t["c"])
                    self.tt(T_("b"), T_("b"), T_("c"), ALU.subtract, [t["b"], t["c"]], t["b"])
                    self.tt(T_("qi"), T_("b"), T_("a"), ALU.mult, [t["b"], t["a"]], t["qi"])
                    self.tt(T_("a"), T_("qr"), br[:], ALU.mult, [t["qr"], br], t["a"])
                    self.tt(T_("b"), T_("qi"), bi[:], ALU.mult, [t["qi"], bi], t["b"])
                    self.tt(bbr[:, c0:c0 + 2048], T_("a"), T_("b"), ALU.subtract, [t["a"], t["b"]], bbr)
                    self.tt(T_("a"), T_("qr"), bi[:], ALU.mult, [t["qr"], bi], t["a"])
                    self.tt(T_("b"), T_("qi"), br[:], ALU.mult, [t["qi"], br], t["b"])
                    self.tt(bbi[:, c0:c0 + 2048], T_("a"), T_("b"), ALU.add, [t["a"], t["b"]], bbi)
                O2 = self.pool(); O2.__enter__()
                tc_ = O2.t([128, 32, TB]); ts_ = O2.t([128, 32, TB])
                with self.pool() as A:
                    t = disc(A, 128, 32, self.s5_areS[l, dr], self.s5_aimS[l, dr], self.s5_ldtS[l, dr], False)
                    self.cp(mag[:], t["mag"][:], [t["mag"]], mag, eng="dve")
                    self.cp(E1c[:], t["cs"][:], [t["cs"]], E1c, eng="dve"); self.cp(E1s[:], t["sn"][:], [t["sn"]], E1s, eng="dve")
                    self.memset(tc_[:, :, 0:1], 1.0, tc_); self.memset(ts_[:, :, 0:1], 0.0, ts_)
                    pc = A.t([128, 32]); psn = A.t([128, 32]); a = A.t([128, 32]); b = A.t([128, 32])
                    w1 = A.t([128, 8, TB // 2]); w2 = A.t([128, 8, TB // 2])
                    self.cp(pc[:], t["cs"][:], [t["cs"]], pc, eng="dve"); self.cp(psn[:], t["sn"][:], [t["sn"]], psn, eng="dve")
                    n = 1
                    while n < TB:
                        for jh in range(4):
                            js = slice(jh * 8, (jh + 1) * 8)
                            pcb = pc[:, js].unsqueeze(2).to_broadcast([128, 8, n]); psb = psn[:, js].unsqueeze(2).to_broadcast([128, 8, n])
                            self.tt(w1[:, :, 0:n], tc_[:, js, 0:n], pcb, ALU.mult, [tc_, pc], w1)
                            self.tt(w2[:, :, 0:n], ts_[:, js, 0:n], psb, ALU.mult, [ts_, psn], w2)
                            self.tt(tc_[:, js, n:2 * n], w1[:, :, 0:n], w2[:, :, 0:n], ALU.subtract, [w1, w2], tc_)
                            self.tt(w1[:, :, 0:n], tc_[:, js, 0:n], psb, ALU.mult, [tc_, psn], w1)
                            self.tt(w2[:, :, 0:n], ts_[:, js, 0:n], pcb, ALU.mult, [ts_, pc], w2)
                            self.tt(ts_[:, js, n:2 * n], w1[:, :, 0:n], w2[:, :, 0:n], ALU.add, [w1, w2], ts_)
                        self.tt(a[:], pc[:], pc[:], ALU.mult, [pc], a); self.tt(b[:], psn[:], psn[:], ALU.mult, [psn], b)
                        self.tt(b[:], a[:], b[:], ALU.subtract, [a, b], b)
                        self.tt(a[:], pc[:], psn[:], ALU.mult, [pc, psn], a)
                        self.ts(psn[:], a[:], 2.0, ALU.mult, [a], psn)
                        self.cp(pc[:], b[:], [b], pc, eng="dve")
                        n *= 2
                    self.cp(ETc[:], pc[:], [pc], ETc, eng="dve"); self.cp(ETs[:], psn[:], [psn], ETs, eng="dve")
                with self.pool() as A:
                    inr = A.t([128, 32]); ini = A.t([128, 32]); h0r = A.t([128, 32]); h0i = A.t([128, 32]); a = A.t([128, 32]); b = A.t([128, 32])
                    if isS:
                        self.dma(h0r[:], self.s5h0r[l, dr], w=h0r); self.dma(h0i[:], self.s5h0i[l, dr], w=h0i)
                        self.tt(a[:], h0r[:], E1c[:], ALU.mult, [h0r, E1c], a); self.tt(b[:], h0i[:], E1s[:], ALU.mult, [h0i, E1s], b)
                        self.tt(inr[:], a[:], b[:], ALU.subtract, [a, b], inr)
                        self.tt(a[:], h0r[:], E1s[:], ALU.mult, [h0r, E1s], a); self.tt(b[:], h0i[:], E1c[:], ALU.mult, [h0i, E1c], b)
                        self.tt(ini[:], a[:], b[:], ALU.add, [a, b], ini)
                    else:
                        self.memset(inr[:], 0.0, inr); self.memset(ini[:], 0.0, ini)
                    dcar = [Dep() for _ in range(32)]
                    self.P.barrier()
                    uT = [A.t([32, TB], BF16) for _ in range(4)]
                    zr = [A.t([128, TB]) for _ in range(2)]; zi = [A.t([128, TB]) for _ in range(2)]
                    w = [A.t([128, TB]) for _ in range(4)]
                    gr = [A.t([128, TB]) for _ in range(2)]; gi = [A.t([128, TB]) for _ in range(2)]
                    hr = [A.t([128, TB], BF16) for _ in range(2)]; hi = [A.t([128, TB], BF16) for _ in range(2)]
                    ys = [A.t([32, TB]) for _ in range(2)]
                    sm = [A.t([128, 4]) for _ in range(2)]
                    fr = A.t([128, 32]); fi = A.t([128, 32])
                    R = (lambda ap: ap[:, ::-1]) if rev else (lambda ap: ap)
                    ui = 0
                    for bk in (range(NBK - 1, -1, -1) if rev else range(NBK)):
                        for j in range(32):
                            c0 = bk * TB
                            U = uT[ui % 4]
                            self.dma(U[:], self.s5uT[32 * j:32 * j + 32, tok0 + c0:tok0 + c0 + TB], w=U)
                            Cj = tc_[:, j, :]; Sj = ts_[:, j, :]
                            k = ui % 2; ui += 1
                            pr = self.bank(0, 3); pi_ = self.bank(3, 6)
                            self.mm(pr[:, 0:TB], bbr[:, j * 128:(j + 1) * 128], U[:, 0:TB], True, True, [bbr, U], pr)
                            self.mm(pi_[:, 0:TB], bbi[:, j * 128:(j + 1) * 128], U[:, 0:TB], True, True, [bbi, U], pi_)
                            self.tt(w[0][:], pr[:, 0:TB], R(Cj), ALU.mult, [pr, tc_], w[0])
                            self.tt(w[1][:], pi_[:, 0:TB], R(Sj), ALU.mult, [pi_, ts_], w[1])
                            self.tt(w[2][:], pi_[:, 0:TB], R(Cj), ALU.mult, [pi_, tc_], w[2])
                            self.tt(w[3][:], pr[:, 0:TB], R(Sj), ALU.mult, [pr, ts_], w[3])
                            self.tt(zr[k][:], w[0][:], w[1][:], ALU.add, [w[0], w[1]], zr[k], eng="pool")
                            self.tt(zi[k][:], w[2][:], w[3][:], ALU.subtract, [w[2], w[3]], zi[k], eng="pool")
                            mb = mag[:, j:j + 1].to_broadcast([128, TB])
                            self.scan(R(gr[k][:]), mb, R(zr[k][:]), inr[:, j:j + 1], ALU.mult, ALU.add, [mag, zr[k], dcar[j]], gr[k])
                            self.scan(R(gi[k][:]), mb, R(zi[k][:]), ini[:, j:j + 1], ALU.mult, ALU.add, [mag, zi[k], dcar[j]], gi[k])
                            le = 0 if rev else TB - 1
                            glr = gr[k][:, le:le + 1]; gli = gi[k][:, le:le + 1]
                            s = sm[k]
                            self.tt(s[:, 0:1], glr, ETc[:, j:j + 1], ALU.mult, [gr[k], ETc], s, eng="pool")
                            self.tt(s[:, 1:2], gli, ETs[:, j:j + 1], ALU.mult, [gi[k], ETs], s, eng="pool")
                            self.tt(s[:, 2:3], glr, ETs[:, j:j + 1], ALU.mult, [gr[k], ETs], s, eng="pool")
                            self.tt(s[:, 3:4], gli, ETc[:, j:j + 1], ALU.mult, [gi[k], ETc], s, eng="pool")
                            self.tt(inr[:, j:j + 1], s[:, 0:1], s[:, 1:2], ALU.subtract, [s], dcar[j], eng="pool")
                            self.tt(ini[:, j:j + 1], s[:, 2:3], s[:, 3:4], ALU.add, [s], dcar[j], eng="pool")
                            self.tt(w[0][:], gr[k][:], R(Cj), ALU.mult, [gr[k], tc_], w[0])
                            self.tt(w[1][:], gi[k][:], R(Sj), ALU.mult, [gi[k], ts_], w[1], eng="pool")
                            self.tt(w[2][:], gi[k][:], R(Cj), ALU.mult, [gi[k], tc_], w[2])
                            self.tt(w[3][:], gr[k][:], R(Sj), ALU.mult, [gr[k], ts_], w[3], eng="pool")
                            self.tt(hr[k][:], w[0][:], w[1][:], ALU.subtract, [w[0], w[1]], hr[k], eng="pool")
                            self.stt(hi[k][:], w[2][:], -1.0, w[3][:], ALU.mult, ALU.subtract, [w[2], w[3]], hi[k])
                            py = self.bank(6, 8)
                            self.mm(py[0:32, 0:TB], cre[:, j, :], hr[k][:], True, False, [cre, hr[k]], py)
                            self.mm(py[0:32, 0:TB], cim[:, j, :], hi[k][:], False, True, [cim, hi[k]], py)
                            y = ys[k]
                            self.cp(y[:], py[0:32, 0:TB], [py], y, eng="act")
                            self.dma(self.ys5[dr, 32 * j:32 * j + 32, tok0 + c0:tok0 + c0 + TB], y[:], [y])
                            if (not isS) and ((bk == 0) if rev else (bk == NBK - 1)):
                                cl = tc_[:, j, TB - 1:TB]; sl = ts_[:, j, TB - 1:TB]
                                self.tt(s[:, 0:1], glr, cl, ALU.mult, [gr[k], tc_], s, eng="pool")
                                self.tt(s[:, 1:2], gli, sl, ALU.mult, [gi[k], ts_], s, eng="pool")
                                self.tt(s[:, 2:3], glr, sl, ALU.mult, [gr[k], ts_], s, eng="pool")
                                self.tt(s[:, 3:4], gli, cl, ALU.mult, [gi[k], tc_], s, eng="pool")
                                self.tt(fr[:, j:j + 1], s[:, 0:1], s[:, 1:2], ALU.subtract, [s], fr, eng="pool")
                                self.tt(fi[:, j:j + 1], s[:, 2:3], s[:, 3:4], ALU.add, [s], fi, eng="pool")
                    if not isS:
                        self.dma(self.o_s5r[seq, l, dr], fr[:], [fr]); self.dma(self.o_s5i[seq, l, dr], fi[:], [fi])
                O2.__exit__(None, None, None)

    def phase_s5_post(self, l, tok0, ntok):
        with self.pool() as A:
            sd = A.t([128, 8]); self.dma(sd[:], self.s5_dT[l], w=sd)
            wg = A.t([128, 8, 1024], BF16)
            self.load_w(wg, self.w_glu[l], [(0, 1024, 0)], 8)
            gss = A.t([128, 8, 512], BF16)
            ya = [A.t([128, 512]) for _ in range(2)]; yb = [A.t([128, 512]) for _ in range(2)]; ub = [A.t([128, 512], BF16) for _ in range(2)]
            w = [A.t([128, 512]) for _ in range(3)]; sg = A.t([128, 512]); yo = [A.t([128, 512], BF16) for _ in range(2)]
            for g in range((ntok + 511) // 512):
                n = min(512, ntok - g * 512); c0 = tok0 + g * 512
                for j in range(8):
                    a = ya[j % 2]; b = yb[j % 2]; u = ub[j % 2]
                    self.dma(a[:, 0:n], self.ys5[0, j * 128:(j + 1) * 128, c0:c0 + n], w=a)
                    self.dma(b[:, 0:n], self.ys5[1, j * 128:(j + 1) * 128, c0:c0 + n], w=b)
                    self.dma(u[:, 0:n], self.s5uT[j * 128:(j + 1) * 128, c0:c0 + n], w=u)
                    self.tt(a[:, 0:n], a[:, 0:n], b[:, 0:n], ALU.add, [a, b], a, eng="pool")
                    self.stt(a[:, 0:n], u[:, 0:n], sd[:, j:j + 1], a[:, 0:n], ALU.mult, ALU.add, [u, sd, a], a)
                    self.act(w[0][:, 0:n], a[:, 0:n], AF.Square, [a], w[0])
                    self.ts(w[0][:, 0:n], w[0][:, 0:n], 0.044715, ALU.mult, [w[0]], w[0], s2=1.0, op1=ALU.add)
                    self.tt(w[1][:, 0:n], w[0][:, 0:n], a[:, 0:n], ALU.mult, [w[0], a], w[1])
                    self.act(w[2][:, 0:n], w[1][:, 0:n], AF.Sigmoid, [w[1]], w[2], scale=1.5957691216057308)
                    self.tt(gss[:, j, 0:n], a[:, 0:n], w[2][:, 0:n], ALU.mult, [a, w[2]], gss)
                for j in range(8):
                    pb = self.bank()
                    for k in range(8):
                        self.mm(pb[:, 0:n], wg[:, k, j * 128:(j + 1) * 128], gss[:, k, 0:n], k == 0, k == 7, [wg, gss], pb)
                    self.act(sg[:, 0:n], pb[:, 0:n], AF.Sigmoid, [pb], sg)
                    y = yo[j % 2]
                    self.tt(y[:, 0:n], gss[:, j, 0:n], sg[:, 0:n], ALU.mult, [gss, sg], y)
                    self.dma(self.yT[2, j * 128:(j + 1) * 128, c0:c0 + n], y[:, 0:n], [y])

    def phase_merge(self, l, xsrc, xdst, tok0, ntok, cnd):
        ntile = ntok // 128
        nch = (ntok + 511) // 512
        with self.pool() as O:
            mT = O.t([128, 16, ntok], BF16)
            with self.pool() as A:
                yb = A.t([128, 4, 8, ntok], BF16)
                for b in range(4):
                    self.dma(yb[:, b], self.yT[b].rearrange("(k p) t -> p k t", p=128)[:, :, tok0:tok0 + ntok], w=yb)
                wts = [A.t([128, 8, 128], BF16) for _ in range(4)]
                gt = [A.t([128, 512], BF16) for _ in range(4)]
                acc = [A.t([128, 512]) for _ in range(2)]; tmp = [A.t([128, 512]) for _ in range(2)]
                i = 0
                for dc in range(16):
                    wl = []
                    for b in range(4):
                        w = wts[b]
                        self.load_w(w, self.w_br[l, b], [(dc * 128, 128, 0)], 8)
                        wl.append(w)
                    for j in range(nch):
                        n = min(512, ntok - j * 512)
                        a = acc[i % 2]; i += 1
                        for b in range(4):
                            pb = self.bank()
                            for k in range(8):
                                self.mm(pb[:, 0:n], wl[b][:, k, :], yb[:, b, k, j * 512:j * 512 + n], k == 0, k == 7, [wl[b], yb], pb)
                            g = gt[b]
                            self.dma(g[:, 0:n], self.gateT[b * 2048 + dc * 128: b * 2048 + (dc + 1) * 128, tok0 + j * 512: tok0 + j * 512 + n], w=g)
                            if b == 0:
                                self.tt(a[:, 0:n], pb[:, 0:n], g[:, 0:n], ALU.mult, [pb, g], a)
                            else:
                                t = tmp[b % 2]
                                self.tt(t[:, 0:n], pb[:, 0:n], g[:, 0:n], ALU.mult, [pb, g], t)
                                if b < 3:
                                    self.tt(a[:, 0:n], a[:, 0:n], t[:, 0:n], ALU.add, [a, t], a, eng="pool")
                                else:
                                    self.tt(mT[:, dc, j * 512:j * 512 + n], a[:, 0:n], t[:, 0:n], ALU.add, [a, t], mT, eng="pool")
            with self.pool() as A:
                gb = A.t([128, D])
                self.dma(gb[:], self.gsc[0, cnd:cnd + 1, :].partition_broadcast(128), w=gb)
                self.resid_out(A, mT, 16, self.w_o[l], xsrc, xdst, tok0, ntile, gb)

    def resid_out(self, A, aT, kch, wsrc, xsrc, xdst, tok0, ntile, gb):
        wts = [A.t([128, kch, 512], BF16) for _ in range(2)]
        xt = [A.t([128, 512]) for _ in range(3)]; tm = [A.t([128, 512]) for _ in range(2)]
        i = 0
        for nb in range(4):
            w = wts[nb % 2]
            self.load_w(w, wsrc, [(nb * 512, 512, 0)], kch)
            for t in range(ntile):
                r0 = tok0 + t * 128
                pb = self.bank()
                for k in range(kch):
                    self.mm(pb[:], aT[:, k, t * 128:(t + 1) * 128], w[:, k, :], k == 0, k == kch - 1, [aT, w], pb)
                x = xt[i % 3]; tt_ = tm[i % 2]; i += 1
                self.dma(x[:], xsrc[r0:r0 + 128, nb * 512:(nb + 1) * 512], w=x)
                self.tt(tt_[:], pb[:], gb[:, nb * 512:(nb + 1) * 512], ALU.mult, [pb, gb], tt_)
                self.tt(x[:], x[:], tt_[:], ALU.add, [x, tt_], x, eng="pool")
                self.dma(xdst[r0:r0 + 128, nb * 512:(nb + 1) * 512], x[:], [x])

    def phase_ffn(self, l, xbuf, tok0, ntok, cnd):
        ntile = ntok // 128
        nch = (ntok + 511) // 512
        HC = FFN_H // 128
        HH = HC // 2
        with self.pool() as O:
            hT = O.t([128, 16, ntok], BF16)
            with self.pool() as A0:
                self.norm_T(A0, xbuf, tok0, ntile, hT,
                            lambda kc: self.A2[:, cnd, kc:kc + 1], lambda kc: self.modT[:, 48 + kc, cnd:cnd + 1])
            gb = O.t([128, D])
            self.dma(gb[:], self.gsc[1, cnd:cnd + 1, :].partition_broadcast(128), w=gb)
            for half in range(2):
                with self.pool() as A:
                    aT = A.t([128, HH, ntok], BF16)
                    sa = [A.t([128, 512]) for _ in range(2)]
                    was = [A.t([128, 16, 128], BF16) for _ in range(3)]; wbs = [A.t([128, 16, 128], BF16) for _ in range(3)]
                    for hc in range(HH):
                        col = (half * HH + hc) * 128
                        wa = was[hc % 3]; wb = wbs[hc % 3]
                        self.load_w(wa, self.w_f1[l], [(col, 128, 0)]); self.load_w(wb, self.w_f1[l], [(FFN_H + col, 128, 0)])
                        for j in range(nch):
                            n = min(512, ntok - j * 512)
                            pa = self.bank(); pb = self.bank()
                            for k in range(16):
                                self.mm(pa[:, 0:n], wa[:, k, :], hT[:, k, j * 512:j * 512 + n], k == 0, k == 15, [wa, hT], pa)
                            for k in range(16):
                                self.mm(pb[:, 0:n], wb[:, k, :], hT[:, k, j * 512:j * 512 + n], k == 0, k == 15, [wb, hT], pb)
                            s = sa[j % 2]
                            self.act(s[:, 0:n], pa[:, 0:n], AF.Silu, [pa], s)
                            self.tt(aT[:, hc, j * 512:j * 512 + n], s[:, 0:n], pb[:, 0:n], ALU.mult, [s, pb], aT)
                    self.resid_out(A, aT, HH, self.w_f2[l][half * HH * 128:(half + 1) * HH * 128, :], xbuf, xbuf, tok0, ntile, gb)
                    self.P.barrier()

    def phase_final(self, xsrc, out, ntok):
        with self.pool() as A:
            fb = A.t([128, D]); self.dma(fb[:], self.fnorm.partition_broadcast(128), w=fb)
            xt = [A.t([128, D]) for _ in range(2)]; junk = A.t([128, D], BF16); ss = A.t([128, ntok // 128])
            for t in range(ntok // 128):
                x = xt[t % 2]
                self.dma(x[:], xsrc[t * 128:(t + 1) * 128, :], w=x)
                self.act(junk[:], x[:], AF.Square, [x], [junk, ss], accum=ss[:, t:t + 1])
                self.act(ss[:, t:t + 1], ss[:, t:t + 1], AF.Sqrt, [ss, self.eps], ss, scale=1.0 / D, bias=self.eps[:, 0:1])
                self.recip(ss[:, t:t + 1], ss[:, t:t + 1], [ss], ss)
                self.stt(x[:], x[:], ss[:, t:t + 1], fb[:], ALU.mult, ALU.mult, [x, ss, fb], x)
                self.dma(out[t * 128:(t + 1) * 128, :], x[:], [x])

    def build(self):
        import os
        c = self.cfg
        self.setup()
        lim = int(os.environ.get("MK_STOP", "100000"))
        self._pc = 0
        for nm in ("phase_ada", "phase_win", "phase_ctx", "phase_mlstm", "phase_s5", "phase_mla", "phase_diff",
                   "phase_mlstm_post", "phase_s5_post", "phase_merge", "phase_ffn", "phase_final"):
            def mk(fn, nm=nm):
                def w(*a, **k):
                    self._pc += 1
                    if self._pc > lim:
                        return
                    if os.environ.get("MK_VERBOSE"):
                        print("PHASE", self._pc, nm, flush=True)
                    return fn(*a, **k)
                return w
            setattr(self, nm, mk(getattr(self, nm)))
        for l in range(DEPTH):
            self.phase_ada(l)
            xin = self.xp if l == 0 else self.xcp
            self.phase_win(l, xin, 0, self.TPT, 0, False)
            for s in range(c.NPR):
                t0 = s * c.TP
                self.phase_mlstm(l, t0, c.TP, False, s)
                self.phase_s5(l, t0, c.TP, False, s)
                self.phase_mla(l, t0, c.TP, False)
                self.phase_diff(l, t0, c.TP, False)
            self.phase_mlstm_post(l, 0, self.TPT)
            self.phase_s5_post(l, 0, self.TPT)
            self.phase_merge(l, xin, self.xcp, 0, self.TPT, 0)
            self.phase_ffn(l, self.xcp, 0, self.TPT, 0)
            xin = self.xs if l == 0 else self.xcs
            for g0 in range(0, c.TS, c.GW):
                self.phase_win(l, xin, g0, min(c.GW, c.TS - g0), 1, True)
            self.phase_ctx(l)
            self.phase_mlstm(l, 0, c.TS, True, 0)
            self.phase_s5(l, 0, c.TS, True, 0)
            self.phase_mla(l, 0, c.TS, True)
            self.phase_diff(l, 0, c.TS, True)
            self.phase_mlstm_post(l, 0, c.TS)
            self.phase_s5_post(l, 0, c.TS)
            for g0 in range(0, c.TS, c.G):
                n = min(c.G, c.TS - g0)
                self.phase_merge(l, xin, self.xcs, g0, n, 1)
                self.phase_ffn(l, self.xcs, g0, n, 1)
        self.phase_final(self.xcp, self.o_yp, self.TPT)
        self.phase_final(self.xcs, self.o_ys, c.TS)
        self.P.emit()
        return self.nc


def _pT(a, n):
    sh = a.shape[:-1]
    return np.ascontiguousarray(np.swapaxes(a.reshape(sh + (n, 128)), -1, -2))


def _SL(a):
    sh = a.shape[:-2]
    b = a.reshape(sh + (32, 2, 64))
    nd = len(sh)
    b = np.transpose(b, tuple(range(nd)) + (nd + 1, nd + 2, nd))
    return np.ascontiguousarray(b.reshape(sh + (128, 32)))


def _unSL(a):
    sh = a.shape[:-2]
    b = a.reshape(sh + (2, 64, 32))
    nd = len(sh)
    b = np.transpose(b, tuple(range(nd)) + (nd + 2, nd, nd + 1))
    return np.ascontiguousarray(b.reshape(sh + (64, 64)))


def _consts(TS):
    f = np.float32
    c = {}
    c["c_ident"] = np.eye(128, dtype=f)
    c["c_ones"] = np.ones((128, 128), f)
    s = np.arange(64)[:, None]; t = np.arange(64)[None, :]
    c["c_maskL"] = np.where(s <= t, 0.0, NEG).astype(f)
    c["c_maskU"] = np.where(s >= t, 0.0, NEG).astype(f)
    sel = np.zeros((4, 4, 128), f)
    for h in range(4):
        sel[h, h, :] = 1.0
    c["c_sel"] = sel
    R = np.zeros((64, 64), f)
    for base in (0, 32):
        for j in range(16):
            R[base + j, base + j + 16] = -1.0
            R[base + j + 16, base + j] = 1.0
    R128 = np.zeros((128, 128), f)
    R128[:64, :64] = R; R128[64:, 64:] = R
    c["c_ropeR"] = np.ascontiguousarray(R128.T)
    inv = (np.float32(10000.0) ** (-np.arange(16, dtype=f) / np.float32(16))).astype(f)
    tt = np.arange(TS)
    rows = (tt // GRID_W).astype(f); cols = (tt % GRID_W).astype(f)
    C = np.zeros((128, TS), f); S = np.zeros((128, TS), f)
    for p in range(128):
        q = p % 64
        pos = rows if q < 32 else cols
        ang = (pos * inv[q % 16]).astype(f)
        C[p] = np.cos(ang); S[p] = np.sin(ang)
    c["c_ropeC"] = C; c["c_ropeS"] = S
    return c


def _weights(I):
    f = np.float32
    L = DEPTH
    w = {}
    w["w_ada"] = I["w_ada"]; w["b_adaT"] = _pT(I["b_ada"], 96)
    w["nmixT"] = _pT(I["norm_mix"], 16); w["nffnT"] = _pT(I["norm_ffn"], 16)
    w["w_in"] = I["w_in"]; w["mlifb"] = np.ascontiguousarray(np.swapaxes(I["ml_if_bias"].reshape(L, 4, 4), 1, 2))
    w["mlnormT"] = _pT(I["ml_norm"], 8); w["kvnorm"] = np.ascontiguousarray(I["mla_kv_norm"].reshape(L, 1, 512))
    w["w_kvb"] = I["mla_w_kvb"]
    w["s5_are"] = np.ascontiguousarray(I["s5_a_re"].reshape(L, 2, 1, 4096)); w["s5_aim"] = np.ascontiguousarray(I["s5_a_im"].reshape(L, 2, 1, 4096))
    ldt = np.repeat(I["s5_log_dt"][..., None], 64, axis=-1)
    w["s5_ldt"] = np.ascontiguousarray(ldt.reshape(L, 2, 1, 4096))
    w["s5_areS"] = _SL(I["s5_a_re"]); w["s5_aimS"] = _SL(I["s5_a_im"]); w["s5_ldtS"] = _SL(ldt)
    for nm, src in (("s5_breT", "s5_b_re"), ("s5_bimT", "s5_b_im")):
        b = I[src].reshape(L, 2, 32, 2, 64, 16)
        o = np.zeros((L, 2, 2, 16, 32, 2, 64), f)
        for g2 in range(2):
            o[:, :, g2, :, :, g2, :] = np.transpose(b[:, :, :, g2, :, :], (0, 1, 4, 2, 3))
        w[nm] = o.reshape(L, 2, 32, 4096)
    for nm, src in (("s5_creT", "s5_c_re"), ("s5_cimT", "s5_c_im")):
        cc = I[src].reshape(L, 2, 32, 2, 16, 64)
        o = np.zeros((L, 2, 2, 64, 32, 2, 16), f)
        for g2 in range(2):
            o[:, :, g2, :, :, g2, :] = np.transpose(cc[:, :, :, g2, :, :], (0, 1, 4, 2, 3))
        w[nm] = o.reshape(L, 2, 128, 32, 32)
    w["s5_dT"] = _pT(I["s5_d"], 8); w["w_glu"] = I["s5_w_glu"]
    w["df_lam"] = np.ascontiguousarray(I["df_lambda"].reshape(L, 1, 256)); w["dfnormT"] = _pT(I["df_norm"], 1)
    w["w_br"] = I["w_branch"]; w["w_o"] = I["w_o"]; w["w_f1"] = I["w_ffn_in"]; w["w_f2"] = I["w_ffn_out"]
    w["fnorm"] = np.ascontiguousarray(I["final_norm"].reshape(1, D))
    return w


def run(cfg, inputs, trace=False):
    I = {k: np.asarray(v) for k, v in inputs.items()}
    f = np.float32
    L = DEPTH
    NB = I["x_prompt"].shape[0]
    ncore = NB // cfg.NPR
    nsamp = I["x_sample"].shape[0]
    assert ncore == 2 * nsamp
    b = Builder(cfg)
    nc = b.build()
    shared = _weights(I)
    shared.update(_consts(cfg.TS))
    maps = []
    for c in range(ncore):
        s = c // 2
        m = dict(shared)
        m["xs"] = np.ascontiguousarray(I["x_sample"][s])
        m["xp"] = np.ascontiguousarray(I["x_prompt"][c * cfg.NPR:(c + 1) * cfg.NPR].reshape(cfg.NPR * cfg.TP, D))
        cond = np.stack([I["c_ctx"], I["c"][s]], 0)
        m["condT"] = np.ascontiguousarray(np.transpose(cond.T.reshape(16, 128, 2), (1, 0, 2)))
        m["ckv_c"] = np.ascontiguousarray(I["cache_mla_ckv"][s]); m["kr_c"] = np.ascontiguousarray(I["cache_mla_krope"][s])
        m["dk_c"] = np.ascontiguousarray(I["cache_diff_k"][s].reshape(L, cfg.PAST, 1024))
        m["dv_c"] = np.ascontiguousarray(I["cache_diff_v"][s].reshape(L, cfg.PAST, 1024))
        m["mlc0T"] = np.ascontiguousarray(np.swapaxes(I["state_mlstm_c"][s], -1, -2))
        m["mln0"] = np.ascontiguousarray(np.swapaxes(I["state_mlstm_n"][s].reshape(L, 2, 4, 2, 128), -1, -2)); m["mlm0"] = np.ascontiguousarray(I["state_mlstm_m"][s][..., None])
        m["s5h0r"] = _SL(I["state_s5_re"][s]); m["s5h0i"] = _SL(I["state_s5_im"][s])
        for k in list(m.keys()):
            if k not in b.din:
                raise KeyError(k)
            m[k] = np.ascontiguousarray(m[k], dtype=f)
            assert list(m[k].shape) == b.din[k][0], (k, m[k].shape, b.din[k][0])
        maps.append(m)
    res = run_bass_kernel_spmd(nc, maps, core_ids=list(range(ncore)), trace=trace)
    R = res.results
    TP, NPR, TS = cfg.TP, cfg.NPR, cfg.TS
    yp = np.concatenate([R[c]["o_yp"].reshape(NPR, TP, D) for c in range(ncore)], 0)
    hs = TS // 2
    ys = np.stack([np.concatenate([R[2 * s]["o_ys"][:hs], R[2 * s + 1]["o_ys"][hs:]], 0) for s in range(nsamp)], 0)

    def tokout(name, tail):
        return np.concatenate([np.transpose(R[c][name].reshape((L, NPR, TP) + tail), (1, 0, 2) + tuple(range(3, 3 + len(tail))))
                               for c in range(ncore)], 0)
    ckv = tokout("o_ckv", (512,)); kr = tokout("o_kr", (64,))
    dk = tokout("o_dk", (8, 128)); dv = tokout("o_dv", (8, 128))
    mlc = np.concatenate([np.swapaxes(R[c]["o_mlc"], -1, -2) for c in range(ncore)], 0)
    mln = np.concatenate([np.swapaxes(R[c]["o_mln"], -1, -2).reshape(NPR, L, 2, 4, 256) for c in range(ncore)], 0)
    mlm = np.concatenate([R[c]["o_mlm"][..., 0] for c in range(ncore)], 0)
    s5r = np.concatenate([_unSL(R[c]["o_s5r"]) for c in range(ncore)], 0)
    s5i = np.concatenate([_unSL(R[c]["o_s5i"]) for c in range(ncore)], 0)
    outs = (yp, ys, ckv, kr, dk, dv, mlc, mln, mlm, s5r, s5i)
    outs = tuple(np.ascontiguousarray(o, dtype=f) for o in outs)
    if trace:
        return outs, res
    return outs


def kernel(**inputs):
    return run(Cfg(), inputs)
```

```python
import contextlib
import math
import numpy as np
import ml_dtypes
import concourse.bass as bass
import concourse.mybir as mybir
from concourse.bass_utils import run_bass_kernel_spmd

F32 = mybir.dt.float32
BF16 = mybir.dt.bfloat16
AF = mybir.ActivationFunctionType
ALU = mybir.AluOpType
AX = mybir.AxisListType

ENGS = ("pe", "act", "dve", "pool", "sp")
EPOCH = 12000
DMA_EPOCH = 700
N_DMA_SEMS = 24


class Dep:
    __slots__ = ("writers", "readers", "ps")

    def __init__(self):
        self.writers = []
        self.readers = []
        self.ps = False


class Prog:
    def __init__(self, nc):
        self.nc = nc
        self.ops = {e: [] for e in ENGS}
        self.count = {e: 0 for e in ENGS}
        self.clock = {e: {} for e in ENGS}
        self.pre = {e: [] for e in ENGS}
        self.last_ev = {e: None for e in ENGS}
        self.dma_i = 0
        self.dma_last = {}
        self.sems = {}
        self.semstack = contextlib.ExitStack()

    def _eng_event(self, eng):
        i = self.count[eng]
        self.count[eng] += 1
        return (("e" + eng, i // EPOCH), i % EPOCH + 1)

    def _need(self, eng, ev, waits):
        key, val = ev
        if self.clock[eng].get(key, 0) >= val:
            return
        self.clock[eng][key] = val
        waits.append(ev)

    def op(self, eng, fn, reads=(), writes=(), dma=False, acc=False):
        waits = []
        for ev in self.pre[eng]:
            self._need(eng, ev, waits)
        self.pre[eng] = []
        me = "e" + eng
        writes = list(writes)
        for d in reads:
            if d.ps and d not in writes:
                writes.append(d)
        for d in reads:
            for ev in d.writers:
                if eng == "pe" and not dma and ev[0][0] == "epe":
                    continue
                self._need(eng, ev, waits)
        if not acc:
            for d in writes:
                for ev in d.writers:
                    if not dma and ev[0][0] == me:
                        continue
                    self._need(eng, ev, waits)
                for ev in d.readers:
                    if not dma and ev[0][0] == me:
                        continue
                    self._need(eng, ev, waits)
        if dma:
            slot = self.dma_i % N_DMA_SEMS
            n = self.dma_i // N_DMA_SEMS
            self.dma_i += 1
            prev = self.dma_last.get(slot)
            if prev is not None:
                self._need(eng, prev, waits)
            ev = (("d%d" % slot, n // DMA_EPOCH), 16 * (n % DMA_EPOCH + 1))
            self.dma_last[slot] = ev
        else:
            ev = self._eng_event(eng)
            self.last_ev[eng] = ev
        self.ops[eng].append((fn, waits, ev, dma))
        for d in reads:
            d.readers.append(ev)
        for d in writes:
            d.writers = [ev]
            d.readers = []
        return ev

    def barrier(self):
        evs = [ev for ev in self.last_ev.values() if ev is not None] + list(self.dma_last.values())
        for e in ENGS:
            self.pre[e] = list(evs)

    def _sem(self, key):
        if key not in self.sems:
            self.sems[key] = self.semstack.enter_context(self.nc.semaphore("s_%s_%d" % key))
        return self.sems[key]

    def emit(self):
        nc = self.nc
        self.barrier()
        final = self.pre["sp"]
        for e in ENGS:
            for (fn, waits, ev, dma) in self.ops[e]:
                self._sem(ev[0])
        names = {"pe": "tensor", "act": "scalar", "dve": "vector", "pool": "gpsimd", "sp": "sync"}
        with nc.Block() as block:
            for e in ENGS:
                def section(engine, ops=self.ops[e], last=(e == "sp")):
                    for (fn, waits, ev, dma) in ops:
                        for (k, v) in waits:
                            engine.wait_ge(self.sems[k], v)
                        fn(engine).then_inc(self.sems[ev[0]], 16 if dma else 1)
                    if last:
                        fin = {}
                        for (k, v) in final:
                            fin[k] = max(fin.get(k, 0), v)
                        for k, v in fin.items():
                            engine.wait_ge(self.sems[k], v)
                getattr(block, names[e])(section)


D = 2048
DEPTH = 2
GRID_W = 64
ML_H, ML_DH, ML_CH = 4, 256, 64
MLA_H, MLA_NOPE, MLA_ROPE, MLA_V, MLA_RANK = 8, 128, 64, 128, 512
S5_G, S5_I, S5_P = 64, 16, 64
DF_H, DF_DQK, DF_DV = 8, 64, 128
FFN_H = 5632
RMS_EPS = 1e-6
IN_SIZES = (1024, 1024, 1024, 1024, 16, 1536, 576, 1024, 1024, 1024, 1024, 8192)
IN_OFF = [0]
for _s in IN_SIZES:
    IN_OFF.append(IN_OFF[-1] + _s)
(O_MLQ, O_MLK, O_MLV, O_MLO, O_MLIF, O_MLAQ, O_KVA, O_S5U, O_DFQ, O_DFK, O_DFV, O_GATE, IN_COLS) = IN_OFF
NEG = -1.0e30


class Cfg:
    def __init__(self, ts=4096, tp=256, npr=2, past=256, g=1024, gw=2048):
        self.TS, self.TP, self.NPR, self.PAST, self.G = ts, tp, npr, past, g
        self.GW = min(gw, ts)


def _d(x):
    return x.d if hasattr(x, "d") else x


class Tile:
    def __init__(self, h):
        self.h = h
        self.d = Dep()

    def __getitem__(self, k):
        return self.h[k]


class Pool:
    def __init__(self, b):
        self.b = b
        self.es = contextlib.ExitStack()

    def __enter__(self):
        return self

    def __exit__(self, *a):
        self.b.P.barrier()
        self.es.close()

    def t(self, shape, dt=F32):
        self.b.uid += 1
        return Tile(self.es.enter_context(self.b.nc.sbuf_tensor("t%d" % self.b.uid, list(shape), dt)))


class Builder:
    def __init__(self, cfg):
        self.cfg = cfg
        self.nc = bass.Bass("TRN2", target_bir_lowering=False)
        self.P = Prog(self.nc)
        self.uid = 0
        self.din = {}
        self.dout = {}
        self.rr = 0

    def inp(self, name, shape, dt=F32):
        a = self.nc.dram_tensor(name, list(shape), dt, kind="ExternalInput").ap()
        self.din[name] = (list(shape), dt)
        return a

    def outp(self, name, shape):
        a = self.nc.dram_tensor(name, list(shape), F32, kind="ExternalOutput").ap()
        self.dout[name] = list(shape)
        return a

    def scr(self, name, shape, dt):
        return self.nc.dram_tensor(name, list(shape), dt, kind="Internal").ap()

    def mm(self, out, lhsT, rhs, start, stop, reads, w):
        self.P.op("pe", lambda e: e.matmul(out, lhsT=lhsT, rhs=rhs, start=start, stop=stop),
                  reads=[_d(r) for r in reads], writes=[_d(w)], acc=not start)

    def tr(self, out, in_, ident, reads, w, first):
        self.P.op("pe", lambda e: e.transpose(out=out, in_=in_, identity=ident),
                  reads=[_d(r) for r in reads], writes=[_d(w)], acc=not first)

    def act(self, out, in_, func, reads, w, scale=1.0, bias=0.0, accum=None):
        ws = [_d(x) for x in (w if isinstance(w, (list, tuple)) else [w])]
        if accum is None:
            f = lambda e: e.activation(out=out, in_=in_, func=func, scale=scale, bias=bias)
        else:
            f = lambda e: e.activation(out=out, in_=in_, func=func, scale=scale, bias=bias, accum_out=accum)
        self.P.op("act", f, reads=[_d(r) for r in reads], writes=ws)

    def tt(self, out, in0, in1, op, reads, w, eng="dve"):
        self.P.op(eng, lambda e: e.tensor_tensor(out=out, in0=in0, in1=in1, op=op),
                  reads=[_d(r) for r in reads], writes=[_d(w)])

    def ts(self, out, in0, s1, op0, reads, w, s2=None, op1=None, eng="dve"):
        if op1 is None:
            f = lambda e: e.tensor_scalar(out=out, in0=in0, scalar1=s1, scalar2=None, op0=op0)
        else:
            f = lambda e: e.tensor_scalar(out=out, in0=in0, scalar1=s1, scalar2=s2, op0=op0, op1=op1)
        self.P.op(eng, f, reads=[_d(r) for r in reads], writes=[_d(w)])

    def stt(self, out, in0, scalar, in1, op0, op1, reads, w):
        self.P.op("dve", lambda e: e.scalar_tensor_tensor(out=out, in0=in0, scalar=scalar, in1=in1, op0=op0, op1=op1),
                  reads=[_d(r) for r in reads], writes=[_d(w)])

    def scan(self, out, d0, d1, init, op0, op1, reads, w):
        self.P.op("dve", lambda e: e.tensor_tensor_scan(out=out, data0=d0, data1=d1, initial=init, op0=op0, op1=op1),
                  reads=[_d(r) for r in reads], writes=[_d(w)])

    def cp(self, out, in_, reads, w, eng=None):
        if eng is None:
            self.rr += 1
            eng = "act" if self.rr % 2 else "dve"
        if eng == "act":
            f = lambda e: e.copy(out=out, in_=in_)
        else:
            f = lambda e: e.tensor_copy(out=out, in_=in_)
        self.P.op(eng, f, reads=[_d(r) for r in reads], writes=[_d(w)])

    def recip(self, out, in_, reads, w):
        self.P.op("dve", lambda e: e.reciprocal(out=out, in_=in_), reads=[_d(r) for r in reads], writes=[_d(w)])

    def memset(self, ap, val, w, eng="pool"):
        self.P.op(eng, lambda e: e.memset(ap, val), writes=[_d(w)])

    def dma(self, out, in_, reads=(), w=None, eng="sp", slow=False):
        ws = [] if w is None else [_d(x) for x in (w if isinstance(w, (list, tuple)) else [w])]
        if slow:
            f = lambda e: e.dma_start(out=out, in_=in_, allow_slow_non_contiguous=True)
        else:
            f = lambda e: e.dma_start(out=out, in_=in_)
        self.P.op(eng, f, reads=[_d(r) for r in reads], writes=ws, dma=True)

    def pool(self):
        return Pool(self)

    def setup(self):
        c = self.cfg
        L = DEPTH
        TS, TP, NPR, PAST = c.TS, c.TP, c.NPR, c.PAST
        self.TPT = NPR * TP
        TM = max(TS, self.TPT)
        self.TM = TM
        I = self.inp
        self.xs = I("xs", [TS, D]); self.xp = I("xp", [self.TPT, D])
        self.condT = I("condT", [128, 16, 2])
        self.ckv_c = I("ckv_c", [L, PAST, 512]); self.kr_c = I("kr_c", [L, PAST, 64])
        self.dk_c = I("dk_c", [L, PAST, 1024]); self.dv_c = I("dv_c", [L, PAST, 1024])
        self.mlc0T = I("mlc0T", [L, 2, 4, 256, 256]); self.mln0 = I("mln0", [L, 2, 4, 128, 2])
        self.mlm0 = I("mlm0", [L, 2, 4, 1])
        self.s5h0r = I("s5h0r", [L, 2, 128, 32]); self.s5h0i = I("s5h0i", [L, 2, 128, 32])
        self.w_ada = I("w_ada", [L, D, 6 * D]); self.b_adaT = I("b_adaT", [L, 128, 96])
        self.nmixT = I("nmixT", [L, 128, 16]); self.nffnT = I("nffnT", [L, 128, 16])
        self.w_in = I("w_in", [L, D, IN_COLS]); self.mlifb = I("mlifb", [L, 4, 4])
        self.mlnormT = I("mlnormT", [L, 128, 8]); self.kvnorm = I("kvnorm", [L, 1, 512])
        self.w_kvb = I("w_kvb", [L, 512, 2048])
        self.s5_are = I("s5_are", [L, 2, 1, 4096]); self.s5_aim = I("s5_aim", [L, 2, 1, 4096])
        self.s5_ldt = I("s5_ldt", [L, 2, 1, 4096])
        self.s5_areS = I("s5_areS", [L, 2, 128, 32]); self.s5_aimS = I("s5_aimS", [L, 2, 128, 32])
        self.s5_ldtS = I("s5_ldtS", [L, 2, 128, 32])
        self.s5_breT = I("s5_breT", [L, 2, 32, 4096]); self.s5_bimT = I("s5_bimT", [L, 2, 32, 4096])
        self.s5_creT = I("s5_creT", [L, 2, 128, 32, 32]); self.s5_cimT = I("s5_cimT", [L, 2, 128, 32, 32])
        self.s5_dT = I("s5_dT", [L, 128, 8]); self.w_glu = I("w_glu", [L, 1024, 1024])
        self.df_lam = I("df_lam", [L, 1, 256]); self.dfnormT = I("dfnormT", [L, 128, 1])
        self.w_br = I("w_br", [L, 4, 1024, D]); self.w_o = I("w_o", [L, D, D])
        self.w_f1 = I("w_f1", [L, D, 2 * FFN_H]); self.w_f2 = I("w_f2", [L, FFN_H, D])
        self.fnorm = I("fnorm", [1, D])
        self.c_ident = I("c_ident", [128, 128]); self.c_ones = I("c_ones", [128, 128])
        self.c_maskL = I("c_maskL", [64, 64]); self.c_maskU = I("c_maskU", [64, 64])
        self.c_sel = I("c_sel", [4, 4, 128]); self.c_ropeR = I("c_ropeR", [128, 128])
        self.c_ropeC = I("c_ropeC", [128, TS]); self.c_ropeS = I("c_ropeS", [128, TS])
        O = self.outp
        self.o_ys = O("o_ys", [TS, D]); self.o_yp = O("o_yp", [self.TPT, D])
        self.o_ckv = O("o_ckv", [L, self.TPT, 512]); self.o_kr = O("o_kr", [L, self.TPT, 64])
        self.o_dk = O("o_dk", [L, self.TPT, 1024]); self.o_dv = O("o_dv", [L, self.TPT, 1024])
        self.o_mlc = O("o_mlc", [NPR, L, 2, 4, 256, 256]); self.o_mln = O("o_mln", [NPR, L, 2, 4, 128, 2])
        self.o_mlm = O("o_mlm", [NPR, L, 2, 4, 1])
        self.o_s5r = O("o_s5r", [NPR, L, 2, 128, 32]); self.o_s5i = O("o_s5i", [NPR, L, 2, 128, 32])
        S = self.scr
        self.xcs = S("xcs", [TS, D], F32); self.xcp = S("xcp", [self.TPT, D], F32)
        self.gsc = S("gsc", [2, 2, D], F32)
        self.q_mlT = S("q_mlT", [1024, TM], BF16); self.k_mlT = S("k_mlT", [1024, TM], BF16)
        self.k_ml = S("k_ml", [TM, 1024], BF16); self.v_ml = S("v_ml", [TM, 1024], BF16)
        self.o_mlT = S("o_mlT", [1024, TM], BF16); self.gates = S("gates", [4, 4, TM], F32)
        self.qnT = S("qnT", [1024, TM], BF16); self.qrT = S("qrT", [512, TM], BF16)
        self.ckvT = S("ckvT", [512, TM + PAST], BF16); self.krT = S("krT", [128, TM + PAST], BF16)
        self.s5uT = S("s5uT", [1024, TM], BF16)
        self.dqT = S("dqT", [1024, TM], BF16); self.dkT = S("dkT", [1024, TM + PAST], BF16)
        self.dvv = S("dvv", [TM + PAST, 1024], BF16)
        self.gateT = S("gateT", [8192, TM], BF16); self.yT = S("yT", [4, 1024, TM], BF16)
        self.hfb = S("hfb", [2, TM, 1024], F32); self.ys5 = S("ys5", [2, 1024, TM], F32)
        self.gp = Pool(self)
        g = self.gp
        self.ps = []
        for i in range(8):
            self.uid += 1
            self.ps.append(Tile(g.es.enter_context(self.nc.psum_tensor("ps%d" % i, [128, 512], F32))))
            self.ps[-1].d.ps = True
        self.identf = g.t([128, 128]); self.onesf = g.t([128, 128])
        self.identb = g.t([128, 128], BF16); self.onesb = g.t([128, 128], BF16)
        self.maskL = g.t([64, 64]); self.maskU = g.t([64, 64]); self.sel = g.t([4, 4, 128])
        self.ropeRb = g.t([128, 128], BF16)
        self.modT = g.t([128, 96, 2]); self.A1 = g.t([128, 2, 16]); self.A2 = g.t([128, 2, 16])
        self.eps = g.t([128, 1])
        self.dma(self.identf[:], self.c_ident, w=self.identf); self.dma(self.onesf[:], self.c_ones, w=self.onesf)
        self.dma(self.maskL[:], self.c_maskL, w=self.maskL); self.dma(self.maskU[:], self.c_maskU, w=self.maskU)
        self.dma(self.sel[:], self.c_sel, w=self.sel)
        self.dma(self.ropeRb[:], self.c_ropeR, w=self.ropeRb, eng="pool")
        self.cp(self.identb[:], self.identf[:], [self.identf], self.identb, eng="dve")
        self.cp(self.onesb[:], self.onesf[:], [self.onesf], self.onesb, eng="dve")
        self.memset(self.eps[:], RMS_EPS, self.eps)
        self.psi = 0

    def bank(self, lo=0, hi=4):
        b = lo + self.psi % (hi - lo)
        self.psi += 1
        return self.ps[b]

    def phase_ada(self, l):
        with self.pool() as A:
            scT = A.t([128, 16, 2]); bT = A.t([128, 96]); nm = A.t([128, 16]); nf = A.t([128, 16])
            self.dma(scT[:], self.condT, w=scT)
            self.dma(bT[:], self.b_adaT[l], w=bT); self.dma(nm[:], self.nmixT[l], w=nm); self.dma(nf[:], self.nffnT[l], w=nf)
            self.act(scT[:], scT[:], AF.Silu, [scT], scT)
            wst = [A.t([128, 16, 128]) for _ in range(3)]
            wv = self.w_ada[l].rearrange("(k p) c -> p k c", p=128)
            ps = self.ps[0]
            for cb in range(96):
                w = wst[cb % 3]
                self.dma(w[:], wv[:, :, cb * 128:(cb + 1) * 128], w=w, eng=("sp" if cb % 2 else "act"))
                for kc in range(16):
                    self.mm(ps[:, cb * 2:cb * 2 + 2], w[:, kc, :], scT[:, kc, :], kc == 0, kc == 15, [w, scT], ps)
            m = self.modT
            psv = ps[:, 0:192].rearrange("p (c t) -> p c t", t=2)
            for cnd in range(2):
                self.tt(m[:, :, cnd], psv[:, :, cnd], bT[:], ALU.add, [ps, bT], m)
            g4 = A.t([128, 16]); g5 = A.t([16, 128])
            for cnd in range(2):
                self.stt(self.A1[:, cnd, :], m[:, 16:32, cnd], 1.0, nm[:], ALU.add, ALU.mult, [m, nm], self.A1)
                self.stt(self.A2[:, cnd, :], m[:, 64:80, cnd], 1.0, nf[:], ALU.add, ALU.mult, [m, nf], self.A2)
                for gi, off in enumerate((32, 80)):
                    self.cp(g4[:], m[:, off:off + 16, cnd], [m], g4, eng="dve")
                    p2 = self.ps[1]
                    self.mm(p2[0:16, 0:128], g4[:], self.identf[:], True, True, [g4, self.identf], p2)
                    self.cp(g5[:], p2[0:16, 0:128], [p2], g5, eng="dve")
                    self.dma(self.gsc[gi, cnd].rearrange("(j p) -> j p", p=128), g5[:], [g5])

    def norm_T(self, A, xsrc, tok0, ntile, hT, acol, bcol_fn):
        xt = [A.t([128, D]) for _ in range(2)]
        xn = [A.t([128, D], BF16) for _ in range(2)]
        junk = A.t([128, D], BF16)
        ss = A.t([128, ntile])
        for t in range(ntile):
            x = xt[t % 2]; n = xn[t % 2]
            self.dma(x[:], xsrc[tok0 + t * 128: tok0 + (t + 1) * 128, :], w=x)
            self.act(junk[:], x[:], AF.Square, [x], [junk, ss], accum=ss[:, t:t + 1])
            self.act(ss[:, t:t + 1], ss[:, t:t + 1], AF.Sqrt, [ss], ss, scale=1.0 / D, bias=self.eps[:, 0:1])
            self.recip(ss[:, t:t + 1], ss[:, t:t + 1], [ss], ss)
            self.ts(n[:], x[:], ss[:, t:t + 1], ALU.mult, [x, ss], n)
            for half in range(2):
                pb = self.bank()
                pv = pb[:].bitcast(BF16)
                for j in range(8):
                    kc = half * 8 + j
                    self.tr(pv[:, j * 128:(j + 1) * 128], n[:, kc * 128:(kc + 1) * 128], self.identb[:],
                            [n, self.identb], pb, j == 0)
                self.cp(hT[:, half * 8:(half + 1) * 8, t * 128:(t + 1) * 128],
                        pv[:, 0:1024].rearrange("p (k t) -> p k t", k=8), [pb], hT)
        nt = ntile * 128
        for kc in range(16):
            self.ts(hT[:, kc, 0:nt], hT[:, kc, 0:nt], acol(kc), ALU.mult, [hT, self.A1, self.A2, self.modT], hT,
                    s2=bcol_fn(kc), op1=ALU.add)

    def load_w(self, wt, wsrc, pieces, kch=16):
        wv = wsrc.rearrange("(k p) c -> p k c", p=128)
        for (c0, n, off) in pieces:
            self.dma(wt[:, 0:kch, off:off + n], wv[:, :, c0:c0 + n], w=wt, eng="pool")

    def proj_F(self, A, hT, ntok, wsrc, blocks, kch=16):
        wts = [A.t([128, kch, 128], BF16) for _ in range(6)]
        nch = (ntok + 511) // 512
        for bi, (pieces, M, evac) in enumerate(blocks):
            wt = wts[bi % 6]
            self.load_w(wt, wsrc, pieces, kch)
            for j in range(nch):
                n = min(512, ntok - j * 512)
                pb = self.bank()
                for kc in range(kch):
                    self.mm(pb[0:M, 0:n], wt[:, kc, 0:M], hT[:, kc, j * 512:j * 512 + n], kc == 0, kc == kch - 1, [wt, hT], pb)
                evac(pb, j, M, n)

    def proj_T(self, A, hT, ntile, wsrc, blocks, kch=16):
        wts = [A.t([128, kch, 512], BF16) for _ in range(2)]
        for bi, (c0, ncols, evac) in enumerate(blocks):
            wt = wts[bi % 2]
            self.load_w(wt, wsrc, [(c0, ncols, 0)], kch)
            for t in range(ntile):
                pb = self.bank()
                for kc in range(kch):
                    self.mm(pb[:, 0:ncols], hT[:, kc, t * 128:(t + 1) * 128], wt[:, kc, 0:ncols], kc == 0, kc == kch - 1, [wt, hT], pb)
                evac(pb, t, ncols)

    def phase_win(self, l, xsrc, tok0, ntok, cnd, isS):
        ntile = ntok // 128
        wsrc = self.w_in[l]
        with self.pool() as A:
            hT = A.t([128, 16, ntok], BF16)
            with self.pool() as A0:
                self.norm_T(A0, xsrc, tok0, ntile, hT,
                            lambda kc: self.A1[:, cnd, kc:kc + 1], lambda kc: self.modT[:, kc, cnd:cnd + 1])
            stg = [A.t([128, 512], BF16) for _ in range(4)]
            stf = [A.t([128, 512]) for _ in range(3)]
            self.si = 0

            def stage():
                self.si += 1
                return stg[self.si % 4]

            def stagef():
                self.si += 1
                return stf[self.si % 3]

            if isS:
                rC = A.t([128, ntok]); rS = A.t([128, ntok])
                self.dma(rC[:], self.c_ropeC[:, tok0:tok0 + ntok], w=rC)
                self.dma(rS[:], self.c_ropeS[:, tok0:tok0 + ntok], w=rS)
            bia = A.t([4, 4])
            self.dma(bia[:], self.mlifb[l], w=bia)

            def ev_store(dst, row0, func=None, scale=1.0):
                def ev(pb, j, M, n):
                    s = stage()
                    if func is None and scale == 1.0:
                        self.cp(s[0:M, 0:n], pb[0:M, 0:n], [pb], s)
                    else:
                        self.act(s[0:M, 0:n], pb[0:M, 0:n], func or AF.Copy, [pb], s, scale=scale)
                    self.dma(dst[row0:row0 + M, tok0 + j * 512: tok0 + j * 512 + n], s[0:M, 0:n], [s])
                return ev

            def ev_rope(dst, row0):
                def ev(pb, j, M, n):
                    xb = stage()
                    self.cp(xb[:, 0:n], pb[:, 0:n], [pb], xb)
                    p2 = self.bank()
                    self.mm(p2[:, 0:n], self.ropeRb[:], xb[:, 0:n], True, True, [self.ropeRb, xb], p2)
                    t1 = stagef(); t2 = stagef(); s = stage()
                    self.tt(t1[:, 0:n], pb[:, 0:n], rC[:, j * 512:j * 512 + n], ALU.mult, [pb, rC], t1)
                    self.tt(t2[:, 0:n], p2[:, 0:n], rS[:, j * 512:j * 512 + n], ALU.mult, [p2, rS], t2)
                    self.tt(s[:, 0:n], t1[:, 0:n], t2[:, 0:n], ALU.add, [t1, t2], s, eng="pool")
                    self.dma(dst[row0:row0 + 128, tok0 + j * 512: tok0 + j * 512 + n], s[:, 0:n], [s])
                return ev

            def ev_gate(kind):
                def ev(pb, j, M, n):
                    s = stagef()
                    self.act(s[0:4, 0:n], pb[0:4, 0:n], AF.Identity, [pb, bia], s, bias=bia[:, kind:kind + 1])
                    self.dma(self.gates[kind, :, tok0 + j * 512: tok0 + j * 512 + n], s[0:4, 0:n], [s])
                return ev

            rp = ev_rope if isS else ev_store
            blocks = []
            for b in range(8):
                blocks.append(([(O_MLQ + b * 128, 128, 0)], 128, ev_store(self.q_mlT, b * 128)))
                blocks.append(([(O_MLK + b * 128, 128, 0)], 128, ev_store(self.k_mlT, b * 128, AF.Copy, 0.0625)))
                blocks.append(([(O_MLO + b * 128, 128, 0)], 128, ev_store(self.o_mlT, b * 128, AF.Sigmoid)))
                blocks.append(([(O_MLAQ + b * 192, 128, 0)], 128, ev_store(self.qnT, b * 128)))
                blocks.append(([(O_S5U + b * 128, 128, 0)], 128, ev_store(self.s5uT, b * 128)))
                blocks.append(([(O_DFQ + b * 128, 128, 0)], 128, rp(self.dqT, b * 128)))
                blocks.append(([(O_DFK + b * 128, 128, 0)], 128, rp(self.dkT, b * 128)))
            for k in range(4):
                blocks.append(([(O_MLIF + (k // 2) * 8 + (k % 2) * 4, 4, 0)], 4, ev_gate(k)))
            for p in range(4):
                blocks.append(([(O_MLAQ + (2 * p) * 192 + 128, 64, 0), (O_MLAQ + (2 * p + 1) * 192 + 128, 64, 64)], 128,
                               rp(self.qrT, p * 128)))
            blocks.append(([(O_KVA + 512, 64, 0), (O_KVA + 512, 64, 64)], 128, rp(self.krT, 0)))
            for b in range(64):
                blocks.append(([(O_GATE + b * 128, 128, 0)], 128, ev_store(self.gateT, b * 128, AF.Sigmoid)))
            import os
            sub = int(os.environ.get("MK_SUB", "9"))
            nblk = int(os.environ.get("MK_NBLK", "100000"))
            if sub >= 2:
                self.proj_F(A, hT, ntok, wsrc, blocks[:nblk])
            if sub < 3:
                return

            kvn = A.t([128, 512]); ssk = A.t([128, ntile])
            self.dma(kvn[:], self.kvnorm[l].partition_broadcast(128), w=kvn)
            junk = A.t([128, 512], BF16)

            def evT_store(dst, c0, scale=1.0, outf=None):
                def ev(pb, t, ncols):
                    s = stage()
                    r0 = tok0 + t * 128
                    if scale == 1.0:
                        self.cp(s[:, 0:ncols], pb[:, 0:ncols], [pb], s)
                    else:
                        self.act(s[:, 0:ncols], pb[:, 0:ncols], AF.Copy, [pb], s, scale=scale)
                    self.dma(dst[r0:r0 + 128, c0:c0 + ncols], s[:, 0:ncols], [s])
                    if outf is not None:
                        f = stagef()
                        self.cp(f[:, 0:ncols], pb[:, 0:ncols], [pb], f)
                        self.dma(outf[l, r0:r0 + 128, c0:c0 + ncols], f[:, 0:ncols], [f])
                return ev

            def evT_out(outf, c0):
                def ev(pb, t, ncols):
                    f = stagef()
                    r0 = tok0 + t * 128
                    self.cp(f[:, 0:ncols], pb[:, 0:ncols], [pb], f)
                    self.dma(outf[l, r0:r0 + 128, c0:c0 + ncols], f[:, 0:ncols], [f])
                return ev

            def evT_kva(pb, t, ncols):
                r0 = tok0 + t * 128
                self.act(junk[:], pb[:], AF.Square, [pb], [junk, ssk], accum=ssk[:, t:t + 1])
                self.act(ssk[:, t:t + 1], ssk[:, t:t + 1], AF.Sqrt, [ssk, self.eps], ssk, scale=1.0 / 512, bias=self.eps[:, 0:1])
                self.recip(ssk[:, t:t + 1], ssk[:, t:t + 1], [ssk], ssk)
                f = stagef()
                self.stt(f[:], pb[:], ssk[:, t:t + 1], kvn[:], ALU.mult, ALU.mult, [pb, ssk, kvn], f)
                if not isS:
                    self.dma(self.o_ckv[l, r0:r0 + 128, :], f[:], [f])
                s = stage()
                self.cp(s[:], f[:], [f], s)
                p2 = self.bank()
                pv = p2[:].bitcast(BF16)
                for kc in range(4):
                    self.tr(pv[:, kc * 128:(kc + 1) * 128], s[:, kc * 128:(kc + 1) * 128], self.identb[:], [s, self.identb], p2, kc == 0)
                s2 = stage()
                self.cp(s2[:], pv[:, 0:512], [p2], s2)
                self.dma(self.ckvT.rearrange("(k p) t -> p k t", p=128)[:, :, r0:r0 + 128],
                         s2[:].rearrange("p (k t) -> p k t", k=4), [s2])

            tb = []
            for hlf in range(2):
                tb.append((O_MLK + hlf * 512, 512, evT_store(self.k_ml, hlf * 512, 0.0625)))
                tb.append((O_MLV + hlf * 512, 512, evT_store(self.v_ml, hlf * 512)))
                tb.append((O_DFV + hlf * 512, 512, evT_store(self.dvv, hlf * 512, 1.0, None if (isS or os.environ.get("MK_VAR") == "noout") else self.o_dv)))
                if not isS:
                    tb.append((O_DFK + hlf * 512, 512, evT_out(self.o_dk, hlf * 512)))
            tb.append((O_KVA, 512, evT_kva))
            if not isS:
                tb.append((O_KVA + 512, 64, evT_out(self.o_kr, 0)))
            tb = tb[::-1][:int(os.environ.get("MK_NT", "1000"))]
            self.proj_T(A, hT, ntile, wsrc, tb)

    def phase_ctx(self, l):
        c = self.cfg
        TS, PAST = c.TS, c.PAST
        with self.pool() as A:
            for t in range(PAST // 128):
                r0 = TS + t * 128
                a = A.t([128, 512], BF16); kr = A.t([128, 64], BF16); dk = A.t([128, 1024], BF16); dv = A.t([128, 1024], BF16)
                self.dma(a[:], self.ckv_c[l, t * 128:(t + 1) * 128, :], w=a, eng="pool")
                self.dma(kr[:], self.kr_c[l, t * 128:(t + 1) * 128, :], w=kr, eng="pool")
                self.dma(dk[:], self.dk_c[l, t * 128:(t + 1) * 128, :], w=dk, eng="pool")
                self.dma(dv[:], self.dv_c[l, t * 128:(t + 1) * 128, :], w=dv, eng="pool")
                self.dma(self.dvv[r0:r0 + 128, :], dv[:], [dv])
                p2 = self.bank(); pv = p2[:].bitcast(BF16)
                for kc in range(4):
                    self.tr(pv[:, kc * 128:(kc + 1) * 128], a[:, kc * 128:(kc + 1) * 128], self.identb[:], [a, self.identb], p2, kc == 0)
                s2 = A.t([128, 512], BF16)
                self.cp(s2[:], pv[:, 0:512], [p2], s2)
                self.dma(self.ckvT.rearrange("(k p) t -> p k t", p=128)[:, :, r0:r0 + 128], s2[:].rearrange("p (k t) -> p k t", k=4), [s2])
                p3 = self.bank(); pv3 = p3[:].bitcast(BF16)
                self.tr(pv3[0:64, 0:128], kr[:, 0:64], self.identb[:], [kr, self.identb], p3, True)
                s3 = A.t([64, 128], BF16)
                self.cp(s3[:], pv3[0:64, 0:128], [p3], s3)
                self.dma(self.krT[0:64, r0:r0 + 128], s3[:], [s3]); self.dma(self.krT[64:128, r0:r0 + 128], s3[:], [s3])
                for hh in range(2):
                    p4 = self.bank(); pv4 = p4[:].bitcast(BF16)
                    for j in range(4):
                        b = hh * 4 + j
                        self.tr(pv4[:, j * 128:(j + 1) * 128], dk[:, b * 128:(b + 1) * 128], self.identb[:], [dk, self.identb], p4, j == 0)
                    s4 = A.t([128, 512], BF16)
                    self.cp(s4[:], pv4[:, 0:512], [p4], s4)
                    self.dma(self.dkT.rearrange("(k p) t -> p k t", p=128)[:, hh * 4:(hh + 1) * 4, r0:r0 + 128],
                             s4[:].rearrange("p (k t) -> p k t", k=4), [s4])

    def key_ranges(self, tok0, T, isS):
        r = [(tok0 + i * 128) for i in range(T // 128)]
        if isS:
            r += [(self.cfg.TS + i * 128) for i in range(self.cfg.PAST // 128)]
        return r

    def phase_mla(self, l, tok0, T, isS):
        kts = self.key_ranges(tok0, T, isS)
        NKT = len(kts)
        QC = min(512, T)
        sc = (MLA_NOPE + MLA_ROPE) ** -0.5
        with self.pool() as A:
            ckv = A.t([128, 4, NKT * 128], BF16); kr = A.t([128, NKT * 128], BF16)
            ckvv = self.ckvT.rearrange("(k p) t -> p k t", p=128)
            self.dma(ckv[:, :, 0:T], ckvv[:, :, tok0:tok0 + T], w=ckv); self.dma(kr[:, 0:T], self.krT[:, tok0:tok0 + T], w=kr)
            if isS:
                TS, PA = self.cfg.TS, self.cfg.PAST
                self.dma(ckv[:, :, T:T + PA], ckvv[:, :, TS:TS + PA], w=ckv); self.dma(kr[:, T:T + PA], self.krT[:, TS:TS + PA], w=kr)
            wk = [A.t([128, 4, 128], BF16) for _ in range(2)]; wv = [A.t([128, 4, 128], BF16) for _ in range(2)]
            knT = [A.t([128, NKT * 128], BF16) for _ in range(2)]; vt = [A.t([128, NKT, 128], BF16) for _ in range(2)]
            qn = [A.t([128, T], BF16) for _ in range(2)]; qr = [A.t([128, T], BF16) for _ in range(2)]
            pts = [A.t([128, 512], BF16) for _ in range(4)]
            rc = A.t([128, 512]); ob = [A.t([128, 512], BF16) for _ in range(2)]
            pi = 0
            for h in range(MLA_H):
                b = h % 2
                self.load_w(wk[b], self.w_kvb[l], [(h * 256, 128, 0)], 4)
                self.load_w(wv[b], self.w_kvb[l], [(h * 256 + 128, 128, 0)], 4)
                self.dma(qn[b][:], self.qnT[h * 128:(h + 1) * 128, tok0:tok0 + T], w=qn[b])
                self.dma(qr[b][:], self.qrT[(h // 2) * 128:(h // 2 + 1) * 128, tok0:tok0 + T], w=qr[b])
                nk = NKT * 128
                for j in range((nk + 511) // 512):
                    n = min(512, nk - j * 512)
                    pb = self.bank(0, 3)
                    for kc in range(4):
                        self.mm(pb[:, 0:n], wk[b][:, kc, :], ckv[:, kc, j * 512:j * 512 + n], kc == 0, kc == 3, [wk[b], ckv], pb)
                    self.cp(knT[b][:, j * 512:j * 512 + n], pb[:, 0:n], [pb], knT[b])
                for g in range((NKT + 3) // 4):
                    m = min(4, NKT - g * 4)
                    pb = self.bank(0, 3)
                    for i in range(m):
                        kt = g * 4 + i
                        for kc in range(4):
                            self.mm(pb[:, i * 128:(i + 1) * 128], ckv[:, kc, kt * 128:(kt + 1) * 128], wv[b][:, kc, :],
                                    kc == 0, kc == 3, [wv[b], ckv], pb)
                    self.cp(vt[b][:, g * 4:g * 4 + m, :], pb[:, 0:m * 128].rearrange("p (k t) -> p k t", k=m), [pb], vt[b])
                hp = (h % 2) * 64
                for qc in range(T // QC):
                    q0 = qc * QC
                    ao = self.ps[4 + qc % 2]; ad = self.ps[6 + qc % 2]
                    LAG = 2
                    pend = []
                    for kt in range(NKT + LAG):
                        if kt < NKT:
                            pb = self.bank(0, 4)
                            self.mm(pb[:, 0:QC], knT[b][:, kt * 128:(kt + 1) * 128], qn[b][:, q0:q0 + QC], True, False, [knT[b], qn[b]], pb)
                            self.mm(pb[:, 0:QC], kr[hp:hp + 64, kt * 128:(kt + 1) * 128], qr[b][hp:hp + 64, q0:q0 + QC], False, True, [kr, qr[b]], pb)
                            pt = pts[pi % 4]; pi += 1
                            self.act(pt[:, 0:QC], pb[:, 0:QC], AF.Exp, [pb], pt, scale=sc)
                            pend.append((kt, pt))
                        if kt >= LAG:
                            k2, p2 = pend.pop(0)
                            self.mm(ao[:, 0:QC], vt[b][:, k2, :], p2[:, 0:QC], k2 == 0, k2 == NKT - 1, [vt[b], p2], ao)
                            self.mm(ad[:, 0:QC], self.onesb[:], p2[:, 0:QC], k2 == 0, k2 == NKT - 1, [self.onesb, p2], ad)
                    self.recip(rc[:, 0:QC], ad[:, 0:QC], [ad], rc)
                    o = ob[qc % 2]
                    self.tt(o[:, 0:QC], ao[:, 0:QC], rc[:, 0:QC], ALU.mult, [ao, rc], o)
                    self.dma(self.yT[1, h * 128:(h + 1) * 128, tok0 + q0:tok0 + q0 + QC], o[:, 0:QC], [o])

    def phase_diff(self, l, tok0, T, isS):
        kts = self.key_ranges(tok0, T, isS)
        NKT = len(kts)
        QC = min(512, T)
        lam_init = 0.8 - 0.6 * math.exp(-0.3 * l)
        with self.pool() as A:
            lb = A.t([128, 256]); pr = A.t([128, 128]); sm = A.t([128, 4]); dfs = A.t([128, 1])
            self.dma(lb[:], self.df_lam[l].partition_broadcast(128), w=lb)
            self.dma(dfs[:], self.dfnormT[l], w=dfs)
            self.tt(pr[:, 0:64], lb[:, 0:64], lb[:, 64:128], ALU.mult, [lb], pr)
            self.tt(pr[:, 64:128], lb[:, 128:192], lb[:, 192:256], ALU.mult, [lb], pr)
            self.P.op("dve", lambda e: e.reduce_sum(out=sm[:, 0:1], in_=pr[:, 0:64], axis=AX.X), reads=[pr.d], writes=[sm.d])
            self.P.op("dve", lambda e: e.reduce_sum(out=sm[:, 1:2], in_=pr[:, 64:128], axis=AX.X), reads=[pr.d], writes=[sm.d])
            self.act(sm[:, 0:2], sm[:, 0:2], AF.Exp, [sm], sm)
            self.tt(sm[:, 2:3], sm[:, 1:2], sm[:, 0:1], ALU.subtract, [sm], sm)
            self.ts(sm[:, 3:4], sm[:, 2:3], -lam_init, ALU.add, [sm], sm)
            self.ts(dfs[:], dfs[:], 1.0 - lam_init, ALU.mult, [dfs], dfs)
            kT = [A.t([128, NKT * 128], BF16) for _ in range(2)]; vt = [A.t([128, NKT, 128], BF16) for _ in range(2)]
            q = [A.t([128, T], BF16) for _ in range(2)]
            pts = [A.t([128, 512], BF16) for _ in range(6)]
            rc = A.t([128, 512]); o0 = A.t([128, 512]); o1 = A.t([128, 512]); sq = A.t([128, 512]); ob = [A.t([128, 512], BF16) for _ in range(2)]
            pi = 0
            dvr = self.dvv.rearrange("(k p) c -> p k c", p=128)
            for h in range(DF_H):
                b = h % 2
                self.dma(q[b][:], self.dqT[h * 128:(h + 1) * 128, tok0:tok0 + T], w=q[b])
                self.dma(kT[b][:, 0:T], self.dkT[h * 128:(h + 1) * 128, tok0:tok0 + T], w=kT[b])
                self.dma(vt[b][:, 0:T // 128, :], dvr[:, tok0 // 128:(tok0 + T) // 128, h * 128:(h + 1) * 128], w=vt[b])
                if isS:
                    TS, PA = self.cfg.TS, self.cfg.PAST
                    self.dma(kT[b][:, T:T + PA], self.dkT[h * 128:(h + 1) * 128, TS:TS + PA], w=kT[b])
                    self.dma(vt[b][:, T // 128:NKT, :], dvr[:, TS // 128:(TS + PA) // 128, h * 128:(h + 1) * 128], w=vt[b])
                for qc in range(T // QC):
                    q0 = qc * QC
                    ao = [self.ps[4], self.ps[5]]; ad = [self.ps[6], self.ps[7]]
                    LAG = 1
                    pend = []
                    for kt in range(NKT + LAG):
                        if kt < NKT:
                            for cc in range(2):
                                pb = self.bank(0, 4)
                                lo = cc * 64
                                self.mm(pb[:, 0:QC], kT[b][lo:lo + 64, kt * 128:(kt + 1) * 128], q[b][lo:lo + 64, q0:q0 + QC], True, True, [kT[b], q[b]], pb)
                                pt = pts[pi % 6]; pi += 1
                                self.act(pt[:, 0:QC], pb[:, 0:QC], AF.Exp, [pb], pt, scale=DF_DQK ** -0.5)
                                pend.append((kt, cc, pt))
                        if kt >= LAG:
                            for _ in range(2):
                                k2, cc, p2 = pend.pop(0)
                                self.mm(ao[cc][:, 0:QC], vt[b][:, k2, :], p2[:, 0:QC], k2 == 0, k2 == NKT - 1, [vt[b], p2], ao[cc])
                                self.mm(ad[cc][:, 0:QC], self.onesb[:], p2[:, 0:QC], k2 == 0, k2 == NKT - 1, [self.onesb, p2], ad[cc])
                    self.recip(rc[:, 0:QC], ad[0][:, 0:QC], [ad[0]], rc)
                    self.tt(o0[:, 0:QC], ao[0][:, 0:QC], rc[:, 0:QC], ALU.mult, [ao[0], rc], o0)
                    self.recip(rc[:, 0:QC], ad[1][:, 0:QC], [ad[1]], rc)
                    self.tt(o1[:, 0:QC], ao[1][:, 0:QC], rc[:, 0:QC], ALU.mult, [ao[1], rc], o1)
                    self.stt(o0[:, 0:QC], o1[:, 0:QC], sm[:, 3:4], o0[:, 0:QC], ALU.mult, ALU.add, [o1, sm, o0], o0)
                    self.act(sq[:, 0:QC], o0[:, 0:QC], AF.Square, [o0], sq)
                    pn = self.bank(0, 4)
                    self.mm(pn[:, 0:QC], self.onesf[:], sq[:, 0:QC], True, True, [self.onesf, sq], pn)
                    self.act(sq[:, 0:QC], pn[:, 0:QC], AF.Sqrt, [pn, self.eps], sq, scale=1.0 / DF_DV, bias=self.eps[:, 0:1])
                    self.recip(sq[:, 0:QC], sq[:, 0:QC], [sq], sq)
                    o = ob[qc % 2]
                    self.stt(o[:, 0:QC], o0[:, 0:QC], dfs[:, 0:1], sq[:, 0:QC], ALU.mult, ALU.mult, [o0, dfs, sq], o)
                    self.dma(self.yT[3, h * 128:(h + 1) * 128, tok0 + q0:tok0 + q0 + QC], o[:, 0:QC], [o])

    def phase_mlstm(self, l, tok0, T, isS, seq):
        NCH = T // 64
        for dr in range(2):
            rev = (dr == 1)

            def V(ap):
                return ap[:, ::-1] if rev else ap
            with self.pool() as O:
                rrow = O.t([4, T]); XT = O.t([64, NCH, 16]); wcb = O.t([128, 4, NCH])
                with self.pool() as A:
                    X = [A.t([4, T]) for _ in range(5)]
                    mk = A.t([4, T], BF16); pen = A.t([4, T], BF16)
                    X16 = A.t([16, T])
                    ac = A.t([4, NCH]); cme = A.t([4, NCH]); cmx = A.t([4, NCH]); msq = A.t([4, NCH]); mpv = A.t([4, NCH])
                    cc = A.t([4, NCH]); wc = A.t([4, NCH]); m0 = A.t([4, 1])
                    self.memset(mk[:], 1.0, mk); self.memset(pen[:], 0.0, pen)
                    self.memset(mk[:].rearrange("p (c l) -> p c l", l=64)[:, :, 0:1], 0.0, mk)
                    self.memset(pen[:].rearrange("p (c l) -> p c l", l=64)[:, :, 0:1], NEG, pen)
                    self.dma(X[0][:], self.gates[2 * dr, :, tok0:tok0 + T], w=X[0])
                    self.dma(X[1][:], self.gates[2 * dr + 1, :, tok0:tok0 + T], w=X[1])
                    if isS:
                        self.dma(m0[:], self.mlm0[l, dr], w=m0)
                    else:
                        self.memset(m0[:], 0.0, m0)
                    self.act(X[2][:], X[1][:], AF.Abs, [X[1]], X[2])
                    self.act(X[2][:], X[2][:], AF.Exp, [X[2]], X[2], scale=-1.0)
                    self.act(X[2][:], X[2][:], AF.Ln, [X[2]], X[2], bias=1.0)
                    self.ts(X[3][:], X[1][:], 0.0, ALU.min, [X[1]], X[3])
                    self.tt(X[1][:], X[3][:], X[2][:], ALU.subtract, [X[3], X[2]], X[1])
                    self.scan(V(X[2][:]), mk[:], V(X[1][:]), 0.0, ALU.mult, ALU.add, [mk, X[1]], X[2])
                    self.tt(X[0][:], X[0][:], X[2][:], ALU.subtract, [X[0], X[2]], X[0])
                    self.scan(V(X[1][:]), pen[:], V(X[0][:]), NEG, ALU.add, ALU.max, [pen, X[0]], X[1])
                    e = 0 if rev else 63
                    b3 = X[2][:].rearrange("p (c l) -> p c l", l=64); c3 = X[1][:].rearrange("p (c l) -> p c l", l=64)
                    self.cp(ac[:].unsqueeze(2), b3[:, :, e:e + 1], [X[2]], ac, eng="dve")
                    self.cp(cme[:].unsqueeze(2), c3[:, :, e:e + 1], [X[1]], cme, eng="dve")
                    self.tt(cmx[:], ac[:], cme[:], ALU.add, [ac, cme], cmx)

                    def Vc(ap):
                        return ap[:, ::-1] if rev else ap
                    self.scan(Vc(msq[:]), Vc(ac[:]), Vc(cmx[:]), m0[:, 0:1], ALU.add, ALU.max, [ac, cmx, m0], msq)
                    if NCH > 1:
                        if rev:
                            self.cp(mpv[:, 0:NCH - 1], msq[:, 1:NCH], [msq], mpv, eng="dve")
                        else:
                            self.cp(mpv[:, 1:NCH], msq[:, 0:NCH - 1], [msq], mpv, eng="dve")
                    pe_ = NCH - 1 if rev else 0
                    self.cp(mpv[:, pe_:pe_ + 1], m0[:, 0:1], [m0], mpv, eng="dve")
                    if not isS:
                        fe = 0 if rev else NCH - 1
                        self.dma(self.o_mlm[seq, l, dr], msq[:, fe:fe + 1], [msq])
                    mpb = mpv[:].unsqueeze(2).to_broadcast([4, NCH, 64])
                    r3 = rrow[:].rearrange("p (c l) -> p c l", l=64)
                    self.tt(c3, c3, mpb, ALU.max, [X[1], mpv], X[1])
                    self.ts(rrow[:], X[1][:], -1.0, ALU.mult, [X[1]], rrow)
                    x3 = X[3][:].rearrange("p (c l) -> p c l", l=64)
                    self.tt(x3, r3, mpb, ALU.add, [rrow, mpv], X[3])
                    self.act(X[3][:], X[3][:], AF.Exp, [X[3]], X[3])
                    self.tt(X[2][:], rrow[:], X[2][:], ALU.subtract, [rrow, X[2]], X[2])
                    self.act(X[2][:], X[2][:], AF.Exp, [X[2]], X[2])
                    self.tt(cc[:], ac[:], msq[:], ALU.subtract, [ac, msq], cc)
                    x4 = X[4][:].rearrange("p (c l) -> p c l", l=64)
                    self.tt(x4, X[0][:].rearrange("p (c l) -> p c l", l=64), cc[:].unsqueeze(2).to_broadcast([4, NCH, 64]),
                            ALU.add, [X[0], cc], X[4])
                    self.act(X[4][:], X[4][:], AF.Exp, [X[4]], X[4])
                    self.tt(wc[:], cc[:], mpv[:], ALU.add, [cc, mpv], wc)
                    self.act(wc[:], wc[:], AF.Exp, [wc], wc)
                    for hh in range(4):
                        pb = self.bank()
                        self.mm(pb[:, 0:NCH], self.sel[:, hh, :], wc[:], True, True, [self.sel, wc], pb)
                        self.cp(wcb[:, hh, :], pb[:, 0:NCH], [pb], wcb)
                    for qi, src in enumerate((X[0], X[3], X[2], X[4])):
                        self.dma(X16[4 * qi:4 * qi + 4, :], src[:], [src], w=X16)
                    for g in range((NCH + 31) // 32):
                        m = min(32, NCH - g * 32)
                        pb = self.bank()
                        for i in range(m):
                            c = g * 32 + i
                            self.mm(pb[0:64, i * 16:(i + 1) * 16], X16[:, c * 64:(c + 1) * 64], self.identf[0:16, 0:16], True, True,
                                    [X16, self.identf], pb)
                        self.cp(XT[:, g * 32:g * 32 + m, :], pb[0:64, 0:m * 16].rearrange("p (c q) -> p c q", q=16), [pb], XT)
                with self.pool() as A:
                    CT = [A.t([128, 2, 257]) for _ in range(4)]; CTb = [A.t([128, 2, 257], BF16) for _ in range(4)]
                    for hh in range(4):
                        if isS:
                            self.dma(CT[hh][:, :, 0:256], self.mlc0T[l, dr, hh].rearrange("(k p) v -> p k v", p=128), w=CT[hh])
                            n0t = A.t([128, 2])
                            self.dma(n0t[:], self.mln0[l, dr, hh], w=n0t)
                            self.cp(CT[hh][:, :, 256], n0t[:], [n0t], CT[hh], eng="dve")
                        else:
                            self.memset(CT[hh][:], 0.0, CT[hh])
                        self.cp(CTb[hh][:], CT[hh][:], [CT[hh]], CTb[hh])
                    NB = 2
                    qc_ = [A.t([128, 8, 64], BF16) for _ in range(NB)]; kc_ = [A.t([128, 8, 64], BF16) for _ in range(NB)]
                    kt_ = [A.t([64, 1024], BF16) for _ in range(NB)]; va_ = [A.t([64, 4, 257], BF16) for _ in range(NB)]
                    for v in va_:
                        self.memset(v[:, :, 256:257], 1.0, v)
                    Dt = [A.t([64, 64]) for _ in range(4)]; SD = [A.t([64, 64], BF16) for _ in range(4)]
                    isb = [A.t([64, 257]) for _ in range(4)]; nd = [A.t([64, 257]) for _ in range(4)]
                    dn = [A.t([64, 2]) for _ in range(4)]; wv = [A.t([64, 257], BF16) for _ in range(4)]
                    hst = [A.t([64, 1024]) for _ in range(2)]
                    mask = self.maskU if rev else self.maskL
                    qv = self.q_mlT.rearrange("(j p) t -> p j t", p=128); kv = self.k_mlT.rearrange("(j p) t -> p j t", p=128)
                    order = range(NCH - 1, -1, -1) if rev else range(NCH)
                    for ci, c in enumerate(order):
                        bb = ci % NB
                        t0 = tok0 + c * 64
                        Q = qc_[bb]; K = kc_[bb]; KT = kt_[bb]; VA = va_[bb]; H = hst[ci % 2]
                        self.dma(Q[:], qv[:, :, t0:t0 + 64], w=Q); self.dma(K[:], kv[:, :, t0:t0 + 64], w=K)
                        self.dma(KT[:], self.k_ml[t0:t0 + 64, :], w=KT)
                        self.dma(VA[:, :, 0:256], self.v_ml[t0:t0 + 64, :].rearrange("s (h v) -> s h v", h=4), w=VA)
                        PA = [self.ps[2 * hh] for hh in range(4)]; PB = [self.ps[2 * hh + 1] for hh in range(4)]
                        for hh in range(4):
                            pa = PA[hh]; pbk = PB[hh]
                            for k2 in range(2):
                                self.mm(pa[0:64, 0:64], K[:, hh * 2 + k2, :], Q[:, hh * 2 + k2, :], k2 == 0, k2 == 1, [K, Q], pa)
                            self.mm(pbk[0:64, 0:64], self.sel[:, hh, 0:64], rrow[:, c * 64:(c + 1) * 64], True, False, [self.sel, rrow], pbk)
                            self.mm(pbk[0:64, 0:64], self.identf[0:64, 0:64], mask[:], False, True, [self.identf, mask], pbk)
                        for hh in range(4):
                            pa = PA[hh]; pbk = PB[hh]
                            self.act(Dt[hh][:], pbk[0:64, 0:64], AF.Exp, [pbk, XT], Dt[hh], bias=XT[:, c, hh:hh + 1])
                            self.tt(SD[hh][:], pa[0:64, 0:64], Dt[hh][:], ALU.mult, [pa, Dt[hh]], SD[hh])
                            self.act(wv[hh][:], VA[:, hh, :], AF.Copy, [VA, XT], wv[hh], scale=XT[:, c, 12 + hh:13 + hh])
                        for hh in range(4):
                            pa = PA[hh]; pbk = PB[hh]
                            self.mm(pbk[0:64, 0:257], SD[hh][:], VA[:, hh, :], True, True, [SD[hh], VA], pbk)
                            for k2 in range(2):
                                self.mm(pa[0:64, 0:257], Q[:, hh * 2 + k2, :], CTb[hh][:, k2, :], k2 == 0, k2 == 1, [Q, CTb[hh]], pa)
                        for hh in range(4):
                            pa = PA[hh]; pbk = PB[hh]
                            self.cp(isb[hh][:], pbk[0:64, 0:257], [pbk], isb[hh], eng="act")
                            self.stt(nd[hh][:], pa[0:64, 0:257], XT[:, c, 4 + hh:5 + hh], isb[hh][:], ALU.mult, ALU.add, [pa, XT, isb[hh]], nd[hh])
                        for hh in range(4):
                            pa = PA[hh]; pbk = PB[hh]
                            for k2, pp in enumerate((pa, pbk)):
                                self.mm(pp[:, 0:257], KT[:, hh * 256 + k2 * 128: hh * 256 + (k2 + 1) * 128], wv[hh][:], True, True, [KT, wv[hh]], pp)
                        for hh in range(4):
                            pa = PA[hh]; pbk = PB[hh]
                            for k2, pp in enumerate((pa, pbk)):
                                self.stt(CT[hh][:, k2, :], CT[hh][:, k2, :], wcb[:, hh, c:c + 1], pp[:, 0:257], ALU.mult, ALU.add,
                                         [CT[hh], wcb, pp], CT[hh])
                            self.cp(CTb[hh][:], CT[hh][:], [CT[hh]], CTb[hh], eng="pool")
                        for hh in range(4):
                            self.stt(dn[hh][:, 0:1], nd[hh][:, 256:257], -1.0, nd[hh][:, 256:257], ALU.mult, ALU.max, [nd[hh]], dn[hh])
                            self.ts(dn[hh][:, 0:1], dn[hh][:, 0:1], XT[:, c, 8 + hh:9 + hh], ALU.max, [dn[hh], XT], dn[hh])
                            self.recip(dn[hh][:, 1:2], dn[hh][:, 0:1], [dn[hh]], dn[hh])
                            self.ts(H[:, hh * 256:(hh + 1) * 256], nd[hh][:, 0:256], dn[hh][:, 1:2], ALU.mult, [nd[hh], dn[hh]], H)
                        self.dma(self.hfb[dr, t0:t0 + 64, :], H[:], [H])
                    if not isS:
                        for hh in range(4):
                            self.dma(self.o_mlc[seq, l, dr, hh].rearrange("(k p) v -> p k v", p=128), CT[hh][:, :, 0:256], [CT[hh]])
                            n1t = A.t([128, 2])
                            self.cp(n1t[:], CT[hh][:, :, 256], [CT[hh]], n1t, eng="dve")
                            self.dma(self.o_mln[seq, l, dr, hh], n1t[:], [n1t])

    def phase_mlstm_post(self, l, tok0, ntok):
        with self.pool() as A:
            mn = A.t([128, 8]); self.dma(mn[:], self.mlnormT[l], w=mn)
            hf = [A.t([128, 1024]) for _ in range(2)]; hb = [A.t([128, 1024]) for _ in range(2)]
            junk = A.t([128, 256], BF16); ss = A.t([128, 4]); hn = A.t([128, 4, 1024], BF16)
            ot = [A.t([128, 512], BF16) for _ in range(2)]; yo = [A.t([128, 512], BF16) for _ in range(2)]
            for g in range((ntok + 511) // 512):
                n = min(512, ntok - g * 512); nt = n // 128
                for t in range(nt):
                    r0 = tok0 + g * 512 + t * 128
                    a = hf[t % 2]; b = hb[t % 2]
                    self.dma(a[:], self.hfb[0, r0:r0 + 128, :], w=a); self.dma(b[:], self.hfb[1, r0:r0 + 128, :], w=b)
                    self.tt(a[:], a[:], b[:], ALU.add, [a, b], a, eng="pool")
                    for hh in range(4):
                        self.act(junk[:], a[:, hh * 256:(hh + 1) * 256], AF.Square, [a], [junk, ss], accum=ss[:, hh:hh + 1])
                    self.act(ss[:], ss[:], AF.Sqrt, [ss, self.eps], ss, scale=1.0 / 256, bias=self.eps[:, 0:1])
                    self.recip(ss[:], ss[:], [ss], ss)
                    for hh in range(4):
                        self.ts(hn[:, t, hh * 256:(hh + 1) * 256], a[:, hh * 256:(hh + 1) * 256], ss[:, hh:hh + 1], ALU.mult, [a, ss], hn)
                for j in range(8):
                    pb = self.bank(); pv = pb[:].bitcast(BF16)
                    for t in range(nt):
                        self.tr(pv[:, t * 128:(t + 1) * 128], hn[:, t, j * 128:(j + 1) * 128], self.identb[:], [hn, self.identb], pb, t == 0)
                    o = ot[j % 2]; y = yo[j % 2]
                    c0 = tok0 + g * 512
                    self.dma(o[:, 0:n], self.o_mlT[j * 128:(j + 1) * 128, c0:c0 + n], w=o)
                    self.stt(y[:, 0:n], pv[:, 0:n], mn[:, j:j + 1], o[:, 0:n], ALU.mult, ALU.mult, [pb, mn, o], y)
                    self.dma(self.yT[0, j * 128:(j + 1) * 128, c0:c0 + n], y[:, 0:n], [y])

    def phase_s5(self, l, tok0, T, isS, seq):
        TB = min(512, T)
        NBK = T // TB
        for dr in range(2):
            rev = (dr == 1)
            with self.pool() as O:
                bbr = O.t([32, 4096], BF16); bbi = O.t([32, 4096], BF16)
                cre = O.t([128, 32, 32], BF16); cim = O.t([128, 32, 32], BF16)
                mag = O.t([128, 32]); E1c = O.t([128, 32]); E1s = O.t([128, 32])
                ETc = O.t([128, 32]); ETs = O.t([128, 32])
                self.dma(cre[:], self.s5_creT[l, dr], w=cre, eng="pool"); self.dma(cim[:], self.s5_cimT[l, dr], w=cim, eng="pool")

                def disc(A, npart, nfree, src_re, src_im, src_dt, bc, c0=0):
                    t = {k: A.t([npart, nfree]) for k in ("lr", "li", "dt", "mag", "cs", "sn", "a", "b", "c", "qr", "qi")}
                    for k, s in (("lr", src_re), ("li", src_im), ("dt", src_dt)):
                        self.dma(t[k][:], s[:, c0:c0 + nfree].partition_broadcast(npart) if bc else s, w=t[k])
                    T_ = lambda k: t[k][:]
                    self.ts(T_("lr"), T_("lr"), -1e-4, ALU.min, [t["lr"]], t["lr"])
                    self.act(T_("dt"), T_("dt"), AF.Exp, [t["dt"]], t["dt"])
                    self.tt(T_("a"), T_("dt"), T_("lr"), ALU.mult, [t["dt"], t["lr"]], t["a"])
                    self.act(T_("mag"), T_("a"), AF.Exp, [t["a"]], t["mag"])
                    self.tt(T_("a"), T_("dt"), T_("li"), ALU.mult, [t["dt"], t["li"]], t["a"])
                    self.ts(T_("a"), T_("a"), 1.0 / 64, ALU.mult, [t["a"]], t["a"])
                    self.tt(T_("b"), T_("a"), T_("a"), ALU.mult, [t["a"]], t["b"])
                    self.ts(T_("c"), T_("b"), -1.0 / 42, ALU.mult, [t["b"]], t["c"], s2=1.0, op1=ALU.add)
                    self.tt(T_("c"), T_("c"), T_("b"), ALU.mult, [t["c"], t["b"]], t["c"])
                    self.ts(T_("c"), T_("c"), -1.0 / 20, ALU.mult, [t["c"]], t["c"], s2=1.0, op1=ALU.add)
                    self.tt(T_("c"), T_("c"), T_("b"), ALU.mult, [t["c"], t["b"]], t["c"])
                    self.ts(T_("c"), T_("c"), -1.0 / 6, ALU.mult, [t["c"]], t["c"], s2=1.0, op1=ALU.add)
                    self.tt(T_("sn"), T_("c"), T_("a"), ALU.mult, [t["c"], t["a"]], t["sn"])
                    self.ts(T_("c"), T_("b"), -1.0 / 56, ALU.mult, [t["b"]], t["c"], s2=1.0, op1=ALU.add)
                    self.tt(T_("c"), T_("c"), T_("b"), ALU.mult, [t["c"], t["b"]], t["c"])
                    self.ts(T_("c"), T_("c"), -1.0 / 30, ALU.mult, [t["c"]], t["c"], s2=1.0, op1=ALU.add)
                    self.tt(T_("c"), T_("c"), T_("b"), ALU.mult, [t["c"], t["b"]], t["c"])
                    self.ts(T_("c"), T_("c"), -1.0 / 12, ALU.mult, [t["c"]], t["c"], s2=1.0, op1=ALU.add)
                    self.tt(T_("c"), T_("c"), T_("b"), ALU.mult, [t["c"], t["b"]], t["c"])
                    self.ts(T_("cs"), T_("c"), -0.5, ALU.mult, [t["c"]], t["cs"], s2=1.0, op1=ALU.add)
                    for _ in range(6):
                        self.tt(T_("a"), T_("cs"), T_("cs"), ALU.mult, [t["cs"]], t["a"])
                        self.tt(T_("b"), T_("sn"), T_("sn"), ALU.mult, [t["sn"]], t["b"])
                        self.tt(T_("c"), T_("sn"), T_("cs"), ALU.mult, [t["sn"], t["cs"]], t["c"])
                        self.tt(T_("cs"), T_("a"), T_("b"), ALU.subtract, [t["a"], t["b"]], t["cs"])
                        self.ts(T_("sn"), T_("c"), 2.0, ALU.mult, [t["c"]], t["sn"])
                    return t

                for hc in range(2):
                  c0 = hc * 2048
                  with self.pool() as A:
                    t = disc(A, 32, 2048, self.s5_are[l, dr], self.s5_aim[l, dr], self.s5_ldt[l, dr], True, c0)
                    T_ = lambda k: t[k][:]
                    br = A.t([32, 2048]); bi = A.t([32, 2048])
                    self.dma(br[:], self.s5_breT[l, dr][:, c0:c0 + 2048], w=br); self.dma(bi[:], self.s5_bimT[l, dr][:, c0:c0 + 2048], w=bi)
                    self.tt(T_("cs"), T_("cs"), T_("mag"), ALU.mult, [t["cs"], t["mag"]], t["cs"])
                    self.tt(T_("sn"), T_("sn"), T_("mag"), ALU.mult, [t["sn"], t["mag"]], t["sn"])
                    self.ts(T_("cs"), T_("cs"), -1.0, ALU.add, [t["cs"]], t["cs"])
                    self.tt(T_("a"), T_("lr"), T_("lr"), ALU.mult, [t["lr"]], t["a"])
                    self.tt(T_("b"), T_("li"), T_("li"), ALU.mult, [t["li"]], t["b"])
                    self.tt(T_("a"), T_("a"), T_("b"), ALU.add, [t["a"], t["b"]], t["a"])
                    self.recip(T_("a"), T_("a"), [t["a"]], t["a"])
                    self.tt(T_("b"), T_("cs"), T_("lr"), ALU.mult, [t["cs"], t["lr"]], t["b"])
                    self.tt(T_("c"), T_("sn"), T_("li"), ALU.mult, [t["sn"], t["li"]], t["c"])
                    self.tt(T_("b"), T_("b"), T_("c"), ALU.add, [t["b"], t["c"]], t["b"])
                    self.tt(T_("qr"), T_("b"), T_("a"), ALU.mult, [t["b"], t["a"]], t["qr"])
                    self.tt(T_("b"), T_("sn"), T_("lr"), ALU.mult, [t["sn"], t["lr"]], t["b"])
                    self.tt(T_("c"), T_("cs"), T_("li"), ALU.mult, [t["cs"], t["li"]], t["c"])
                    self.tt(T_("b"), T_("b"), T_("c"), ALU.subtract, [t["b"], t["c"]], t["b"])
                    self.tt(T_("qi"), T_("b"), T_("a"), ALU.mult, [t["b"], t["a"]], t["qi"])
                    self.tt(T_("a"), T_("qr"), br[:], ALU.mult, [t["qr"], br], t["a"])
                    self.tt(T_("b"), T_("qi"), bi[:], ALU.mult, [t["qi"], bi], t["b"])
                    self.tt(bbr[:, c0:c0 + 2048], T_("a"), T_("b"), ALU.subtract, [t["a"], t["b"]], bbr)
                    self.tt(T_("a"), T_("qr"), bi[:], ALU.mult, [t["qr"], bi], t["a"])
                    self.tt(T_("b"), T_("qi"), br[:], ALU.mult, [t["qi"], br], t["b"])
                    self.tt(bbi[:, c0:c0 + 2048], T_("a"), T_("b"), ALU.add, [t["a"], t["b"]], bbi)
                O2 = self.pool(); O2.__enter__()
                tc_ = O2.t([128, 32, TB]); ts_ = O2.t([128, 32, TB])
                with self.pool() as A:
                    t = disc(A, 128, 32, self.s5_areS[l, dr], self.s5_aimS[l, dr], self.s5_ldtS[l, dr], False)
                    self.cp(mag[:], t["mag"][:], [t["mag"]], mag, eng="dve")
                    self.cp(E1c[:], t["cs"][:], [t["cs"]], E1c, eng="dve"); self.cp(E1s[:], t["sn"][:], [t["sn"]], E1s, eng="dve")
                    self.memset(tc_[:, :, 0:1], 1.0, tc_); self.memset(ts_[:, :, 0:1], 0.0, ts_)
                    pc = A.t([128, 32]); psn = A.t([128, 32]); a = A.t([128, 32]); b = A.t([128, 32])
                    w1 = A.t([128, 8, TB // 2]); w2 = A.t([128, 8, TB // 2])
                    self.cp(pc[:], t["cs"][:], [t["cs"]], pc, eng="dve"); self.cp(psn[:], t["sn"][:], [t["sn"]], psn, eng="dve")
                    n = 1
                    while n < TB:
                        for jh in range(4):
                            js = slice(jh * 8, (jh + 1) * 8)
                            pcb = pc[:, js].unsqueeze(2).to_broadcast([128, 8, n]); psb = psn[:, js].unsqueeze(2).to_broadcast([128, 8, n])
                            self.tt(w1[:, :, 0:n], tc_[:, js, 0:n], pcb, ALU.mult, [tc_, pc], w1)
                            self.tt(w2[:, :, 0:n], ts_[:, js, 0:n], psb, ALU.mult, [ts_, psn], w2)
                            self.tt(tc_[:, js, n:2 * n], w1[:, :, 0:n], w2[:, :, 0:n], ALU.subtract, [w1, w2], tc_)
                            self.tt(w1[:, :, 0:n], tc_[:, js, 0:n], psb, ALU.mult, [tc_, psn], w1)
                            self.tt(w2[:, :, 0:n], ts_[:, js, 0:n], pcb, ALU.mult, [ts_, pc], w2)
                            self.tt(ts_[:, js, n:2 * n], w1[:, :, 0:n], w2[:, :, 0:n], ALU.add, [w1, w2], ts_)
                        self.tt(a[:], pc[:], pc[:], ALU.mult, [pc], a); self.tt(b[:], psn[:], psn[:], ALU.mult, [psn], b)
                        self.tt(b[:], a[:], b[:], ALU.subtract, [a, b], b)
                        self.tt(a[:], pc[:], psn[:], ALU.mult, [pc, psn], a)
                        self.ts(psn[:], a[:], 2.0, ALU.mult, [a], psn)
                        self.cp(pc[:], b[:], [b], pc, eng="dve")
                        n *= 2
                    self.cp(ETc[:], pc[:], [pc], ETc, eng="dve"); self.cp(ETs[:], psn[:], [psn], ETs, eng="dve")
                with self.pool() as A:
                    inr = A.t([128, 32]); ini = A.t([128, 32]); h0r = A.t([128, 32]); h0i = A.t([128, 32]); a = A.t([128, 32]); b = A.t([128, 32])
                    if isS:
                        self.dma(h0r[:], self.s5h0r[l, dr], w=h0r); self.dma(h0i[:], self.s5h0i[l, dr], w=h0i)
                        self.tt(a[:], h0r[:], E1c[:], ALU.mult, [h0r, E1c], a); self.tt(b[:], h0i[:], E1s[:], ALU.mult, [h0i, E1s], b)
                        self.tt(inr[:], a[:], b[:], ALU.subtract, [a, b], inr)
                        self.tt(a[:], h0r[:], E1s[:], ALU.mult, [h0r, E1s], a); self.tt(b[:], h0i[:], E1c[:], ALU.mult, [h0i, E1c], b)
                        self.tt(ini[:], a[:], b[:], ALU.add, [a, b], ini)
                    else:
                        self.memset(inr[:], 0.0, inr); self.memset(ini[:], 0.0, ini)
                    dcar = [Dep() for _ in range(32)]
                    self.P.barrier()
                    uT = [A.t([32, TB], BF16) for _ in range(4)]
                    zr = [A.t([128, TB]) for _ in range(2)]; zi = [A.t([128, TB]) for _ in range(2)]
                    w = [A.t([128, TB]) for _ in range(4)]
                    gr = [A.t([128, TB]) for _ in range(2)]; gi = [A.t([128, TB]) for _ in range(2)]
                    hr = [A.t([128, TB], BF16) for _ in range(2)]; hi = [A.t([128, TB], BF16) for _ in range(2)]
                    ys = [A.t([32, TB]) for _ in range(2)]
                    sm = [A.t([128, 4]) for _ in range(2)]
                    fr = A.t([128, 32]); fi = A.t([128, 32])
                    R = (lambda ap: ap[:, ::-1]) if rev else (lambda ap: ap)
                    ui = 0
                    for bk in (range(NBK - 1, -1, -1) if rev else range(NBK)):
                        for j in range(32):
                            c0 = bk * TB
                            U = uT[ui % 4]
                            self.dma(U[:], self.s5uT[32 * j:32 * j + 32, tok0 + c0:tok0 + c0 + TB], w=U)
                            Cj = tc_[:, j, :]; Sj = ts_[:, j, :]
                            k = ui % 2; ui += 1
                            pr = self.bank(0, 3); pi_ = self.bank(3, 6)
                            self.mm(pr[:, 0:TB], bbr[:, j * 128:(j + 1) * 128], U[:, 0:TB], True, True, [bbr, U], pr)
                            self.mm(pi_[:, 0:TB], bbi[:, j * 128:(j + 1) * 128], U[:, 0:TB], True, True, [bbi, U], pi_)
                            self.tt(w[0][:], pr[:, 0:TB], R(Cj), ALU.mult, [pr, tc_], w[0])
                            self.tt(w[1][:], pi_[:, 0:TB], R(Sj), ALU.mult, [pi_, ts_], w[1])
                            self.tt(w[2][:], pi_[:, 0:TB], R(Cj), ALU.mult, [pi_, tc_], w[2])
                            self.tt(w[3][:], pr[:, 0:TB], R(Sj), ALU.mult, [pr, ts_], w[3])
                            self.tt(zr[k][:], w[0][:], w[1][:], ALU.add, [w[0], w[1]], zr[k], eng="pool")
                            self.tt(zi[k][:], w[2][:], w[3][:], ALU.subtract, [w[2], w[3]], zi[k], eng="pool")
                            mb = mag[:, j:j + 1].to_broadcast([128, TB])
                            self.scan(R(gr[k][:]), mb, R(zr[k][:]), inr[:, j:j + 1], ALU.mult, ALU.add, [mag, zr[k], dcar[j]], gr[k])
                            self.scan(R(gi[k][:]), mb, R(zi[k][:]), ini[:, j:j + 1], ALU.mult, ALU.add, [mag, zi[k], dcar[j]], gi[k])
                            le = 0 if rev else TB - 1
                            glr = gr[k][:, le:le + 1]; gli = gi[k][:, le:le + 1]
                            s = sm[k]
                            self.tt(s[:, 0:1], glr, ETc[:, j:j + 1], ALU.mult, [gr[k], ETc], s, eng="pool")
                            self.tt(s[:, 1:2], gli, ETs[:, j:j + 1], ALU.mult, [gi[k], ETs], s, eng="pool")
                            self.tt(s[:, 2:3], glr, ETs[:, j:j + 1], ALU.mult, [gr[k], ETs], s, eng="pool")
                            self.tt(s[:, 3:4], gli, ETc[:, j:j + 1], ALU.mult, [gi[k], ETc], s, eng="pool")
                            self.tt(inr[:, j:j + 1], s[:, 0:1], s[:, 1:2], ALU.subtract, [s], dcar[j], eng="pool")
                            self.tt(ini[:, j:j + 1], s[:, 2:3], s[:, 3:4], ALU.add, [s], dcar[j], eng="pool")
                            self.tt(w[0][:], gr[k][:], R(Cj), ALU.mult, [gr[k], tc_], w[0])
                            self.tt(w[1][:], gi[k][:], R(Sj), ALU.mult, [gi[k], ts_], w[1], eng="pool")
                            self.tt(w[2][:], gi[k][:], R(Cj), ALU.mult, [gi[k], tc_], w[2])
                            self.tt(w[3][:], gr[k][:], R(Sj), ALU.mult, [gr[k], ts_], w[3], eng="pool")
                            self.tt(hr[k][:], w[0][:], w[1][:], ALU.subtract, [w[0], w[1]], hr[k], eng="pool")
                            self.stt(hi[k][:], w[2][:], -1.0, w[3][:], ALU.mult, ALU.subtract, [w[2], w[3]], hi[k])
                            py = self.bank(6, 8)
                            self.mm(py[0:32, 0:TB], cre[:, j, :], hr[k][:], True, False, [cre, hr[k]], py)
                            self.mm(py[0:32, 0:TB], cim[:, j, :], hi[k][:], False, True, [cim, hi[k]], py)
                            y = ys[k]
                            self.cp(y[:], py[0:32, 0:TB], [py], y, eng="act")
                            self.dma(self.ys5[dr, 32 * j:32 * j + 32, tok0 + c0:tok0 + c0 + TB], y[:], [y])
                            if (not isS) and ((bk == 0) if rev else (bk == NBK - 1)):
                                cl = tc_[:, j, TB - 1:TB]; sl = ts_[:, j, TB - 1:TB]
                                self.tt(s[:, 0:1], glr, cl, ALU.mult, [gr[k], tc_], s, eng="pool")
                                self.tt(s[:, 1:2], gli, sl, ALU.mult, [gi[k], ts_], s, eng="pool")
                                self.tt(s[:, 2:3], glr, sl, ALU.mult, [gr[k], ts_], s, eng="pool")
                                self.tt(s[:, 3:4], gli, cl, ALU.mult, [gi[k], tc_], s, eng="pool")
                                self.tt(fr[:, j:j + 1], s[:, 0:1], s[:, 1:2], ALU.subtract, [s], fr, eng="pool")
                                self.tt(fi[:, j:j + 1], s[:, 2:3], s[:, 3:4], ALU.add, [s], fi, eng="pool")
                    if not isS:
                        self.dma(self.o_s5r[seq, l, dr], fr[:], [fr]); self.dma(self.o_s5i[seq, l, dr], fi[:], [fi])
                O2.__exit__(None, None, None)

    def phase_s5_post(self, l, tok0, ntok):
        with self.pool() as A:
            sd = A.t([128, 8]); self.dma(sd[:], self.s5_dT[l], w=sd)
            wg = A.t([128, 8, 1024], BF16)
            self.load_w(wg, self.w_glu[l], [(0, 1024, 0)], 8)
            gss = A.t([128, 8, 512], BF16)
            ya = [A.t([128, 512]) for _ in range(2)]; yb = [A.t([128, 512]) for _ in range(2)]; ub = [A.t([128, 512], BF16) for _ in range(2)]
            w = [A.t([128, 512]) for _ in range(3)]; sg = A.t([128, 512]); yo = [A.t([128, 512], BF16) for _ in range(2)]
            for g in range((ntok + 511) // 512):
                n = min(512, ntok - g * 512); c0 = tok0 + g * 512
                for j in range(8):
                    a = ya[j % 2]; b = yb[j % 2]; u = ub[j % 2]
                    self.dma(a[:, 0:n], self.ys5[0, j * 128:(j + 1) * 128, c0:c0 + n], w=a)
                    self.dma(b[:, 0:n], self.ys5[1, j * 128:(j + 1) * 128, c0:c0 + n], w=b)
                    self.dma(u[:, 0:n], self.s5uT[j * 128:(j + 1) * 128, c0:c0 + n], w=u)
                    self.tt(a[:, 0:n], a[:, 0:n], b[:, 0:n], ALU.add, [a, b], a, eng="pool")
                    self.stt(a[:, 0:n], u[:, 0:n], sd[:, j:j + 1], a[:, 0:n], ALU.mult, ALU.add, [u, sd, a], a)
                    self.act(w[0][:, 0:n], a[:, 0:n], AF.Square, [a], w[0])
                    self.ts(w[0][:, 0:n], w[0][:, 0:n], 0.044715, ALU.mult, [w[0]], w[0], s2=1.0, op1=ALU.add)
                    self.tt(w[1][:, 0:n], w[0][:, 0:n], a[:, 0:n], ALU.mult, [w[0], a], w[1])
                    self.act(w[2][:, 0:n], w[1][:, 0:n], AF.Sigmoid, [w[1]], w[2], scale=1.5957691216057308)
                    self.tt(gss[:, j, 0:n], a[:, 0:n], w[2][:, 0:n], ALU.mult, [a, w[2]], gss)
                for j in range(8):
                    pb = self.bank()
                    for k in range(8):
                        self.mm(pb[:, 0:n], wg[:, k, j * 128:(j + 1) * 128], gss[:, k, 0:n], k == 0, k == 7, [wg, gss], pb)
                    self.act(sg[:, 0:n], pb[:, 0:n], AF.Sigmoid, [pb], sg)
                    y = yo[j % 2]
                    self.tt(y[:, 0:n], gss[:, j, 0:n], sg[:, 0:n], ALU.mult, [gss, sg], y)
                    self.dma(self.yT[2, j * 128:(j + 1) * 128, c0:c0 + n], y[:, 0:n], [y])

    def phase_merge(self, l, xsrc, xdst, tok0, ntok, cnd):
        ntile = ntok // 128
        nch = (ntok + 511) // 512
        with self.pool() as O:
            mT = O.t([128, 16, ntok], BF16)
            with self.pool() as A:
                yb = A.t([128, 4, 8, ntok], BF16)
                for b in range(4):
                    self.dma(yb[:, b], self.yT[b].rearrange("(k p) t -> p k t", p=128)[:, :, tok0:tok0 + ntok], w=yb)
                wts = [A.t([128, 8, 128], BF16) for _ in range(4)]
                gt = [A.t([128, 512], BF16) for _ in range(4)]
                acc = [A.t([128, 512]) for _ in range(2)]; tmp = [A.t([128, 512]) for _ in range(2)]
                i = 0
                for dc in range(16):
                    wl = []
                    for b in range(4):
                        w = wts[b]
                        self.load_w(w, self.w_br[l, b], [(dc * 128, 128, 0)], 8)
                        wl.append(w)
                    for j in range(nch):
                        n = min(512, ntok - j * 512)
                        a = acc[i % 2]; i += 1
                        for b in range(4):
                            pb = self.bank()
                            for k in range(8):
                                self.mm(pb[:, 0:n], wl[b][:, k, :], yb[:, b, k, j * 512:j * 512 + n], k == 0, k == 7, [wl[b], yb], pb)
                            g = gt[b]
                            self.dma(g[:, 0:n], self.gateT[b * 2048 + dc * 128: b * 2048 + (dc + 1) * 128, tok0 + j * 512: tok0 + j * 512 + n], w=g)
                            if b == 0:
                                self.tt(a[:, 0:n], pb[:, 0:n], g[:, 0:n], ALU.mult, [pb, g], a)
                            else:
                                t = tmp[b % 2]
                                self.tt(t[:, 0:n], pb[:, 0:n], g[:, 0:n], ALU.mult, [pb, g], t)
                                if b < 3:
                                    self.tt(a[:, 0:n], a[:, 0:n], t[:, 0:n], ALU.add, [a, t], a, eng="pool")
                                else:
                                    self.tt(mT[:, dc, j * 512:j * 512 + n], a[:, 0:n], t[:, 0:n], ALU.add, [a, t], mT, eng="pool")
            with self.pool() as A:
                gb = A.t([128, D])
                self.dma(gb[:], self.gsc[0, cnd:cnd + 1, :].partition_broadcast(128), w=gb)
                self.resid_out(A, mT, 16, self.w_o[l], xsrc, xdst, tok0, ntile, gb)

    def resid_out(self, A, aT, kch, wsrc, xsrc, xdst, tok0, ntile, gb):
        wts = [A.t([128, kch, 512], BF16) for _ in range(2)]
        xt = [A.t([128, 512]) for _ in range(3)]; tm = [A.t([128, 512]) for _ in range(2)]
        i = 0
        for nb in range(4):
            w = wts[nb % 2]
            self.load_w(w, wsrc, [(nb * 512, 512, 0)], kch)
            for t in range(ntile):
                r0 = tok0 + t * 128
                pb = self.bank()
                for k in range(kch):
                    self.mm(pb[:], aT[:, k, t * 128:(t + 1) * 128], w[:, k, :], k == 0, k == kch - 1, [aT, w], pb)
                x = xt[i % 3]; tt_ = tm[i % 2]; i += 1
                self.dma(x[:], xsrc[r0:r0 + 128, nb * 512:(nb + 1) * 512], w=x)
                self.tt(tt_[:], pb[:], gb[:, nb * 512:(nb + 1) * 512], ALU.mult, [pb, gb], tt_)
                self.tt(x[:], x[:], tt_[:], ALU.add, [x, tt_], x, eng="pool")
                self.dma(xdst[r0:r0 + 128, nb * 512:(nb + 1) * 512], x[:], [x])

    def phase_ffn(self, l, xbuf, tok0, ntok, cnd):
        ntile = ntok // 128
        nch = (ntok + 511) // 512
        HC = FFN_H // 128
        HH = HC // 2
        with self.pool() as O:
            hT = O.t([128, 16, ntok], BF16)
            with self.pool() as A0:
                self.norm_T(A0, xbuf, tok0, ntile, hT,
                            lambda kc: self.A2[:, cnd, kc:kc + 1], lambda kc: self.modT[:, 48 + kc, cnd:cnd + 1])
            gb = O.t([128, D])
            self.dma(gb[:], self.gsc[1, cnd:cnd + 1, :].partition_broadcast(128), w=gb)
            for half in range(2):
                with self.pool() as A:
                    aT = A.t([128, HH, ntok], BF16)
                    sa = [A.t([128, 512]) for _ in range(2)]
                    was = [A.t([128, 16, 128], BF16) for _ in range(3)]; wbs = [A.t([128, 16, 128], BF16) for _ in range(3)]
                    for hc in range(HH):
                        col = (half * HH + hc) * 128
                        wa = was[hc % 3]; wb = wbs[hc % 3]
                        self.load_w(wa, self.w_f1[l], [(col, 128, 0)]); self.load_w(wb, self.w_f1[l], [(FFN_H + col, 128, 0)])
                        for j in range(nch):
                            n = min(512, ntok - j * 512)
                            pa = self.bank(); pb = self.bank()
                            for k in range(16):
                                self.mm(pa[:, 0:n], wa[:, k, :], hT[:, k, j * 512:j * 512 + n], k == 0, k == 15, [wa, hT], pa)
                            for k in range(16):
                                self.mm(pb[:, 0:n], wb[:, k, :], hT[:, k, j * 512:j * 512 + n], k == 0, k == 15, [wb, hT], pb)
                            s = sa[j % 2]
                            self.act(s[:, 0:n], pa[:, 0:n], AF.Silu, [pa], s)
                            self.tt(aT[:, hc, j * 512:j * 512 + n], s[:, 0:n], pb[:, 0:n], ALU.mult, [s, pb], aT)
                    self.resid_out(A, aT, HH, self.w_f2[l][half * HH * 128:(half + 1) * HH * 128, :], xbuf, xbuf, tok0, ntile, gb)
                    self.P.barrier()

    def phase_final(self, xsrc, out, ntok):
        with self.pool() as A:
            fb = A.t([128, D]); self.dma(fb[:], self.fnorm.partition_broadcast(128), w=fb)
            xt = [A.t([128, D]) for _ in range(2)]; junk = A.t([128, D], BF16); ss = A.t([128, ntok // 128])
            for t in range(ntok // 128):
                x = xt[t % 2]
                self.dma(x[:], xsrc[t * 128:(t + 1) * 128, :], w=x)
                self.act(junk[:], x[:], AF.Square, [x], [junk, ss], accum=ss[:, t:t + 1])
                self.act(ss[:, t:t + 1], ss[:, t:t + 1], AF.Sqrt, [ss, self.eps], ss, scale=1.0 / D, bias=self.eps[:, 0:1])
                self.recip(ss[:, t:t + 1], ss[:, t:t + 1], [ss], ss)
                self.stt(x[:], x[:], ss[:, t:t + 1], fb[:], ALU.mult, ALU.mult, [x, ss, fb], x)
                self.dma(out[t * 128:(t + 1) * 128, :], x[:], [x])

    def build(self):
        import os
        c = self.cfg
        self.setup()
        lim = int(os.environ.get("MK_STOP", "100000"))
        self._pc = 0
        for nm in ("phase_ada", "phase_win", "phase_ctx", "phase_mlstm", "phase_s5", "phase_mla", "phase_diff",
                   "phase_mlstm_post", "phase_s5_post", "phase_merge", "phase_ffn", "phase_final"):
            def mk(fn, nm=nm):
                def w(*a, **k):
                    self._pc += 1
                    if self._pc > lim:
                        return
                    if os.environ.get("MK_VERBOSE"):
                        print("PHASE", self._pc, nm, flush=True)
                    return fn(*a, **k)
                return w
            setattr(self, nm, mk(getattr(self, nm)))
        for l in range(DEPTH):
            self.phase_ada(l)
            xin = self.xp if l == 0 else self.xcp
            self.phase_win(l, xin, 0, self.TPT, 0, False)
            for s in range(c.NPR):
                t0 = s * c.TP
                self.phase_mlstm(l, t0, c.TP, False, s)
                self.phase_s5(l, t0, c.TP, False, s)
                self.phase_mla(l, t0, c.TP, False)
                self.phase_diff(l, t0, c.TP, False)
            self.phase_mlstm_post(l, 0, self.TPT)
            self.phase_s5_post(l, 0, self.TPT)
            self.phase_merge(l, xin, self.xcp, 0, self.TPT, 0)
            self.phase_ffn(l, self.xcp, 0, self.TPT, 0)
            xin = self.xs if l == 0 else self.xcs
            for g0 in range(0, c.TS, c.GW):
                self.phase_win(l, xin, g0, min(c.GW, c.TS - g0), 1, True)
            self.phase_ctx(l)
            self.phase_mlstm(l, 0, c.TS, True, 0)
            self.phase_s5(l, 0, c.TS, True, 0)
            self.phase_mla(l, 0, c.TS, True)
            self.phase_diff(l, 0, c.TS, True)
            self.phase_mlstm_post(l, 0, c.TS)
            self.phase_s5_post(l, 0, c.TS)
            for g0 in range(0, c.TS, c.G):
                n = min(c.G, c.TS - g0)
                self.phase_merge(l, xin, self.xcs, g0, n, 1)
                self.phase_ffn(l, self.xcs, g0, n, 1)
        self.phase_final(self.xcp, self.o_yp, self.TPT)
        self.phase_final(self.xcs, self.o_ys, c.TS)
        self.P.emit()
        return self.nc


def _pT(a, n):
    sh = a.shape[:-1]
    return np.ascontiguousarray(np.swapaxes(a.reshape(sh + (n, 128)), -1, -2))


def _SL(a):
    sh = a.shape[:-2]
    b = a.reshape(sh + (32, 2, 64))
    nd = len(sh)
    b = np.transpose(b, tuple(range(nd)) + (nd + 1, nd + 2, nd))
    return np.ascontiguousarray(b.reshape(sh + (128, 32)))


def _unSL(a):
    sh = a.shape[:-2]
    b = a.reshape(sh + (2, 64, 32))
    nd = len(sh)
    b = np.transpose(b, tuple(range(nd)) + (nd + 2, nd, nd + 1))
    return np.ascontiguousarray(b.reshape(sh + (64, 64)))


def _consts(TS):
    f = np.float32
    c = {}
    c["c_ident"] = np.eye(128, dtype=f)
    c["c_ones"] = np.ones((128, 128), f)
    s = np.arange(64)[:, None]; t = np.arange(64)[None, :]
    c["c_maskL"] = np.where(s <= t, 0.0, NEG).astype(f)
    c["c_maskU"] = np.where(s >= t, 0.0, NEG).astype(f)
    sel = np.zeros((4, 4, 128), f)
    for h in range(4):
        sel[h, h, :] = 1.0
    c["c_sel"] = sel
    R = np.zeros((64, 64), f)
    for base in (0, 32):
        for j in range(16):
            R[base + j, base + j + 16] = -1.0
            R[base + j + 16, base + j] = 1.0
    R128 = np.zeros((128, 128), f)
    R128[:64, :64] = R; R128[64:, 64:] = R
    c["c_ropeR"] = np.ascontiguousarray(R128.T)
    inv = (np.float32(10000.0) ** (-np.arange(16, dtype=f) / np.float32(16))).astype(f)
    tt = np.arange(TS)
    rows = (tt // GRID_W).astype(f); cols = (tt % GRID_W).astype(f)
    C = np.zeros((128, TS), f); S = np.zeros((128, TS), f)
    for p in range(128):
        q = p % 64
        pos = rows if q < 32 else cols
        ang = (pos * inv[q % 16]).astype(f)
        C[p] = np.cos(ang); S[p] = np.sin(ang)
    c["c_ropeC"] = C; c["c_ropeS"] = S
    return c


def _weights(I):
    f = np.float32
    L = DEPTH
    w = {}
    w["w_ada"] = I["w_ada"]; w["b_adaT"] = _pT(I["b_ada"], 96)
    w["nmixT"] = _pT(I["norm_mix"], 16); w["nffnT"] = _pT(I["norm_ffn"], 16)
    w["w_in"] = I["w_in"]; w["mlifb"] = np.ascontiguousarray(np.swapaxes(I["ml_if_bias"].reshape(L, 4, 4), 1, 2))
    w["mlnormT"] = _pT(I["ml_norm"], 8); w["kvnorm"] = np.ascontiguousarray(I["mla_kv_norm"].reshape(L, 1, 512))
    w["w_kvb"] = I["mla_w_kvb"]
    w["s5_are"] = np.ascontiguousarray(I["s5_a_re"].reshape(L, 2, 1, 4096)); w["s5_aim"] = np.ascontiguousarray(I["s5_a_im"].reshape(L, 2, 1, 4096))
    ldt = np.repeat(I["s5_log_dt"][..., None], 64, axis=-1)
    w["s5_ldt"] = np.ascontiguousarray(ldt.reshape(L, 2, 1, 4096))
    w["s5_areS"] = _SL(I["s5_a_re"]); w["s5_aimS"] = _SL(I["s5_a_im"]); w["s5_ldtS"] = _SL(ldt)
    for nm, src in (("s5_breT", "s5_b_re"), ("s5_bimT", "s5_b_im")):
        b = I[src].reshape(L, 2, 32, 2, 64, 16)
        o = np.zeros((L, 2, 2, 16, 32, 2, 64), f)
        for g2 in range(2):
            o[:, :, g2, :, :, g2, :] = np.transpose(b[:, :, :, g2, :, :], (0, 1, 4, 2, 3))
        w[nm] = o.reshape(L, 2, 32, 4096)
    for nm, src in (("s5_creT", "s5_c_re"), ("s5_cimT", "s5_c_im")):
        cc = I[src].reshape(L, 2, 32, 2, 16, 64)
        o = np.zeros((L, 2, 2, 64, 32, 2, 16), f)
        for g2 in range(2):
            o[:, :, g2, :, :, g2, :] = np.transpose(cc[:, :, :, g2, :, :], (0, 1, 4, 2, 3))
        w[nm] = o.reshape(L, 2, 128, 32, 32)
    w["s5_dT"] = _pT(I["s5_d"], 8); w["w_glu"] = I["s5_w_glu"]
    w["df_lam"] = np.ascontiguousarray(I["df_lambda"].reshape(L, 1, 256)); w["dfnormT"] = _pT(I["df_norm"], 1)
    w["w_br"] = I["w_branch"]; w["w_o"] = I["w_o"]; w["w_f1"] = I["w_ffn_in"]; w["w_f2"] = I["w_ffn_out"]
    w["fnorm"] = np.ascontiguousarray(I["final_norm"].reshape(1, D))
    return w


def run(cfg, inputs, trace=False):
    I = {k: np.asarray(v) for k, v in inputs.items()}
    f = np.float32
    L = DEPTH
    NB = I["x_prompt"].shape[0]
    ncore = NB // cfg.NPR
    nsamp = I["x_sample"].shape[0]
    assert ncore == 2 * nsamp
    b = Builder(cfg)
    nc = b.build()
    shared = _weights(I)
    shared.update(_consts(cfg.TS))
    maps = []
    for c in range(ncore):
        s = c // 2
        m = dict(shared)
        m["xs"] = np.ascontiguousarray(I["x_sample"][s])
        m["xp"] = np.ascontiguousarray(I["x_prompt"][c * cfg.NPR:(c + 1) * cfg.NPR].reshape(cfg.NPR * cfg.TP, D))
        cond = np.stack([I["c_ctx"], I["c"][s]], 0)
        m["condT"] = np.ascontiguousarray(np.transpose(cond.T.reshape(16, 128, 2), (1, 0, 2)))
        m["ckv_c"] = np.ascontiguousarray(I["cache_mla_ckv"][s]); m["kr_c"] = np.ascontiguousarray(I["cache_mla_krope"][s])
        m["dk_c"] = np.ascontiguousarray(I["cache_diff_k"][s].reshape(L, cfg.PAST, 1024))
        m["dv_c"] = np.ascontiguousarray(I["cache_diff_v"][s].reshape(L, cfg.PAST, 1024))
        m["mlc0T"] = np.ascontiguousarray(np.swapaxes(I["state_mlstm_c"][s], -1, -2))
        m["mln0"] = np.ascontiguousarray(np.swapaxes(I["state_mlstm_n"][s].reshape(L, 2, 4, 2, 128), -1, -2)); m["mlm0"] = np.ascontiguousarray(I["state_mlstm_m"][s][..., None])
        m["s5h0r"] = _SL(I["state_s5_re"][s]); m["s5h0i"] = _SL(I["state_s5_im"][s])
        for k in list(m.keys()):
            if k not in b.din:
                raise KeyError(k)
            m[k] = np.ascontiguousarray(m[k], dtype=f)
            assert list(m[k].shape) == b.din[k][0], (k, m[k].shape, b.din[k][0])
        maps.append(m)
    res = run_bass_kernel_spmd(nc, maps, core_ids=list(range(ncore)), trace=trace)
    R = res.results
    TP, NPR, TS = cfg.TP, cfg.NPR, cfg.TS
    yp = np.concatenate([R[c]["o_yp"].reshape(NPR, TP, D) for c in range(ncore)], 0)
    hs = TS // 2
    ys = np.stack([np.concatenate([R[2 * s]["o_ys"][:hs], R[2 * s + 1]["o_ys"][hs:]], 0) for s in range(nsamp)], 0)

    def tokout(name, tail):
        return np.concatenate([np.transpose(R[c][name].reshape((L, NPR, TP) + tail), (1, 0, 2) + tuple(range(3, 3 + len(tail))))
                               for c in range(ncore)], 0)
    ckv = tokout("o_ckv", (512,)); kr = tokout("o_kr", (64,))
    dk = tokout("o_dk", (8, 128)); dv = tokout("o_dv", (8, 128))
    mlc = np.concatenate([np.swapaxes(R[c]["o_mlc"], -1, -2) for c in range(ncore)], 0)
    mln = np.concatenate([np.swapaxes(R[c]["o_mln"], -1, -2).reshape(NPR, L, 2, 4, 256) for c in range(ncore)], 0)
    mlm = np.concatenate([R[c]["o_mlm"][..., 0] for c in range(ncore)], 0)
    s5r = np.concatenate([_unSL(R[c]["o_s5r"]) for c in range(ncore)], 0)
    s5i = np.concatenate([_unSL(R[c]["o_s5i"]) for c in range(ncore)], 0)
    outs = (yp, ys, ckv, kr, dk, dv, mlc, mln, mlm, s5r, s5i)
    outs = tuple(np.ascontiguousarray(o, dtype=f) for o in outs)
    if trace:
        return outs, res
    return outs


def kernel(**inputs):
    return run(Cfg(), inputs)
```

```python
import contextlib
import math
import numpy as np
import ml_dtypes
import concourse.bass as bass
import concourse.mybir as mybir
from concourse.bass_utils import run_bass_kernel_spmd

F32 = mybir.dt.float32
BF16 = mybir.dt.bfloat16
AF = mybir.ActivationFunctionType
ALU = mybir.AluOpType
AX = mybir.AxisListType

ENGS = ("pe", "act", "dve", "pool", "sp")
EPOCH = 12000
DMA_EPOCH = 700
N_DMA_SEMS = 24


class Dep:
    __slots__ = ("writers", "readers", "ps")

    def __init__(self):
        self.writers = []
        self.readers = []
        self.ps = False


class Prog:
    def __init__(self, nc):
        self.nc = nc
        self.ops = {e: [] for e in ENGS}
        self.count = {e: 0 for e in ENGS}
        self.clock = {e: {} for e in ENGS}
        self.pre = {e: [] for e in ENGS}
        self.last_ev = {e: None for e in ENGS}
        self.dma_i = 0
        self.dma_last = {}
        self.sems = {}
        self.semstack = contextlib.ExitStack()

    def _eng_event(self, eng):
        i = self.count[eng]
        self.count[eng] += 1
        return (("e" + eng, i // EPOCH), i % EPOCH + 1)

    def _need(self, eng, ev, waits):
        key, val = ev
        if self.clock[eng].get(key, 0) >= val:
            return
        self.clock[eng][key] = val
        waits.append(ev)

    def op(self, eng, fn, reads=(), writes=(), dma=False, acc=False):
        waits = []
        for ev in self.pre[eng]:
            self._need(eng, ev, waits)
        self.pre[eng] = []
        me = "e" + eng
        writes = list(writes)
        for d in reads:
            if d.ps and d not in writes:
                writes.append(d)
        for d in reads:
            for ev in d.writers:
                if eng == "pe" and not dma and ev[0][0] == "epe":
                    continue
                self._need(eng, ev, waits)
        if not acc:
            for d in writes:
                for ev in d.writers:
                    if not dma and ev[0][0] == me:
                        continue
                    self._need(eng, ev, waits)
                for ev in d.readers:
                    if not dma and ev[0][0] == me:
                        continue
                    self._need(eng, ev, waits)
        if dma:
            slot = self.dma_i % N_DMA_SEMS
            n = self.dma_i // N_DMA_SEMS
            self.dma_i += 1
            prev = self.dma_last.get(slot)
            if prev is not None:
                self._need(eng, prev, waits)
            ev = (("d%d" % slot, n // DMA_EPOCH), 16 * (n % DMA_EPOCH + 1))
            self.dma_last[slot] = ev
        else:
            ev = self._eng_event(eng)
            self.last_ev[eng] = ev
        self.ops[eng].append((fn, waits, ev, dma))
        for d in reads:
            d.readers.append(ev)
        for d in writes:
            d.writers = [ev]
            d.readers = []
        return ev

    def barrier(self):
        evs = [ev for ev in self.last_ev.values() if ev is not None] + list(self.dma_last.values())
        for e in ENGS:
            self.pre[e] = list(evs)

    def _sem(self, key):
        if key not in self.sems:
            self.sems[key] = self.semstack.enter_context(self.nc.semaphore("s_%s_%d" % key))
        return self.sems[key]

    def emit(self):
        nc = self.nc
        self.barrier()
        final = self.pre["sp"]
        for e in ENGS:
            for (fn, waits, ev, dma) in self.ops[e]:
                self._sem(ev[0])
        names = {"pe": "tensor", "act": "scalar", "dve": "vector", "pool": "gpsimd", "sp": "sync"}
        with nc.Block() as block:
            for e in ENGS:
                def section(engine, ops=self.ops[e], last=(e == "sp")):
                    for (fn, waits, ev, dma) in ops:
                        for (k, v) in waits:
                            engine.wait_ge(self.sems[k], v)
                        fn(engine).then_inc(self.sems[ev[0]], 16 if dma else 1)
                    if last:
                        fin = {}
                        for (k, v) in final:
                            fin[k] = max(fin.get(k, 0), v)
                        for k, v in fin.items():
                            engine.wait_ge(self.sems[k], v)
                getattr(block, names[e])(section)


D = 2048
DEPTH = 2
GRID_W = 64
ML_H, ML_DH, ML_CH = 4, 256, 64
MLA_H, MLA_NOPE, MLA_ROPE, MLA_V, MLA_RANK = 8, 128, 64, 128, 512
S5_G, S5_I, S5_P = 64, 16, 64
DF_H, DF_DQK, DF_DV = 8, 64, 128
FFN_H = 5632
RMS_EPS = 1e-6
IN_SIZES = (1024, 1024, 1024, 1024, 16, 1536, 576, 1024, 1024, 1024, 1024, 8192)
IN_OFF = [0]
for _s in IN_SIZES:
    IN_OFF.append(IN_OFF[-1] + _s)
(O_MLQ, O_MLK, O_MLV, O_MLO, O_MLIF, O_MLAQ, O_KVA, O_S5U, O_DFQ, O_DFK, O_DFV, O_GATE, IN_COLS) = IN_OFF
NEG = -1.0e30


class Cfg:
    def __init__(self, ts=4096, tp=256, npr=2, past=256, g=1024, gw=2048):
        self.TS, self.TP, self.NPR, self.PAST, self.G = ts, tp, npr, past, g
        self.GW = min(gw, ts)


def _d(x):
    return x.d if hasattr(x, "d") else x


class Tile:
    def __init__(self, h):
        self.h = h
        self.d = Dep()

    def __getitem__(self, k):
        return self.h[k]


class Pool:
    def __init__(self, b):
        self.b = b
        self.es = contextlib.ExitStack()

    def __enter__(self):
        return self

    def __exit__(self, *a):
        self.b.P.barrier()
        self.es.close()

    def t(self, shape, dt=F32):
        self.b.uid += 1
        return Tile(self.es.enter_context(self.b.nc.sbuf_tensor("t%d" % self.b.uid, list(shape), dt)))


class Builder:
    def __init__(self, cfg):
        self.cfg = cfg
        self.nc = bass.Bass("TRN2", target_bir_lowering=False)
        self.P = Prog(self.nc)
        self.uid = 0
        self.din = {}
        self.dout = {}
        self.rr = 0

    def inp(self, name, shape, dt=F32):
        a = self.nc.dram_tensor(name, list(shape), dt, kind="ExternalInput").ap()
        self.din[name] = (list(shape), dt)
        return a

    def outp(self, name, shape):
        a = self.nc.dram_tensor(name, list(shape), F32, kind="ExternalOutput").ap()
        self.dout[name] = list(shape)
        return a

    def scr(self, name, shape, dt):
        return self.nc.dram_tensor(name, list(shape), dt, kind="Internal").ap()

    def mm(self, out, lhsT, rhs, start, stop, reads, w):
        self.P.op("pe", lambda e: e.matmul(out, lhsT=lhsT, rhs=rhs, start=start, stop=stop),
                  reads=[_d(r) for r in reads], writes=[_d(w)], acc=not start)

    def tr(self, out, in_, ident, reads, w, first):
        self.P.op("pe", lambda e: e.transpose(out=out, in_=in_, identity=ident),
                  reads=[_d(r) for r in reads], writes=[_d(w)], acc=not first)

    def act(self, out, in_, func, reads, w, scale=1.0, bias=0.0, accum=None):
        ws = [_d(x) for x in (w if isinstance(w, (list, tuple)) else [w])]
        if accum is None:
            f = lambda e: e.activation(out=out, in_=in_, func=func, scale=scale, bias=bias)
        else:
            f = lambda e: e.activation(out=out, in_=in_, func=func, scale=scale, bias=bias, accum_out=accum)
        self.P.op("act", f, reads=[_d(r) for r in reads], writes=ws)

    def tt(self, out, in0, in1, op, reads, w, eng="dve"):
        self.P.op(eng, lambda e: e.tensor_tensor(out=out, in0=in0, in1=in1, op=op),
                  reads=[_d(r) for r in reads], writes=[_d(w)])

    def ts(self, out, in0, s1, op0, reads, w, s2=None, op1=None, eng="dve"):
        if op1 is None:
            f = lambda e: e.tensor_scalar(out=out, in0=in0, scalar1=s1, scalar2=None, op0=op0)
        else:
            f = lambda e: e.tensor_scalar(out=out, in0=in0, scalar1=s1, scalar2=s2, op0=op0, op1=op1)
        self.P.op(eng, f, reads=[_d(r) for r in reads], writes=[_d(w)])

    def stt(self, out, in0, scalar, in1, op0, op1, reads, w):
        self.P.op("dve", lambda e: e.scalar_tensor_tensor(out=out, in0=in0, scalar=scalar, in1=in1, op0=op0, op1=op1),
                  reads=[_d(r) for r in reads], writes=[_d(w)])

    def scan(self, out, d0, d1, init, op0, op1, reads, w):
        self.P.op("dve", lambda e: e.tensor_tensor_scan(out=out, data0=d0, data1=d1, initial=init, op0=op0, op1=op1),
                  reads=[_d(r) for r in reads], writes=[_d(w)])

    def cp(self, out, in_, reads, w, eng=None):
        if eng is None:
            self.rr += 1
            eng = "act" if self.rr % 2 else "dve"
        if eng == "act":
            f = lambda e: e.copy(out=out, in_=in_)
        else:
            f = lambda e: e.tensor_copy(out=out, in_=in_)
        self.P.op(eng, f, reads=[_d(r) for r in reads], writes=[_d(w)])

    def recip(self, out, in_, reads, w):
        self.P.op("dve", lambda e: e.reciprocal(out=out, in_=in_), reads=[_d(r) for r in reads], writes=[_d(w)])

    def memset(self, ap, val, w, eng="pool"):
        self.P.op(eng, lambda e: e.memset(ap, val), writes=[_d(w)])

    def dma(self, out, in_, reads=(), w=None, eng="sp", slow=False):
        ws = [] if w is None else [_d(x) for x in (w if isinstance(w, (list, tuple)) else [w])]
        if slow:
            f = lambda e: e.dma_start(out=out, in_=in_, allow_slow_non_contiguous=True)
        else:
            f = lambda e: e.dma_start(out=out, in_=in_)
        self.P.op(eng, f, reads=[_d(r) for r in reads], writes=ws, dma=True)

    def pool(self):
        return Pool(self)

    def setup(self):
        c = self.cfg
        L = DEPTH
        TS, TP, NPR, PAST = c.TS, c.TP, c.NPR, c.PAST
        self.TPT = NPR * TP
        TM = max(TS, self.TPT)
        self.TM = TM
        I = self.inp
        self.xs = I("xs", [TS, D]); self.xp = I("xp", [self.TPT, D])
        self.condT = I("condT", [128, 16, 2])
        self.ckv_c = I("ckv_c", [L, PAST, 512]); self.kr_c = I("kr_c", [L, PAST, 64])
        self.dk_c = I("dk_c", [L, PAST, 1024]); self.dv_c = I("dv_c", [L, PAST, 1024])
        self.mlc0T = I("mlc0T", [L, 2, 4, 256, 256]); self.mln0 = I("mln0", [L, 2, 4, 128, 2])
        self.mlm0 = I("mlm0", [L, 2, 4, 1])
        self.s5h0r = I("s5h0r", [L, 2, 128, 32]); self.s5h0i = I("s5h0i", [L, 2, 128, 32])
        self.w_ada = I("w_ada", [L, D, 6 * D]); self.b_adaT = I("b_adaT", [L, 128, 96])
        self.nmixT = I("nmixT", [L, 128, 16]); self.nffnT = I("nffnT", [L, 128, 16])
        self.w_in = I("w_in", [L, D, IN_COLS]); self.mlifb = I("mlifb", [L, 4, 4])
        self.mlnormT = I("mlnormT", [L, 128, 8]); self.kvnorm = I("kvnorm", [L, 1, 512])
        self.w_kvb = I("w_kvb", [L, 512, 2048])
        self.s5_are = I("s5_are", [L, 2, 1, 4096]); self.s5_aim = I("s5_aim", [L, 2, 1, 4096])
        self.s5_ldt = I("s5_ldt", [L, 2, 1, 4096])
        self.s5_areS = I("s5_areS", [L, 2, 128, 32]); self.s5_aimS = I("s5_aimS", [L, 2, 128, 32])
        self.s5_ldtS = I("s5_ldtS", [L, 2, 128, 32])
        self.s5_breT = I("s5_breT", [L, 2, 32, 4096]); self.s5_bimT = I("s5_bimT", [L, 2, 32, 4096])
        self.s5_creT = I("s5_creT", [L, 2, 128, 32, 32]); self.s5_cimT = I("s5_cimT", [L, 2, 128, 32, 32])
        self.s5_dT = I("s5_dT", [L, 128, 8]); self.w_glu = I("w_glu", [L, 1024, 1024])
        self.df_lam = I("df_lam", [L, 1, 256]); self.dfnormT = I("dfnormT", [L, 128, 1])
        self.w_br = I("w_br", [L, 4, 1024, D]); self.w_o = I("w_o", [L, D, D])
        self.w_f1 = I("w_f1", [L, D, 2 * FFN_H]); self.w_f2 = I("w_f2", [L, FFN_H, D])
        self.fnorm = I("fnorm", [1, D])
        self.c_ident = I("c_ident", [128, 128]); self.c_ones = I("c_ones", [128, 128])
        self.c_maskL = I("c_maskL", [64, 64]); self.c_maskU = I("c_maskU", [64, 64])
        self.c_sel = I("c_sel", [4, 4, 128]); self.c_ropeR = I("c_ropeR", [128, 128])
        self.c_ropeC = I("c_ropeC", [128, TS]); self.c_ropeS = I("c_ropeS", [128, TS])
        O = self.outp
        self.o_ys = O("o_ys", [TS, D]); self.o_yp = O("o_yp", [self.TPT, D])
        self.o_ckv = O("o_ckv", [L, self.TPT, 512]); self.o_kr = O("o_kr", [L, self.TPT, 64])
        self.o_dk = O("o_dk", [L, self.TPT, 1024]); self.o_dv = O("o_dv", [L, self.TPT, 1024])
        self.o_mlc = O("o_mlc", [NPR, L, 2, 4, 256, 256]); self.o_mln = O("o_mln", [NPR, L, 2, 4, 128, 2])
        self.o_mlm = O("o_mlm", [NPR, L, 2, 4, 1])
        self.o_s5r = O("o_s5r", [NPR, L, 2, 128, 32]); self.o_s5i = O("o_s5i", [NPR, L, 2, 128, 32])
        S = self.scr
        self.xcs = S("xcs", [TS, D], F32); self.xcp = S("xcp", [self.TPT, D], F32)
        self.gsc = S("gsc", [2, 2, D], F32)
        self.q_mlT = S("q_mlT", [1024, TM], BF16); self.k_mlT = S("k_mlT", [1024, TM], BF16)
        self.k_ml = S("k_ml", [TM, 1024], BF16); self.v_ml = S("v_ml", [TM, 1024], BF16)
        self.o_mlT = S("o_mlT", [1024, TM], BF16); self.gates = S("gates", [4, 4, TM], F32)
        self.qnT = S("qnT", [1024, TM], BF16); self.qrT = S("qrT", [512, TM], BF16)
        self.ckvT = S("ckvT", [512, TM + PAST], BF16); self.krT = S("krT", [128, TM + PAST], BF16)
        self.s5uT = S("s5uT", [1024, TM], BF16)
        self.dqT = S("dqT", [1024, TM], BF16); self.dkT = S("dkT", [1024, TM + PAST], BF16)
        self.dvv = S("dvv", [TM + PAST, 1024], BF16)
        self.gateT = S("gateT", [8192, TM], BF16); self.yT = S("yT", [4, 1024, TM], BF16)
        self.hfb = S("hfb", [2, TM, 1024], F32); self.ys5 = S("ys5", [2, 1024, TM], F32)
        self.gp = Pool(self)
        g = self.gp
        self.ps = []
        for i in range(8):
            self.uid += 1
            self.ps.append(Tile(g.es.enter_context(self.nc.psum_tensor("ps%d" % i, [128, 512], F32))))
            self.ps[-1].d.ps = True
        self.identf = g.t([128, 128]); self.onesf = g.t([128, 128])
        self.identb = g.t([128, 128], BF16); self.onesb = g.t([128, 128], BF16)
        self.maskL = g.t([64, 64]); self.maskU = g.t([64, 64]); self.sel = g.t([4, 4, 128])
        self.ropeRb = g.t([128, 128], BF16)
        self.modT = g.t([128, 96, 2]); self.A1 = g.t([128, 2, 16]); self.A2 = g.t([128, 2, 16])
        self.eps = g.t([128, 1])
        self.dma(self.identf[:], self.c_ident, w=self.identf); self.dma(self.onesf[:], self.c_ones, w=self.onesf)
        self.dma(self.maskL[:], self.c_maskL, w=self.maskL); self.dma(self.maskU[:], self.c_maskU, w=self.maskU)
        self.dma(self.sel[:], self.c_sel, w=self.sel)
        self.dma(self.ropeRb[:], self.c_ropeR, w=self.ropeRb, eng="pool")
        self.cp(self.identb[:], self.identf[:], [self.identf], self.identb, eng="dve")
        self.cp(self.onesb[:], self.onesf[:], [self.onesf], self.onesb, eng="dve")
        self.memset(self.eps[:], RMS_EPS, self.eps)
        self.psi = 0

    def bank(self, lo=0, hi=4):
        b = lo + self.psi % (hi - lo)
        self.psi += 1
        return self.ps[b]

    def phase_ada(self, l):
        with self.pool() as A:
            scT = A.t([128, 16, 2]); bT = A.t([128, 96]); nm = A.t([128, 16]); nf = A.t([128, 16])
            self.dma(scT[:], self.condT, w=scT)
            self.dma(bT[:], self.b_adaT[l], w=bT); self.dma(nm[:], self.nmixT[l], w=nm); self.dma(nf[:], self.nffnT[l], w=nf)
            self.act(scT[:], scT[:], AF.Silu, [scT], scT)
            wst = [A.t([128, 16, 128]) for _ in range(3)]
            wv = self.w_ada[l].rearrange("(k p) c -> p k c", p=128)
            ps = self.ps[0]
            for cb in range(96):
                w = wst[cb % 3]
                self.dma(w[:], wv[:, :, cb * 128:(cb + 1) * 128], w=w, eng=("sp" if cb % 2 else "act"))
                for kc in range(16):
                    self.mm(ps[:, cb * 2:cb * 2 + 2], w[:, kc, :], scT[:, kc, :], kc == 0, kc == 15, [w, scT], ps)
            m = self.modT
            psv = ps[:, 0:192].rearrange("p (c t) -> p c t", t=2)
            for cnd in range(2):
                self.tt(m[:, :, cnd], psv[:, :, cnd], bT[:], ALU.add, [ps, bT], m)
            g4 = A.t([128, 16]); g5 = A.t([16, 128])
            for cnd in range(2):
                self.stt(self.A1[:, cnd, :], m[:, 16:32, cnd], 1.0, nm[:], ALU.add, ALU.mult, [m, nm], self.A1)
                self.stt(self.A2[:, cnd, :], m[:, 64:80, cnd], 1.0, nf[:], ALU.add, ALU.mult, [m, nf], self.A2)
                for gi, off in enumerate((32, 80)):
                    self.cp(g4[:], m[:, off:off + 16, cnd], [m], g4, eng="dve")
                    p2 = self.ps[1]
                    self.mm(p2[0:16, 0:128], g4[:], self.identf[:], True, True, [g4, self.identf], p2)
                    self.cp(g5[:], p2[0:16, 0:128], [p2], g5, eng="dve")
                    self.dma(self.gsc[gi, cnd].rearrange("(j p) -> j p", p=128), g5[:], [g5])

    def norm_T(self, A, xsrc, tok0, ntile, hT, acol, bcol_fn):
        xt = [A.t([128, D]) for _ in range(2)]
        xn = [A.t([128, D], BF16) for _ in range(2)]
        junk = A.t([128, D], BF16)
        ss = A.t([128, ntile])
        for t in range(ntile):
            x = xt[t % 2]; n = xn[t % 2]
            self.dma(x[:], xsrc[tok0 + t * 128: tok0 + (t + 1) * 128, :], w=x)
            self.act(junk[:], x[:], AF.Square, [x], [junk, ss], accum=ss[:, t:t + 1])
            self.act(ss[:, t:t + 1], ss[:, t:t + 1], AF.Sqrt, [ss], ss, scale=1.0 / D, bias=self.eps[:, 0:1])
            self.recip(ss[:, t:t + 1], ss[:, t:t + 1], [ss], ss)
            self.ts(n[:], x[:], ss[:, t:t + 1], ALU.mult, [x, ss], n)
            for half in range(2):
                pb = self.bank()
                pv = pb[:].bitcast(BF16)
                for j in range(8):
                    kc = half * 8 + j
                    self.tr(pv[:, j * 128:(j + 1) * 128], n[:, kc * 128:(kc + 1) * 128], self.identb[:],
                            [n, self.identb], pb, j == 0)
                self.cp(hT[:, half * 8:(half + 1) * 8, t * 128:(t + 1) * 128],
                        pv[:, 0:1024].rearrange("p (k t) -> p k t", k=8), [pb], hT)
        nt = ntile * 128
        for kc in range(16):
            self.ts(hT[:, kc, 0:nt], hT[:, kc, 0:nt], acol(kc), ALU.mult, [hT, self.A1, self.A2, self.modT], hT,
                    s2=bcol_fn(kc), op1=ALU.add)

    def load_w(self, wt, wsrc, pieces, kch=16):
        wv = wsrc.rearrange("(k p) c -> p k c", p=128)
        for (c0, n, off) in pieces:
            self.dma(wt[:, 0:kch, off:off + n], wv[:, :, c0:c0 + n], w=wt, eng="pool")

    def proj_F(self, A, hT, ntok, wsrc, blocks, kch=16):
        wts = [A.t([128, kch, 128], BF16) for _ in range(6)]
        nch = (ntok + 511) // 512
        for bi, (pieces, M, evac) in enumerate(blocks):
            wt = wts[bi % 6]
            self.load_w(wt, wsrc, pieces, kch)
            for j in range(nch):
                n = min(512, ntok - j * 512)
                pb = self.bank()
                for kc in range(kch):
                    self.mm(pb[0:M, 0:n], wt[:, kc, 0:M], hT[:, kc, j * 512:j * 512 + n], kc == 0, kc == kch - 1, [wt, hT], pb)
                evac(pb, j, M, n)

    def proj_T(self, A, hT, ntile, wsrc, blocks, kch=16):
        wts = [A.t([128, kch, 512], BF16) for _ in range(2)]
        for bi, (c0, ncols, evac) in enumerate(blocks):
            wt = wts[bi % 2]
            self.load_w(wt, wsrc, [(c0, ncols, 0)], kch)
            for t in range(ntile):
                pb = self.bank()
                for kc in range(kch):
                    self.mm(pb[:, 0:ncols], hT[:, kc, t * 128:(t + 1) * 128], wt[:, kc, 0:ncols], kc == 0, kc == kch - 1, [wt, hT], pb)
                evac(pb, t, ncols)

    def phase_win(self, l, xsrc, tok0, ntok, cnd, isS):
        ntile = ntok // 128
        wsrc = self.w_in[l]
        with self.pool() as A:
            hT = A.t([128, 16, ntok], BF16)
            with self.pool() as A0:
                self.norm_T(A0, xsrc, tok0, ntile, hT,
                            lambda kc: self.A1[:, cnd, kc:kc + 1], lambda kc: self.modT[:, kc, cnd:cnd + 1])
            stg = [A.t([128, 512], BF16) for _ in range(4)]
            stf = [A.t([128, 512]) for _ in range(3)]
            self.si = 0

            def stage():
                self.si += 1
                return stg[self.si % 4]

            def stagef():
                self.si += 1
                return stf[self.si % 3]

            if isS:
                rC = A.t([128, ntok]); rS = A.t([128, ntok])
                self.dma(rC[:], self.c_ropeC[:, tok0:tok0 + ntok], w=rC)
                self.dma(rS[:], self.c_ropeS[:, tok0:tok0 + ntok], w=rS)
            bia = A.t([4, 4])
            self.dma(bia[:], self.mlifb[l], w=bia)

            def ev_store(dst, row0, func=None, scale=1.0):
                def ev(pb, j, M, n):
                    s = stage()
                    if func is None and scale == 1.0:
                        self.cp(s[0:M, 0:n], pb[0:M, 0:n], [pb], s)
                    else:
                        self.act(s[0:M, 0:n], pb[0:M, 0:n], func or AF.Copy, [pb], s, scale=scale)
                    self.dma(dst[row0:row0 + M, tok0 + j * 512: tok0 + j * 512 + n], s[0:M, 0:n], [s])
                return ev

            def ev_rope(dst, row0):
                def ev(pb, j, M, n):
                    xb = stage()
                    self.cp(xb[:, 0:n], pb[:, 0:n], [pb], xb)
                    p2 = self.bank()
                    self.mm(p2[:, 0:n], self.ropeRb[:], xb[:, 0:n], True, True, [self.ropeRb, xb], p2)
                    t1 = stagef(); t2 = stagef(); s = stage()
                    self.tt(t1[:, 0:n], pb[:, 0:n], rC[:, j * 512:j * 512 + n], ALU.mult, [pb, rC], t1)
                    self.tt(t2[:, 0:n], p2[:, 0:n], rS[:, j * 512:j * 512 + n], ALU.mult, [p2, rS], t2)
                    self.tt(s[:, 0:n], t1[:, 0:n], t2[:, 0:n], ALU.add, [t1, t2], s, eng="pool")
                    self.dma(dst[row0:row0 + 128, tok0 + j * 512: tok0 + j * 512 + n], s[:, 0:n], [s])
                return ev

            def ev_gate(kind):
                def ev(pb, j, M, n):
                    s = stagef()
                    self.act(s[0:4, 0:n], pb[0:4, 0:n], AF.Identity, [pb, bia], s, bias=bia[:, kind:kind + 1])
                    self.dma(self.gates[kind, :, tok0 + j * 512: tok0 + j * 512 + n], s[0:4, 0:n], [s])
                return ev

            rp = ev_rope if isS else ev_store
            blocks = []
            for b in range(8):
                blocks.append(([(O_MLQ + b * 128, 128, 0)], 128, ev_store(self.q_mlT, b * 128)))
                blocks.append(([(O_MLK + b * 128, 128, 0)], 128, ev_store(self.k_mlT, b * 128, AF.Copy, 0.0625)))
                blocks.append(([(O_MLO + b * 128, 128, 0)], 128, ev_store(self.o_mlT, b * 128, AF.Sigmoid)))
                blocks.append(([(O_MLAQ + b * 192, 128, 0)], 128, ev_store(self.qnT, b * 128)))
                blocks.append(([(O_S5U + b * 128, 128, 0)], 128, ev_store(self.s5uT, b * 128)))
                blocks.append(([(O_DFQ + b * 128, 128, 0)], 128, rp(self.dqT, b * 128)))
                blocks.append(([(O_DFK + b * 128, 128, 0)], 128, rp(self.dkT, b * 128)))
            for k in range(4):
                blocks.append(([(O_MLIF + (k // 2) * 8 + (k % 2) * 4, 4, 0)], 4, ev_gate(k)))
            for p in range(4):
                blocks.append(([(O_MLAQ + (2 * p) * 192 + 128, 64, 0), (O_MLAQ + (2 * p + 1) * 192 + 128, 64, 64)], 128,
                               rp(self.qrT, p * 128)))
            blocks.append(([(O_KVA + 512, 64, 0), (O_KVA + 512, 64, 64)], 128, rp(self.krT, 0)))
            for b in range(64):
                blocks.append(([(O_GATE + b * 128, 128, 0)], 128, ev_store(self.gateT, b * 128, AF.Sigmoid)))
            import os
            sub = int(os.environ.get("MK_SUB", "9"))
            nblk = int(os.environ.get("MK_NBLK", "100000"))
            if sub >= 2:
                self.proj_F(A, hT, ntok, wsrc, blocks[:nblk])
            if sub < 3:
                return

            kvn = A.t([128, 512]); ssk = A.t([128, ntile])
            self.dma(kvn[:], self.kvnorm[l].partition_broadcast(128), w=kvn)
            junk = A.t([128, 512], BF16)

            def evT_store(dst, c0, scale=1.0, outf=None):
                def ev(pb, t, ncols):
                    s = stage()
                    r0 = tok0 + t * 128
                    if scale == 1.0:
                        self.cp(s[:, 0:ncols], pb[:, 0:ncols], [pb], s)
                    else:
                        self.act(s[:, 0:ncols], pb[:, 0:ncols], AF.Copy, [pb], s, scale=scale)
                    self.dma(dst[r0:r0 + 128, c0:c0 + ncols], s[:, 0:ncols], [s])
                    if outf is not None:
                        f = stagef()
                        self.cp(f[:, 0:ncols], pb[:, 0:ncols], [pb], f)
                        self.dma(outf[l, r0:r0 + 128, c0:c0 + ncols], f[:, 0:ncols], [f])
                return ev

            def evT_out(outf, c0):
                def ev(pb, t, ncols):
                    f = stagef()
                    r0 = tok0 + t * 128
                    self.cp(f[:, 0:ncols], pb[:, 0:ncols], [pb], f)
                    self.dma(outf[l, r0:r0 + 128, c0:c0 + ncols], f[:, 0:ncols], [f])
                return ev

            def evT_kva(pb, t, ncols):
                r0 = tok0 + t * 128
                self.act(junk[:], pb[:], AF.Square, [pb], [junk, ssk], accum=ssk[:, t:t + 1])
                self.act(ssk[:, t:t + 1], ssk[:, t:t + 1], AF.Sqrt, [ssk, self.eps], ssk, scale=1.0 / 512, bias=self.eps[:, 0:1])
                self.recip(ssk[:, t:t + 1], ssk[:, t:t + 1], [ssk], ssk)
                f = stagef()
                self.stt(f[:], pb[:], ssk[:, t:t + 1], kvn[:], ALU.mult, ALU.mult, [pb, ssk, kvn], f)
                if not isS:
                    self.dma(self.o_ckv[l, r0:r0 + 128, :], f[:], [f])
                s = stage()
                self.cp(s[:], f[:], [f], s)
                p2 = self.bank()
                pv = p2[:].bitcast(BF16)
                for kc in range(4):
                    self.tr(pv[:, kc * 128:(kc + 1) * 128], s[:, kc * 128:(kc + 1) * 128], self.identb[:], [s, self.identb], p2, kc == 0)
                s2 = stage()
                self.cp(s2[:], pv[:, 0:512], [p2], s2)
                self.dma(self.ckvT.rearrange("(k p) t -> p k t", p=128)[:, :, r0:r0 + 128],
                         s2[:].rearrange("p (k t) -> p k t", k=4), [s2])

            tb = []
            for hlf in range(2):
                tb.append((O_MLK + hlf * 512, 512, evT_store(self.k_ml, hlf * 512, 0.0625)))
                tb.append((O_MLV + hlf * 512, 512, evT_store(self.v_ml, hlf * 512)))
                tb.append((O_DFV + hlf * 512, 512, evT_store(self.dvv, hlf * 512, 1.0, None if (isS or os.environ.get("MK_VAR") == "noout") else self.o_dv)))
                if not isS:
                    tb.append((O_DFK + hlf * 512, 512, evT_out(self.o_dk, hlf * 512)))
            tb.append((O_KVA, 512, evT_kva))
            if not isS:
                tb.append((O_KVA + 512, 64, evT_out(self.o_kr, 0)))
            tb = tb[::-1][:int(os.environ.get("MK_NT", "1000"))]
            self.proj_T(A, hT, ntile, wsrc, tb)

    def phase_ctx(self, l):
        c = self.cfg
        TS, PAST = c.TS, c.PAST
        with self.pool() as A:
            for t in range(PAST // 128):
                r0 = TS + t * 128
                a = A.t([128, 512], BF16); kr = A.t([128, 64], BF16); dk = A.t([128, 1024], BF16); dv = A.t([128, 1024], BF16)
                self.dma(a[:], self.ckv_c[l, t * 128:(t + 1) * 128, :], w=a, eng="pool")
                self.dma(kr[:], self.kr_c[l, t * 128:(t + 1) * 128, :], w=kr, eng="pool")
                self.dma(dk[:], self.dk_c[l, t * 128:(t + 1) * 128, :], w=dk, eng="pool")
                self.dma(dv[:], self.dv_c[l, t * 128:(t + 1) * 128, :], w=dv, eng="pool")
                self.dma(self.dvv[r0:r0 + 128, :], dv[:], [dv])
                p2 = self.bank(); pv = p2[:].bitcast(BF16)
                for kc in range(4):
                    self.tr(pv[:, kc * 128:(kc + 1) * 128], a[:, kc * 128:(kc + 1) * 128], self.identb[:], [a, self.identb], p2, kc == 0)
                s2 = A.t([128, 512], BF16)
                self.cp(s2[:], pv[:, 0:512], [p2], s2)
                self.dma(self.ckvT.rearrange("(k p) t -> p k t", p=128)[:, :, r0:r0 + 128], s2[:].rearrange("p (k t) -> p k t", k=4), [s2])
                p3 = self.bank(); pv3 = p3[:].bitcast(BF16)
                self.tr(pv3[0:64, 0:128], kr[:, 0:64], self.identb[:], [kr, self.identb], p3, True)
                s3 = A.t([64, 128], BF16)
                self.cp(s3[:], pv3[0:64, 0:128], [p3], s3)
                self.dma(self.krT[0:64, r0:r0 + 128], s3[:], [s3]); self.dma(self.krT[64:128, r0:r0 + 128], s3[:], [s3])
                for hh in range(2):
                    p4 = self.bank(); pv4 = p4[:].bitcast(BF16)
                    for j in range(4):
                        b = hh * 4 + j
                        self.tr(pv4[:, j * 128:(j + 1) * 128], dk[:, b * 128:(b + 1) * 128], self.identb[:], [dk, self.identb], p4, j == 0)
                    s4 = A.t([128, 512], BF16)
                    self.cp(s4[:], pv4[:, 0:512], [p4], s4)
                    self.dma(self.dkT.rearrange("(k p) t -> p k t", p=128)[:, hh * 4:(hh + 1) * 4, r0:r0 + 128],
                             s4[:].rearrange("p (k t) -> p k t", k=4), [s4])

    def key_ranges(self, tok0, T, isS):
        r = [(tok0 + i * 128) for i in range(T // 128)]
        if isS:
            r += [(self.cfg.TS + i * 128) for i in range(self.cfg.PAST // 128)]
        return r

    def phase_mla(self, l, tok0, T, isS):
        kts = self.key_ranges(tok0, T, isS)
        NKT = len(kts)
        QC = min(512, T)
        sc = (MLA_NOPE + MLA_ROPE) ** -0.5
        with self.pool() as A:
            ckv = A.t([128, 4, NKT * 128], BF16); kr = A.t([128, NKT * 128], BF16)
            ckvv = self.ckvT.rearrange("(k p) t -> p k t", p=128)
            self.dma(ckv[:, :, 0:T], ckvv[:, :, tok0:tok0 + T], w=ckv); self.dma(kr[:, 0:T], self.krT[:, tok0:tok0 + T], w=kr)
            if isS:
                TS, PA = self.cfg.TS, self.cfg.PAST
                self.dma(ckv[:, :, T:T + PA], ckvv[:, :, TS:TS + PA], w=ckv); self.dma(kr[:, T:T + PA], self.krT[:, TS:TS + PA], w=kr)
            wk = [A.t([128, 4, 128], BF16) for _ in range(2)]; wv = [A.t([128, 4, 128], BF16) for _ in range(2)]
            knT = [A.t([128, NKT * 128], BF16) for _ in range(2)]; vt = [A.t([128, NKT, 128], BF16) for _ in range(2)]
            qn = [A.t([128, T], BF16) for _ in range(2)]; qr = [A.t([128, T], BF16) for _ in range(2)]
            pts = [A.t([128, 512], BF16) for _ in range(4)]
            rc = A.t([128, 512]); ob = [A.t([128, 512], BF16) for _ in range(2)]
            pi = 0
            for h in range(MLA_H):
                b = h % 2
                self.load_w(wk[b], self.w_kvb[l], [(h * 256, 128, 0)], 4)
                self.load_w(wv[b], self.w_kvb[l], [(h * 256 + 128, 128, 0)], 4)
                self.dma(qn[b][:], self.qnT[h * 128:(h + 1) * 128, tok0:tok0 + T], w=qn[b])
                self.dma(qr[b][:], self.qrT[(h // 2) * 128:(h // 2 + 1) * 128, tok0:tok0 + T], w=qr[b])
                nk = NKT * 128
                for j in range((nk + 511) // 512):
                    n = min(512, nk - j * 512)
                    pb = self.bank(0, 3)
                    for kc in range(4):
                        self.mm(pb[:, 0:n], wk[b][:, kc, :], ckv[:, kc, j * 512:j * 512 + n], kc == 0, kc == 3, [wk[b], ckv], pb)
                    self.cp(knT[b][:, j * 512:j * 512 + n], pb[:, 0:n], [pb], knT[b])
                for g in range((NKT + 3) // 4):
                    m = min(4, NKT - g * 4)
                    pb = self.bank(0, 3)
                    for i in range(m):
                        kt = g * 4 + i
                        for kc in range(4):
                            self.mm(pb[:, i * 128:(i + 1) * 128], ckv[:, kc, kt * 128:(kt + 1) * 128], wv[b][:, kc, :],
                                    kc == 0, kc == 3, [wv[b], ckv], pb)
                    self.cp(vt[b][:, g * 4:g * 4 + m, :], pb[:, 0:m * 128].rearrange("p (k t) -> p k t", k=m), [pb], vt[b])
                hp = (h % 2) * 64
                for qc in range(T // QC):
                    q0 = qc * QC
                    ao = self.ps[4 + qc % 2]; ad = self.ps[6 + qc % 2]
                    LAG = 2
                    pend = []
                    for kt in range(NKT + LAG):
                        if kt < NKT:
                            pb = self.bank(0, 4)
                            self.mm(pb[:, 0:QC], knT[b][:, kt * 128:(kt + 1) * 128], qn[b][:, q0:q0 + QC], True, False, [knT[b], qn[b]], pb)
                            self.mm(pb[:, 0:QC], kr[hp:hp + 64, kt * 128:(kt + 1) * 128], qr[b][hp:hp + 64, q0:q0 + QC], False, True, [kr, qr[b]], pb)
                            pt = pts[pi % 4]; pi += 1
                            self.act(pt[:, 0:QC], pb[:, 0:QC], AF.Exp, [pb], pt, scale=sc)
                            pend.append((kt, pt))
                        if kt >= LAG:
                            k2, p2 = pend.pop(0)
                            self.mm(ao[:, 0:QC], vt[b][:, k2, :], p2[:, 0:QC], k2 == 0, k2 == NKT - 1, [vt[b], p2], ao)
                            self.mm(ad[:, 0:QC], self.onesb[:], p2[:, 0:QC], k2 == 0, k2 == NKT - 1, [self.onesb, p2], ad)
                    self.recip(rc[:, 0:QC], ad[:, 0:QC], [ad], rc)
                    o = ob[qc % 2]
                    self.tt(o[:, 0:QC], ao[:, 0:QC], rc[:, 0:QC], ALU.mult, [ao, rc], o)
                    self.dma(self.yT[1, h * 128:(h + 1) * 128, tok0 + q0:tok0 + q0 + QC], o[:, 0:QC], [o])

    def phase_diff(self, l, tok0, T, isS):
        kts = self.key_ranges(tok0, T, isS)
        NKT = len(kts)
        QC = min(512, T)
        lam_init = 0.8 - 0.6 * math.exp(-0.3 * l)
        with self.pool() as A:
            lb = A.t([128, 256]); pr = A.t([128, 128]); sm = A.t([128, 4]); dfs = A.t([128, 1])
            self.dma(lb[:], self.df_lam[l].partition_broadcast(128), w=lb)
            self.dma(dfs[:], self.dfnormT[l], w=dfs)
            self.tt(pr[:, 0:64], lb[:, 0:64], lb[:, 64:128], ALU.mult, [lb], pr)
            self.tt(pr[:, 64:128], lb[:, 128:192], lb[:, 192:256], ALU.mult, [lb], pr)
            self.P.op("dve", lambda e: e.reduce_sum(out=sm[:, 0:1], in_=pr[:, 0:64], axis=AX.X), reads=[pr.d], writes=[sm.d])
            self.P.op("dve", lambda e: e.reduce_sum(out=sm[:, 1:2], in_=pr[:, 64:128], axis=AX.X), reads=[pr.d], writes=[sm.d])
            self.act(sm[:, 0:2], sm[:, 0:2], AF.Exp, [sm], sm)
            self.tt(sm[:, 2:3], sm[:, 1:2], sm[:, 0:1], ALU.subtract, [sm], sm)
            self.ts(sm[:, 3:4], sm[:, 2:3], -lam_init, ALU.add, [sm], sm)
            self.ts(dfs[:], dfs[:], 1.0 - lam_init, ALU.mult, [dfs], dfs)
            kT = [A.t([128, NKT * 128], BF16) for _ in range(2)]; vt = [A.t([128, NKT, 128], BF16) for _ in range(2)]
            q = [A.t([128, T], BF16) for _ in range(2)]
            pts = [A.t([128, 512], BF16) for _ in range(6)]
            rc = A.t([128, 512]); o0 = A.t([128, 512]); o1 = A.t([128, 512]); sq = A.t([128, 512]); ob = [A.t([128, 512], BF16) for _ in range(2)]
            pi = 0
            dvr = self.dvv.rearrange("(k p) c -> p k c", p=128)
            for h in range(DF_H):
                b = h % 2
                self.dma(q[b][:], self.dqT[h * 128:(h + 1) * 128, tok0:tok0 + T], w=q[b])
                self.dma(kT[b][:, 0:T], self.dkT[h * 128:(h + 1) * 128, tok0:tok0 + T], w=kT[b])
                self.dma(vt[b][:, 0:T // 128, :], dvr[:, tok0 // 128:(tok0 + T) // 128, h * 128:(h + 1) * 128], w=vt[b])
                if isS:
                    TS, PA = self.cfg.TS, self.cfg.PAST
                    self.dma(kT[b][:, T:T + PA], self.dkT[h * 128:(h + 1) * 128, TS:TS + PA], w=kT[b])
                    self.dma(vt[b][:, T // 128:NKT, :], dvr[:, TS // 128:(TS + PA) // 128, h * 128:(h + 1) * 128], w=vt[b])
                for qc in range(T // QC):
                    q0 = qc * QC
                    ao = [self.ps[4], self.ps[5]]; ad = [self.ps[6], self.ps[7]]
                    LAG = 1
                    pend = []
                    for kt in range(NKT + LAG):
                        if kt < NKT:
                            for cc in range(2):
                                pb = self.bank(0, 4)
                                lo = cc * 64
                                self.mm(pb[:, 0:QC], kT[b][lo:lo + 64, kt * 128:(kt + 1) * 128], q[b][lo:lo + 64, q0:q0 + QC], True, True, [kT[b], q[b]], pb)
                                pt = pts[pi % 6]; pi += 1
                                self.act(pt[:, 0:QC], pb[:, 0:QC], AF.Exp, [pb], pt, scale=DF_DQK ** -0.5)
                                pend.append((kt, cc, pt))
                        if kt >= LAG:
                            for _ in range(2):
                                k2, cc, p2 = pend.pop(0)
                                self.mm(ao[cc][:, 0:QC], vt[b][:, k2, :], p2[:, 0:QC], k2 == 0, k2 == NKT - 1, [vt[b], p2], ao[cc])
                                self.mm(ad[cc][:, 0:QC], self.onesb[:], p2[:, 0:QC], k2 == 0, k2 == NKT - 1, [self.onesb, p2], ad[cc])
                    self.recip(rc[:, 0:QC], ad[0][:, 0:QC], [ad[0]], rc)
                    self.tt(o0[:, 0:QC], ao[0][:, 0:QC], rc[:, 0:QC], ALU.mult, [ao[0], rc], o0)
                    self.recip(rc[:, 0:QC], ad[1][:, 0:QC], [ad[1]], rc)
                    self.tt(o1[:, 0:QC], ao[1][:, 0:QC], rc[:, 0:QC], ALU.mult, [ao[1], rc], o1)
                    self.stt(o0[:, 0:QC], o1[:, 0:QC], sm[:, 3:4], o0[:, 0:QC], ALU.mult, ALU.add, [o1, sm, o0], o0)
                    self.act(sq[:, 0:QC], o0[:, 0:QC], AF.Square, [o0], sq)
                    pn = self.bank(0, 4)
                    self.mm(pn[:, 0:QC], self.onesf[:], sq[:, 0:QC], True, True, [self.onesf, sq], pn)
                    self.act(sq[:, 0:QC], pn[:, 0:QC], AF.Sqrt, [pn, self.eps], sq, scale=1.0 / DF_DV, bias=self.eps[:, 0:1])
                    self.recip(sq[:, 0:QC], sq[:, 0:QC], [sq], sq)
                    o = ob[qc % 2]
                    self.stt(o[:, 0:QC], o0[:, 0:QC], dfs[:, 0:1], sq[:, 0:QC], ALU.mult, ALU.mult, [o0, dfs, sq], o)
                    self.dma(self.yT[3, h * 128:(h + 1) * 128, tok0 + q0:tok0 + q0 + QC], o[:, 0:QC], [o])

    def phase_mlstm(self, l, tok0, T, isS, seq):
        NCH = T // 64
        for dr in range(2):
            rev = (dr == 1)

            def V(ap):
                return ap[:, ::-1] if rev else ap
            with self.pool() as O:
                rrow = O.t([4, T]); XT = O.t([64, NCH, 16]); wcb = O.t([128, 4, NCH])
                with self.pool() as A:
                    X = [A.t([4, T]) for _ in range(5)]
                    mk = A.t([4, T], BF16); pen = A.t([4, T], BF16)
                    X16 = A.t([16, T])
                    ac = A.t([4, NCH]); cme = A.t([4, NCH]); cmx = A.t([4, NCH]); msq = A.t([4, NCH]); mpv = A.t([4, NCH])
                    cc = A.t([4, NCH]); wc = A.t([4, NCH]); m0 = A.t([4, 1])
                    self.memset(mk[:], 1.0, mk); self.memset(pen[:], 0.0, pen)
                    self.memset(mk[:].rearrange("p (c l) -> p c l", l=64)[:, :, 0:1], 0.0, mk)
                    self.memset(pen[:].rearrange("p (c l) -> p c l", l=64)[:, :, 0:1], NEG, pen)
                    self.dma(X[0][:], self.gates[2 * dr, :, tok0:tok0 + T], w=X[0])
                    self.dma(X[1][:], self.gates[2 * dr + 1, :, tok0:tok0 + T], w=X[1])
                    if isS:
                        self.dma(m0[:], self.mlm0[l, dr], w=m0)
                    else:
                        self.memset(m0[:], 0.0, m0)
                    self.act(X[2][:], X[1][:], AF.Abs, [X[1]], X[2])
                    self.act(X[2][:], X[2][:], AF.Exp, [X[2]], X[2], scale=-1.0)
                    self.act(X[2][:], X[2][:], AF.Ln, [X[2]], X[2], bias=1.0)
                    self.ts(X[3][:], X[1][:], 0.0, ALU.min, [X[1]], X[3])
                    self.tt(X[1][:], X[3][:], X[2][:], ALU.subtract, [X[3], X[2]], X[1])
                    self.scan(V(X[2][:]), mk[:], V(X[1][:]), 0.0, ALU.mult, ALU.add, [mk, X[1]], X[2])
                    self.tt(X[0][:], X[0][:], X[2][:], ALU.subtract, [X[0], X[2]], X[0])
                    self.scan(V(X[1][:]), pen[:], V(X[0][:]), NEG, ALU.add, ALU.max, [pen, X[0]], X[1])
                    e = 0 if rev else 63
                    b3 = X[2][:].rearrange("p (c l) -> p c l", l=64); c3 = X[1][:].rearrange("p (c l) -> p c l", l=64)
                    self.cp(ac[:].unsqueeze(2), b3[:, :, e:e + 1], [X[2]], ac, eng="dve")
                    self.cp(cme[:].unsqueeze(2), c3[:, :, e:e + 1], [X[1]], cme, eng="dve")
                    self.tt(cmx[:], ac[:], cme[:], ALU.add, [ac, cme], cmx)

                    def Vc(ap):
                        return ap[:, ::-1] if rev else ap
                    self.scan(Vc(msq[:]), Vc(ac[:]), Vc(cmx[:]), m0[:, 0:1], ALU.add, ALU.max, [ac, cmx, m0], msq)
                    if NCH > 1:
                        if rev:
                            self.cp(mpv[:, 0:NCH - 1], msq[:, 1:NCH], [msq], mpv, eng="dve")
                        else:
                            self.cp(mpv[:, 1:NCH], msq[:, 0:NCH - 1], [msq], mpv, eng="dve")
                    pe_ = NCH - 1 if rev else 0
                    self.cp(mpv[:, pe_:pe_ + 1], m0[:, 0:1], [m0], mpv, eng="dve")
                    if not isS:
                        fe = 0 if rev else NCH - 1
                        self.dma(self.o_mlm[seq, l, dr], msq[:, fe:fe + 1], [msq])
                    mpb = mpv[:].unsqueeze(2).to_broadcast([4, NCH, 64])
                    r3 = rrow[:].rearrange("p (c l) -> p c l", l=64)
                    self.tt(c3, c3, mpb, ALU.max, [X[1], mpv], X[1])
                    self.ts(rrow[:], X[1][:], -1.0, ALU.mult, [X[1]], rrow)
                    x3 = X[3][:].rearrange("p (c l) -> p c l", l=64)
                    self.tt(x3, r3, mpb, ALU.add, [rrow, mpv], X[3])
                    self.act(X[3][:], X[3][:], AF.Exp, [X[3]], X[3])
                    self.tt(X[2][:], rrow[:], X[2][:], ALU.subtract, [rrow, X[2]], X[2])
                    self.act(X[2][:], X[2][:], AF.Exp, [X[2]], X[2])
                    self.tt(cc[:], ac[:], msq[:], ALU.subtract, [ac, msq], cc)
                    x4 = X[4][:].rearrange("p (c l) -> p c l", l=64)
                    self.tt(x4, X[0][:].rearrange("p (c l) -> p c l", l=64), cc[:].unsqueeze(2).to_broadcast([4, NCH, 64]),
                            ALU.add, [X[0], cc], X[4])
                    self.act(X[4][:], X[4][:], AF.Exp, [X[4]], X[4])
                    self.tt(wc[:], cc[:], mpv[:], ALU.add, [cc, mpv], wc)
                    self.act(wc[:], wc[:], AF.Exp, [wc], wc)
                    for hh in range(4):
                        pb = self.bank()
                        self.mm(pb[:, 0:NCH], self.sel[:, hh, :], wc[:], True, True, [self.sel, wc], pb)
                        self.cp(wcb[:, hh, :], pb[:, 0:NCH], [pb], wcb)
                    for qi, src in enumerate((X[0], X[3], X[2], X[4])):
                        self.dma(X16[4 * qi:4 * qi + 4, :], src[:], [src], w=X16)
                    for g in range((NCH + 31) // 32):
                        m = min(32, NCH - g * 32)
                        pb = self.bank()
                        for i in range(m):
                            c = g * 32 + i
                            self.mm(pb[0:64, i * 16:(i + 1) * 16], X16[:, c * 64:(c + 1) * 64], self.identf[0:16, 0:16], True, True,
                                    [X16, self.identf], pb)
                        self.cp(XT[:, g * 32:g * 32 + m, :], pb[0:64, 0:m * 16].rearrange("p (c q) -> p c q", q=16), [pb], XT)
                with self.pool() as A:
                    CT = [A.t([128, 2, 257]) for _ in range(4)]; CTb = [A.t([128, 2, 257], BF16) for _ in range(4)]
                    for hh in range(4):
                        if isS:
                            self.dma(CT[hh][:, :, 0:256], self.mlc0T[l, dr, hh].rearrange("(k p) v -> p k v", p=128), w=CT[hh])
                            n0t = A.t([128, 2])
                            self.dma(n0t[:], self.mln0[l, dr, hh], w=n0t)
                            self.cp(CT[hh][:, :, 256], n0t[:], [n0t], CT[hh], eng="dve")
                        else:
                            self.memset(CT[hh][:], 0.0, CT[hh])
                        self.cp(CTb[hh][:], CT[hh][:], [CT[hh]], CTb[hh])
                    NB = 2
                    qc_ = [A.t([128, 8, 64], BF16) for _ in range(NB)]; kc_ = [A.t([128, 8, 64], BF16) for _ in range(NB)]
                    kt_ = [A.t([64, 1024], BF16) for _ in range(NB)]; va_ = [A.t([64, 4, 257], BF16) for _ in range(NB)]
                    for v in va_:
                        self.memset(v[:, :, 256:257], 1.0, v)
                    Dt = [A.t([64, 64]) for _ in range(4)]; SD = [A.t([64, 64], BF16) for _ in range(4)]
                    isb = [A.t([64, 257]) for _ in range(4)]; nd = [A.t([64, 257]) for _ in range(4)]
                    dn = [A.t([64, 2]) for _ in range(4)]; wv = [A.t([64, 257], BF16) for _ in range(4)]
                    hst = [A.t([64, 1024]) for _ in range(2)]
                    mask = self.maskU if rev else self.maskL
                    qv = self.q_mlT.rearrange("(j p) t -> p j t", p=128); kv = self.k_mlT.rearrange("(j p) t -> p j t", p=128)
                    order = range(NCH - 1, -1, -1) if rev else range(NCH)
                    for ci, c in enumerate(order):
                        bb = ci % NB
                        t0 = tok0 + c * 64
                        Q = qc_[bb]; K = kc_[bb]; KT = kt_[bb]; VA = va_[bb]; H = hst[ci % 2]
                        self.dma(Q[:], qv[:, :, t0:t0 + 64], w=Q); self.dma(K[:], kv[:, :, t0:t0 + 64], w=K)
                        self.dma(KT[:], self.k_ml[t0:t0 + 64, :], w=KT)
                        self.dma(VA[:, :, 0:256], self.v_ml[t0:t0 + 64, :].rearrange("s (h v) -> s h v", h=4), w=VA)
                        PA = [self.ps[2 * hh] for hh in range(4)]; PB = [self.ps[2 * hh + 1] for hh in range(4)]
                        for hh in range(4):
                            pa = PA[hh]; pbk = PB[hh]
                            for k2 in range(2):
                                self.mm(pa[0:64, 0:64], K[:, hh * 2 + k2, :], Q[:, hh * 2 + k2, :], k2 == 0, k2 == 1, [K, Q], pa)
                            self.mm(pbk[0:64, 0:64], self.sel[:, hh, 0:64], rrow[:, c * 64:(c + 1) * 64], True, False, [self.sel, rrow], pbk)
                            self.mm(pbk[0:64, 0:64], self.identf[0:64, 0:64], mask[:], False, True, [self.identf, mask], pbk)
                        for hh in range(4):
                            pa = PA[hh]; pbk = PB[hh]
                            self.act(Dt[hh][:], pbk[0:64, 0:64], AF.Exp, [pbk, XT], Dt[hh], bias=XT[:, c, hh:hh + 1])
                            self.tt(SD[hh][:], pa[0:64, 0:64], Dt[hh][:], ALU.mult, [pa, Dt[hh]], SD[hh])
                            self.act(wv[hh][:], VA[:, hh, :], AF.Copy, [VA, XT], wv[hh], scale=XT[:, c, 12 + hh:13 + hh])
                        for hh in range(4):
                            pa = PA[hh]; pbk = PB[hh]
                            self.mm(pbk[0:64, 0:257], SD[hh][:], VA[:, hh, :], True, True, [SD[hh], VA], pbk)
                            for k2 in range(2):
                                self.mm(pa[0:64, 0:257], Q[:, hh * 2 + k2, :], CTb[hh][:, k2, :], k2 == 0, k2 == 1, [Q, CTb[hh]], pa)
                        for hh in range(4):
                            pa = PA[hh]; pbk = PB[hh]
                            self.cp(isb[hh][:], pbk[0:64, 0:257], [pbk], isb[hh], eng="act")
                            self.stt(nd[hh][:], pa[0:64, 0:257], XT[:, c, 4 + hh:5 + hh], isb[hh][:], ALU.mult, ALU.add, [pa, XT, isb[hh]], nd[hh])
                        for hh in range(4):
                            pa = PA[hh]; pbk = PB[hh]
                            for k2, pp in enumerate((pa, pbk)):
                                self.mm(pp[:, 0:257], KT[:, hh * 256 + k2 * 128: hh * 256 + (k2 + 1) * 128], wv[hh][:], True, True, [KT, wv[hh]], pp)
                        for hh in range(4):
                            pa = PA[hh]; pbk = PB[hh]
                            for k2, pp in enumerate((pa, pbk)):
                                self.stt(CT[hh][:, k2, :], CT[hh][:, k2, :], wcb[:, hh, c:c + 1], pp[:, 0:257], ALU.mult, ALU.add,
                                         [CT[hh], wcb, pp], CT[hh])
                            self.cp(CTb[hh][:], CT[hh][:], [CT[hh]], CTb[hh], eng="pool")
                        for hh in range(4):
                            self.stt(dn[hh][:, 0:1], nd[hh][:, 256:257], -1.0, nd[hh][:, 256:257], ALU.mult, ALU.max, [nd[hh]], dn[hh])
                            self.ts(dn[hh][:, 0:1], dn[hh][:, 0:1], XT[:, c, 8 + hh:9 + hh], ALU.max, [dn[hh], XT], dn[hh])
                            self.recip(dn[hh][:, 1:2], dn[hh][:, 0:1], [dn[hh]], dn[hh])
                            self.ts(H[:, hh * 256:(hh + 1) * 256], nd[hh][:, 0:256], dn[hh][:, 1:2], ALU.mult, [nd[hh], dn[hh]], H)
                        self.dma(self.hfb[dr, t0:t0 + 64, :], H[:], [H])
                    if not isS:
                        for hh in range(4):
                            self.dma(self.o_mlc[seq, l, dr, hh].rearrange("(k p) v -> p k v", p=128), CT[hh][:, :, 0:256], [CT[hh]])
                            n1t = A.t([128, 2])
                            self.cp(n1t[:], CT[hh][:, :, 256], [CT[hh]], n1t, eng="dve")
                            self.dma(self.o_mln[seq, l, dr, hh], n1t[:], [n1t])

    def phase_mlstm_post(self, l, tok0, ntok):
        with self.pool() as A:
            mn = A.t([128, 8]); self.dma(mn[:], self.mlnormT[l], w=mn)
            hf = [A.t([128, 1024]) for _ in range(2)]; hb = [A.t([128, 1024]) for _ in range(2)]
            junk = A.t([128, 256], BF16); ss = A.t([128, 4]); hn = A.t([128, 4, 1024], BF16)
            ot = [A.t([128, 512], BF16) for _ in range(2)]; yo = [A.t([128, 512], BF16) for _ in range(2)]
            for g in range((ntok + 511) // 512):
                n = min(512, ntok - g * 512); nt = n // 128
                for t in range(nt):
                    r0 = tok0 + g * 512 + t * 128
                    a = hf[t % 2]; b = hb[t % 2]
                    self.dma(a[:], self.hfb[0, r0:r0 + 128, :], w=a); self.dma(b[:], self.hfb[1, r0:r0 + 128, :], w=b)
                    self.tt(a[:], a[:], b[:], ALU.add, [a, b], a, eng="pool")
                    for hh in range(4):
                        self.act(junk[:], a[:, hh * 256:(hh + 1) * 256], AF.Square, [a], [junk, ss], accum=ss[:, hh:hh + 1])
                    self.act(ss[:], ss[:], AF.Sqrt, [ss, self.eps], ss, scale=1.0 / 256, bias=self.eps[:, 0:1])
                    self.recip(ss[:], ss[:], [ss], ss)
                    for hh in range(4):
                        self.ts(hn[:, t, hh * 256:(hh + 1) * 256], a[:, hh * 256:(hh + 1) * 256], ss[:, hh:hh + 1], ALU.mult, [a, ss], hn)
                for j in range(8):
                    pb = self.bank(); pv = pb[:].bitcast(BF16)
                    for t in range(nt):
                        self.tr(pv[:, t * 128:(t + 1) * 128], hn[:, t, j * 128:(j + 1) * 128], self.identb[:], [hn, self.identb], pb, t == 0)
                    o = ot[j % 2]; y = yo[j % 2]
                    c0 = tok0 + g * 512
                    self.dma(o[:, 0:n], self.o_mlT[j * 128:(j + 1) * 128, c0:c0 + n], w=o)
                    self.stt(y[:, 0:n], pv[:, 0:n], mn[:, j:j + 1], o[:, 0:n], ALU.mult, ALU.mult, [pb, mn, o], y)
                    self.dma(self.yT[0, j * 128:(j + 1) * 128, c0:c0 + n], y[:, 0:n], [y])

    def phase_s5(self, l, tok0, T, isS, seq):
        TB = min(512, T)
        NBK = T // TB
        for dr in range(2):
            rev = (dr == 1)
            with self.pool() as O:
                bbr = O.t([32, 4096], BF16); bbi = O.t([32, 4096], BF16)
                cre = O.t([128, 32, 32], BF16); cim = O.t([128, 32, 32], BF16)
                mag = O.t([128, 32]); E1c = O.t([128, 32]); E1s = O.t([128, 32])
                ETc = O.t([128, 32]); ETs = O.t([128, 32])
                self.dma(cre[:], self.s5_creT[l, dr], w=cre, eng="pool"); self.dma(cim[:], self.s5_cimT[l, dr], w=cim, eng="pool")

                def disc(A, npart, nfree, src_re, src_im, src_dt, bc, c0=0):
                    t = {k: A.t([npart, nfree]) for k in ("lr", "li", "dt", "mag", "cs", "sn", "a", "b", "c", "qr", "qi")}
                    for k, s in (("lr", src_re), ("li", src_im), ("dt", src_dt)):
                        self.dma(t[k][:], s[:, c0:c0 + nfree].partition_broadcast(npart) if bc else s, w=t[k])
                    T_ = lambda k: t[k][:]
                    self.ts(T_("lr"), T_("lr"), -1e-4, ALU.min, [t["lr"]], t["lr"])
                    self.act(T_("dt"), T_("dt"), AF.Exp, [t["dt"]], t["dt"])
                    self.tt(T_("a"), T_("dt"), T_("lr"), ALU.mult, [t["dt"], t["lr"]], t["a"])
                    self.act(T_("mag"), T_("a"), AF.Exp, [t["a"]], t["mag"])
                    self.tt(T_("a"), T_("dt"), T_("li"), ALU.mult, [t["dt"], t["li"]], t["a"])
                    self.ts(T_("a"), T_("a"), 1.0 / 64, ALU.mult, [t["a"]], t["a"])
                    self.tt(T_("b"), T_("a"), T_("a"), ALU.mult, [t["a"]], t["b"])
                    self.ts(T_("c"), T_("b"), -1.0 / 42, ALU.mult, [t["b"]], t["c"], s2=1.0, op1=ALU.add)
                    self.tt(T_("c"), T_("c"), T_("b"), ALU.mult, [t["c"], t["b"]], t["c"])
                    self.ts(T_("c"), T_("c"), -1.0 / 20, ALU.mult, [t["c"]], t["c"], s2=1.0, op1=ALU.add)
                    self.tt(T_("c"), T_("c"), T_("b"), ALU.mult, [t["c"], t["b"]], t["c"])
                    self.ts(T_("c"), T_("c"), -1.0 / 6, ALU.mult, [t["c"]], t["c"], s2=1.0, op1=ALU.add)
                    self.tt(T_("sn"), T_("c"), T_("a"), ALU.mult, [t["c"], t["a"]], t["sn"])
                    self.ts(T_("c"), T_("b"), -1.0 / 56, ALU.mult, [t["b"]], t["c"], s2=1.0, op1=ALU.add)
                    self.tt(T_("c"), T_("c"), T_("b"), ALU.mult, [t["c"], t["b"]], t["c"])
                    self.ts(T_("c"), T_("c"), -1.0 / 30, ALU.mult, [t["c"]], t["c"], s2=1.0, op1=ALU.add)
                    self.tt(T_("c"), T_("c"), T_("b"), ALU.mult, [t["c"], t["b"]], t["c"])
                    self.ts(T_("c"), T_("c"), -1.0 / 12, ALU.mult, [t["c"]], t["c"], s2=1.0, op1=ALU.add)
                    self.tt(T_("c"), T_("c"), T_("b"), ALU.mult, [t["c"], t["b"]], t["c"])
                    self.ts(T_("cs"), T_("c"), -0.5, ALU.mult, [t["c"]], t["cs"], s2=1.0, op1=ALU.add)
                    for _ in range(6):
                        self.tt(T_("a"), T_("cs"), T_("cs"), ALU.mult, [t["cs"]], t["a"])
                        self.tt(T_("b"), T_("sn"), T_("sn"), ALU.mult, [t["sn"]], t["b"])
                        self.tt(T_("c"), T_("sn"), T_("cs"), ALU.mult, [t["sn"], t["cs"]], t["c"])
                        self.tt(T_("cs"), T_("a"), T_("b"), ALU.subtract, [t["a"], t["b"]], t["cs"])
                        self.ts(T_("sn"), T_("c"), 2.0, ALU.mult, [t["c"]], t["sn"])
                    return t

                for hc in range(2):
                  c0 = hc * 2048
                  with self.pool() as A:
                    t = disc(A, 32, 2048, self.s5_are[l, dr], self.s5_aim[l, dr], self.s5_ldt[l, dr], True, c0)
                    T_ = lambda k: t[k][:]
                    br = A.t([32, 2048]); bi = A.t([32, 2048])
                    self.dma(br[:], self.s5_breT[l, dr][:, c0:c0 + 2048], w=br); self.dma(bi[:], self.s5_bimT[l, dr][:, c0:c0 + 2048], w=bi)
                    self.tt(T_("cs"), T_("cs"), T_("mag"), ALU.mult, [t["cs"], t["mag"]], t["cs"])
                    self.tt(T_("sn"), T_("sn"), T_("mag"), ALU.mult, [t["sn"], t["mag"]], t["sn"])
                    self.ts(T_("cs"), T_("cs"), -1.0, ALU.add, [t["cs"]], t["cs"])
                    self.tt(T_("a"), T_("lr"), T_("lr"), ALU.mult, [t["lr"]], t["a"])
                    self.tt(T_("b"), T_("li"), T_("li"), ALU.mult, [t["li"]], t["b"])
                    self.tt(T_("a"), T_("a"), T_("b"), ALU.add, [t["a"], t["b"]], t["a"])
                    self.recip(T_("a"), T_("a"), [t["a"]], t["a"])
                    self.tt(T_("b"), T_("cs"), T_("lr"), ALU.mult, [t["cs"], t["lr"]], t["b"])
                    self.tt(T_("c"), T_("sn"), T_("li"), ALU.mult, [t["sn"], t["li"]], t["c"])
                    self.tt(T_("b"), T_("b"), T_("c"), ALU.add, [t["b"], t["c"]], t["b"])
                    self.tt(T_("qr"), T_("b"), T_("a"), ALU.mult, [t["b"], t["a"]], t["qr"])
                    self.tt(T_("b"), T_("sn"), T_("lr"), ALU.mult, [t["sn"], t["lr"]], t["b"])
                    self.tt(T_("c"), T_("cs"), T_("li"), ALU.mult, [t["cs"], t["li"]], t["c"])
                    self.tt(T_("b"), T_("b"), T_("c"), ALU.subtract, [t["b"], t["c"]], t["b"])
                    self.tt(T_("qi"), T_("b"), T_("a"), ALU.mult, [t["b"], t["a"]], t["qi"])
                    self.tt(T_("a"), T_("qr"), br[:], ALU.mult, [t["qr"], br], t["a"])
                    self.tt(T_("b"), T_("qi"), bi[:], ALU.mult, [t["qi"], bi], t["b"])
                    self.tt(bbr[:, c0:c0 + 2048], T_("a"), T_("b"), ALU.subtract, [t["a"], t["b"]], bbr)
                    self.tt(T_("a"), T_("qr"), bi[:], ALU.mult, [t["qr"], bi], t["a"])
                    self.tt(T_("b"), T_("qi"), br[:], ALU.mult, [t["qi"], br], t["b"])
                    self.tt(bbi[:, c0:c0 + 2048], T_("a"), T_("b"), ALU.add, [t["a"], t["b"]], bbi)
                O2 = self.pool(); O2.__enter__()
                tc_ = O2.t([128, 32, TB]); ts_ = O2.t([128, 32, TB])
                with self.pool() as A:
                    t = disc(A, 128, 32, self.s5_areS[l, dr], self.s5_aimS[l, dr], self.s5_ldtS[l, dr], False)
                    self.cp(mag[:], t["mag"][:], [t["mag"]], mag, eng="dve")
                    self.cp(E1c[:], t["cs"][:], [t["cs"]], E1c, eng="dve"); self.cp(E1s[:], t["sn"][:], [t["sn"]], E1s, eng="dve")
                    self.memset(tc_[:, :, 0:1], 1.0, tc_); self.memset(ts_[:, :, 0:1], 0.0, ts_)
                    pc = A.t([128, 32]); psn = A.t([128, 32]); a = A.t([128, 32]); b = A.t([128, 32])
                    w1 = A.t([128, 8, TB // 2]); w2 = A.t([128, 8, TB // 2])
                    self.cp(pc[:], t["cs"][:], [t["cs"]], pc, eng="dve"); self.cp(psn[:], t["sn"][:], [t["sn"]], psn, eng="dve")
                    n = 1
                    while n < TB:
                        for jh in range(4):
                            js = slice(jh * 8, (jh + 1) * 8)
                            pcb = pc[:, js].unsqueeze(2).to_broadcast([128, 8, n]); psb = psn[:, js].unsqueeze(2).to_broadcast([128, 8, n])
                            self.tt(w1[:, :, 0:n], tc_[:, js, 0:n], pcb, ALU.mult, [tc_, pc], w1)
                            self.tt(w2[:, :, 0:n], ts_[:, js, 0:n], psb, ALU.mult, [ts_, psn], w2)
                            self.tt(tc_[:, js, n:2 * n], w1[:, :, 0:n], w2[:, :, 0:n], ALU.subtract, [w1, w2], tc_)
                            self.tt(w1[:, :, 0:n], tc_[:, js, 0:n], psb, ALU.mult, [tc_, psn], w1)
                            self.tt(w2[:, :, 0:n], ts_[:, js, 0:n], pcb, ALU.mult, [ts_, pc], w2)
                            self.tt(ts_[:, js, n:2 * n], w1[:, :, 0:n], w2[:, :, 0:n], ALU.add, [w1, w2], ts_)
                        self.tt(a[:], pc[:], pc[:], ALU.mult, [pc], a); self.tt(b[:], psn[:], psn[:], ALU.mult, [psn], b)
                        self.tt(b[:], a[:], b[:], ALU.subtract, [a, b], b)
                        self.tt(a[:], pc[:], psn[:], ALU.mult, [pc, psn], a)
                        self.ts(psn[:], a[:], 2.0, ALU.mult, [a], psn)
                        self.cp(pc[:], b[:], [b], pc, eng="dve")
                        n *= 2
                    self.cp(ETc[:], pc[:], [pc], ETc, eng="dve"); self.cp(ETs[:], psn[:], [psn], ETs, eng="dve")
                with self.pool() as A:
                    inr = A.t([128, 32]); ini = A.t([128, 32]); h0r = A.t([128, 32]); h0i = A.t([128, 32]); a = A.t([128, 32]); b = A.t([128, 32])
                    if isS:
                        self.dma(h0r[:], self.s5h0r[l, dr], w=h0r); self.dma(h0i[:], self.s5h0i[l, dr], w=h0i)
                        self.tt(a[:], h0r[:], E1c[:], ALU.mult, [h0r, E1c], a); self.tt(b[:], h0i[:], E1s[:], ALU.mult, [h0i, E1s], b)
                        self.tt(inr[:], a[:], b[:], ALU.subtract, [a, b], inr)
                        self.tt(a[:], h0r[:], E1s[:], ALU.mult, [h0r, E1s], a); self.tt(b[:], h0i[:], E1c[:], ALU.mult, [h0i, E1c], b)
                        self.tt(ini[:], a[:], b[:], ALU.add, [a, b], ini)
                    else:
                        self.memset(inr[:], 0.0, inr); self.memset(ini[:], 0.0, ini)
                    dcar = [Dep() for _ in range(32)]
                    self.P.barrier()
                    uT = [A.t([32, TB], BF16) for _ in range(4)]
                    zr = [A.t([128, TB]) for _ in range(2)]; zi = [A.t([128, TB]) for _ in range(2)]
                    w = [A.t([128, TB]) for _ in range(4)]
                    gr = [A.t([128, TB]) for _ in range(2)]; gi = [A.t([128, TB]) for _ in range(2)]
                    hr = [A.t([128, TB], BF16) for _ in range(2)]; hi = [A.t([128, TB], BF16) for _ in range(2)]
                    ys = [A.t([32, TB]) for _ in range(2)]
                    sm = [A.t([128, 4]) for _ in range(2)]
                    fr = A.t([128, 32]); fi = A.t([128, 32])
                    R = (lambda ap: ap[:, ::-1]) if rev else (lambda ap: ap)
                    ui = 0
                    wd = [A.t([128, TB]) for _ in range(4)]
                    units = [(bk, j) for bk in (range(NBK - 1, -1, -1) if rev else range(NBK)) for j in range(32)]

                    def s5_front(idx):
                        bk, j = units[idx]; k = idx % 2; c0 = bk * TB
                        U = uT[idx % 4]
                        self.dma(U[:], self.s5uT[32 * j:32 * j + 32, tok0 + c0:tok0 + c0 + TB], w=U)
                        Cj = tc_[:, j, :]; Sj = ts_[:, j, :]
                        pr = self.bank(0, 3); pi_ = self.bank(3, 6)
                        self.mm(pr[:, 0:TB], bbr[:, j * 128:(j + 1) * 128], U[:, 0:TB], True, True, [bbr, U], pr)
                        self.mm(pi_[:, 0:TB], bbi[:, j * 128:(j + 1) * 128], U[:, 0:TB], True, True, [bbi, U], pi_)
                        self.tt(w[0][:], pr[:, 0:TB], R(Cj), ALU.mult, [pr, tc_], w[0])
                        self.tt(w[1][:], pi_[:, 0:TB], R(Sj), ALU.mult, [pi_, ts_], w[1])
                        self.tt(w[2][:], pi_[:, 0:TB], R(Cj), ALU.mult, [pi_, tc_], w[2])
                        self.tt(w[3][:], pr[:, 0:TB], R(Sj), ALU.mult, [pr, ts_], w[3])
                        self.tt(zr[k][:], w[0][:], w[1][:], ALU.add, [w[0], w[1]], zr[k], eng="pool")
                        self.tt(zi[k][:], w[2][:], w[3][:], ALU.subtract, [w[2], w[3]], zi[k], eng="pool")

                    def s5_back(idx):
                        bk, j = units[idx]; k = idx % 2; c0 = bk * TB
                        Cj = tc_[:, j, :]; Sj = ts_[:, j, :]
                        mb = mag[:, j:j + 1].to_broadcast([128, TB])
                        self.scan(R(gr[k][:]), mb, R(zr[k][:]), inr[:, j:j + 1], ALU.mult, ALU.add, [mag, zr[k], dcar[j]], gr[k])
                        self.scan(R(gi[k][:]), mb, R(zi[k][:]), ini[:, j:j + 1], ALU.mult, ALU.add, [mag, zi[k], dcar[j]], gi[k])
                        le = 0 if rev else TB - 1
                        glr = gr[k][:, le:le + 1]; gli = gi[k][:, le:le + 1]
                        s = sm[k]
                        self.tt(s[:, 0:1], glr, ETc[:, j:j + 1], ALU.mult, [gr[k], ETc], s, eng="pool")
                        self.tt(s[:, 1:2], gli, ETs[:, j:j + 1], ALU.mult, [gi[k], ETs], s, eng="pool")
                        self.tt(s[:, 2:3], glr, ETs[:, j:j + 1], ALU.mult, [gr[k], ETs], s, eng="pool")
                        self.tt(s[:, 3:4], gli, ETc[:, j:j + 1], ALU.mult, [gi[k], ETc], s, eng="pool")
                        self.tt(inr[:, j:j + 1], s[:, 0:1], s[:, 1:2], ALU.subtract, [s], dcar[j], eng="pool")
                        self.tt(ini[:, j:j + 1], s[:, 2:3], s[:, 3:4], ALU.add, [s], dcar[j], eng="pool")
                        self.tt(wd[0][:], gr[k][:], R(Cj), ALU.mult, [gr[k], tc_], wd[0])
                        self.tt(wd[1][:], gi[k][:], R(Sj), ALU.mult, [gi[k], ts_], wd[1], eng="pool")
                        self.tt(wd[2][:], gi[k][:], R(Cj), ALU.mult, [gi[k], tc_], wd[2])
                        self.tt(wd[3][:], gr[k][:], R(Sj), ALU.mult, [gr[k], ts_], wd[3], eng="pool")
                        self.tt(hr[k][:], wd[0][:], wd[1][:], ALU.subtract, [wd[0], wd[1]], hr[k], eng="pool")
                        self.stt(hi[k][:], wd[2][:], -1.0, wd[3][:], ALU.mult, ALU.subtract, [wd[2], wd[3]], hi[k])
                        py = self.bank(6, 8)
                        self.mm(py[0:32, 0:TB], cre[:, j, :], hr[k][:], True, False, [cre, hr[k]], py)
                        self.mm(py[0:32, 0:TB], cim[:, j, :], hi[k][:], False, True, [cim, hi[k]], py)
                        y = ys[k]
                        self.cp(y[:], py[0:32, 0:TB], [py], y, eng="act")
                        self.dma(self.ys5[dr, 32 * j:32 * j + 32, tok0 + c0:tok0 + c0 + TB], y[:], [y])
                        if (not isS) and ((bk == 0) if rev else (bk == NBK - 1)):
                            cl = tc_[:, j, TB - 1:TB]; sl = ts_[:, j, TB - 1:TB]
                            self.tt(s[:, 0:1], glr, cl, ALU.mult, [gr[k], tc_], s, eng="pool")
                            self.tt(s[:, 1:2], gli, sl, ALU.mult, [gi[k], ts_], s, eng="pool")
                            self.tt(s[:, 2:3], glr, sl, ALU.mult, [gr[k], ts_], s, eng="pool")
                            self.tt(s[:, 3:4], gli, cl, ALU.mult, [gi[k], tc_], s, eng="pool")
                            self.tt(fr[:, j:j + 1], s[:, 0:1], s[:, 1:2], ALU.subtract, [s], fr, eng="pool")
                            self.tt(fi[:, j:j + 1], s[:, 2:3], s[:, 3:4], ALU.add, [s], fi, eng="pool")

                    for idx in range(len(units) + 1):
                        if idx < len(units):
                            s5_front(idx)
                        if idx >= 1:
                            s5_back(idx - 1)
                    if not isS:
                        self.dma(self.o_s5r[seq, l, dr], fr[:], [fr]); self.dma(self.o_s5i[seq, l, dr], fi[:], [fi])
                O2.__exit__(None, None, None)

    def phase_s5_post(self, l, tok0, ntok):
        with self.pool() as A:
            sd = A.t([128, 8]); self.dma(sd[:], self.s5_dT[l], w=sd)
            wg = A.t([128, 8, 1024], BF16)
            self.load_w(wg, self.w_glu[l], [(0, 1024, 0)], 8)
            gss = A.t([128, 8, 512], BF16)
            ya = [A.t([128, 512]) for _ in range(2)]; yb = [A.t([128, 512]) for _ in range(2)]; ub = [A.t([128, 512], BF16) for _ in range(2)]
            w = [A.t([128, 512]) for _ in range(3)]; sg = A.t([128, 512]); yo = [A.t([128, 512], BF16) for _ in range(2)]
            for g in range((ntok + 511) // 512):
                n = min(512, ntok - g * 512); c0 = tok0 + g * 512
                for j in range(8):
                    a = ya[j % 2]; b = yb[j % 2]; u = ub[j % 2]
                    self.dma(a[:, 0:n], self.ys5[0, j * 128:(j + 1) * 128, c0:c0 + n], w=a)
                    self.dma(b[:, 0:n], self.ys5[1, j * 128:(j + 1) * 128, c0:c0 + n], w=b)
                    self.dma(u[:, 0:n], self.s5uT[j * 128:(j + 1) * 128, c0:c0 + n], w=u)
                    self.tt(a[:, 0:n], a[:, 0:n], b[:, 0:n], ALU.add, [a, b], a, eng="pool")
                    self.stt(a[:, 0:n], u[:, 0:n], sd[:, j:j + 1], a[:, 0:n], ALU.mult, ALU.add, [u, sd, a], a)
                    self.act(w[0][:, 0:n], a[:, 0:n], AF.Square, [a], w[0])
                    self.ts(w[0][:, 0:n], w[0][:, 0:n], 0.044715, ALU.mult, [w[0]], w[0], s2=1.0, op1=ALU.add)
                    self.tt(w[1][:, 0:n], w[0][:, 0:n], a[:, 0:n], ALU.mult, [w[0], a], w[1])
                    self.act(w[2][:, 0:n], w[1][:, 0:n], AF.Sigmoid, [w[1]], w[2], scale=1.5957691216057308)
                    self.tt(gss[:, j, 0:n], a[:, 0:n], w[2][:, 0:n], ALU.mult, [a, w[2]], gss)
                for j in range(8):
                    pb = self.bank()
                    for k in range(8):
                        self.mm(pb[:, 0:n], wg[:, k, j * 128:(j + 1) * 128], gss[:, k, 0:n], k == 0, k == 7, [wg, gss], pb)
                    self.act(sg[:, 0:n], pb[:, 0:n], AF.Sigmoid, [pb], sg)
                    y = yo[j % 2]
                    self.tt(y[:, 0:n], gss[:, j, 0:n], sg[:, 0:n], ALU.mult, [gss, sg], y)
                    self.dma(self.yT[2, j * 128:(j + 1) * 128, c0:c0 + n], y[:, 0:n], [y])

    def phase_merge(self, l, xsrc, xdst, tok0, ntok, cnd):
        ntile = ntok // 128
        nch = (ntok + 511) // 512
        with self.pool() as O:
            mT = O.t([128, 16, ntok], BF16)
            with self.pool() as A:
                yb = A.t([128, 4, 8, ntok], BF16)
                for b in range(4):
                    self.dma(yb[:, b], self.yT[b].rearrange("(k p) t -> p k t", p=128)[:, :, tok0:tok0 + ntok], w=yb)
                wts = [A.t([128, 8, 128], BF16) for _ in range(4)]
                gt = [A.t([128, 512], BF16) for _ in range(4)]
                acc = [A.t([128, 512]) for _ in range(2)]; tmp = [A.t([128, 512]) for _ in range(2)]
                i = 0
                for dc in range(16):
                    wl = []
                    for b in range(4):
                        w = wts[b]
                        self.load_w(w, self.w_br[l, b], [(dc * 128, 128, 0)], 8)
                        wl.append(w)
                    for j in range(nch):
                        n = min(512, ntok - j * 512)
                        a = acc[i % 2]; i += 1
                        for b in range(4):
                            pb = self.bank()
                            for k in range(8):
                                self.mm(pb[:, 0:n], wl[b][:, k, :], yb[:, b, k, j * 512:j * 512 + n], k == 0, k == 7, [wl[b], yb], pb)
                            g = gt[b]
                            self.dma(g[:, 0:n], self.gateT[b * 2048 + dc * 128: b * 2048 + (dc + 1) * 128, tok0 + j * 512: tok0 + j * 512 + n], w=g)
                            if b == 0:
                                self.tt(a[:, 0:n], pb[:, 0:n], g[:, 0:n], ALU.mult, [pb, g], a)
                            else:
                                t = tmp[b % 2]
                                self.tt(t[:, 0:n], pb[:, 0:n], g[:, 0:n], ALU.mult, [pb, g], t)
                                if b < 3:
                                    self.tt(a[:, 0:n], a[:, 0:n], t[:, 0:n], ALU.add, [a, t], a, eng="pool")
                                else:
                                    self.tt(mT[:, dc, j * 512:j * 512 + n], a[:, 0:n], t[:, 0:n], ALU.add, [a, t], mT, eng="pool")
            with self.pool() as A:
                gb = A.t([128, D])
                self.dma(gb[:], self.gsc[0, cnd:cnd + 1, :].partition_broadcast(128), w=gb)
                self.resid_out(A, mT, 16, self.w_o[l], xsrc, xdst, tok0, ntile, gb)

    def resid_out(self, A, aT, kch, wsrc, xsrc, xdst, tok0, ntile, gb):
        wts = [A.t([128, kch, 512], BF16) for _ in range(2)]
        xt = [A.t([128, 512]) for _ in range(3)]; tm = [A.t([128, 512]) for _ in range(2)]
        i = 0
        for nb in range(4):
            w = wts[nb % 2]
            self.load_w(w, wsrc, [(nb * 512, 512, 0)], kch)
            for t in range(ntile):
                r0 = tok0 + t * 128
                pb = self.bank()
                for k in range(kch):
                    self.mm(pb[:], aT[:, k, t * 128:(t + 1) * 128], w[:, k, :], k == 0, k == kch - 1, [aT, w], pb)
                x = xt[i % 3]; tt_ = tm[i % 2]; i += 1
                self.dma(x[:], xsrc[r0:r0 + 128, nb * 512:(nb + 1) * 512], w=x)
                self.tt(tt_[:], pb[:], gb[:, nb * 512:(nb + 1) * 512], ALU.mult, [pb, gb], tt_)
                self.tt(x[:], x[:], tt_[:], ALU.add, [x, tt_], x, eng="pool")
                self.dma(xdst[r0:r0 + 128, nb * 512:(nb + 1) * 512], x[:], [x])

    def phase_ffn(self, l, xbuf, tok0, ntok, cnd):
        ntile = ntok // 128
        nch = (ntok + 511) // 512
        HC = FFN_H // 128
        HH = HC // 2
        with self.pool() as O:
            hT = O.t([128, 16, ntok], BF16)
            with self.pool() as A0:
                self.norm_T(A0, xbuf, tok0, ntile, hT,
                            lambda kc: self.A2[:, cnd, kc:kc + 1], lambda kc: self.modT[:, 48 + kc, cnd:cnd + 1])
            gb = O.t([128, D])
            self.dma(gb[:], self.gsc[1, cnd:cnd + 1, :].partition_broadcast(128), w=gb)
            for half in range(2):
                with self.pool() as A:
                    aT = A.t([128, HH, ntok], BF16)
                    sa = [A.t([128, 512]) for _ in range(2)]
                    was = [A.t([128, 16, 128], BF16) for _ in range(3)]; wbs = [A.t([128, 16, 128], BF16) for _ in range(3)]
                    for hc in range(HH):
                        col = (half * HH + hc) * 128
                        wa = was[hc % 3]; wb = wbs[hc % 3]
                        self.load_w(wa, self.w_f1[l], [(col, 128, 0)]); self.load_w(wb, self.w_f1[l], [(FFN_H + col, 128, 0)])
                        for j in range(nch):
                            n = min(512, ntok - j * 512)
                            pa = self.bank(); pb = self.bank()
                            for k in range(16):
                                self.mm(pa[:, 0:n], wa[:, k, :], hT[:, k, j * 512:j * 512 + n], k == 0, k == 15, [wa, hT], pa)
                            for k in range(16):
                                self.mm(pb[:, 0:n], wb[:, k, :], hT[:, k, j * 512:j * 512 + n], k == 0, k == 15, [wb, hT], pb)
                            s = sa[j % 2]
                            self.act(s[:, 0:n], pa[:, 0:n], AF.Silu, [pa], s)
                            self.tt(aT[:, hc, j * 512:j * 512 + n], s[:, 0:n], pb[:, 0:n], ALU.mult, [s, pb], aT)
                    self.resid_out(A, aT, HH, self.w_f2[l][half * HH * 128:(half + 1) * HH * 128, :], xbuf, xbuf, tok0, ntile, gb)
                    self.P.barrier()

    def phase_final(self, xsrc, out, ntok):
        with self.pool() as A:
            fb = A.t([128, D]); self.dma(fb[:], self.fnorm.partition_broadcast(128), w=fb)
            xt = [A.t([128, D]) for _ in range(2)]; junk = A.t([128, D], BF16); ss = A.t([128, ntok // 128])
            for t in range(ntok // 128):
                x = xt[t % 2]
                self.dma(x[:], xsrc[t * 128:(t + 1) * 128, :], w=x)
                self.act(junk[:], x[:], AF.Square, [x], [junk, ss], accum=ss[:, t:t + 1])
                self.act(ss[:, t:t + 1], ss[:, t:t + 1], AF.Sqrt, [ss, self.eps], ss, scale=1.0 / D, bias=self.eps[:, 0:1])
                self.recip(ss[:, t:t + 1], ss[:, t:t + 1], [ss], ss)
                self.stt(x[:], x[:], ss[:, t:t + 1], fb[:], ALU.mult, ALU.mult, [x, ss, fb], x)
                self.dma(out[t * 128:(t + 1) * 128, :], x[:], [x])

    def build(self):
        import os
        c = self.cfg
        self.setup()
        lim = int(os.environ.get("MK_STOP", "100000"))
        self._pc = 0
        for nm in ("phase_ada", "phase_win", "phase_ctx", "phase_mlstm", "phase_s5", "phase_mla", "phase_diff",
                   "phase_mlstm_post", "phase_s5_post", "phase_merge", "phase_ffn", "phase_final"):
            def mk(fn, nm=nm):
                def w(*a, **k):
                    self._pc += 1
                    if self._pc > lim:
                        return
                    if os.environ.get("MK_VERBOSE"):
                        print("PHASE", self._pc, nm, flush=True)
                    return fn(*a, **k)
                return w
            setattr(self, nm, mk(getattr(self, nm)))
        for l in range(DEPTH):
            self.phase_ada(l)
            xin = self.xp if l == 0 else self.xcp
            self.phase_win(l, xin, 0, self.TPT, 0, False)
            for s in range(c.NPR):
                t0 = s * c.TP
                self.phase_mlstm(l, t0, c.TP, False, s)
                self.phase_s5(l, t0, c.TP, False, s)
                self.phase_mla(l, t0, c.TP, False)
                self.phase_diff(l, t0, c.TP, False)
            self.phase_mlstm_post(l, 0, self.TPT)
            self.phase_s5_post(l, 0, self.TPT)
            self.phase_merge(l, xin, self.xcp, 0, self.TPT, 0)
            self.phase_ffn(l, self.xcp, 0, self.TPT, 0)
            xin = self.xs if l == 0 else self.xcs
            for g0 in range(0, c.TS, c.GW):
                self.phase_win(l, xin, g0, min(c.GW, c.TS - g0), 1, True)
            self.phase_ctx(l)
            self.phase_mlstm(l, 0, c.TS, True, 0)
            self.phase_s5(l, 0, c.TS, True, 0)
            self.phase_mla(l, 0, c.TS, True)
            self.phase_diff(l, 0, c.TS, True)
            self.phase_mlstm_post(l, 0, c.TS)
            self.phase_s5_post(l, 0, c.TS)
            for g0 in range(0, c.TS, c.G):
                n = min(c.G, c.TS - g0)
                self.phase_merge(l, xin, self.xcs, g0, n, 1)
                self.phase_ffn(l, self.xcs, g0, n, 1)
        self.phase_final(self.xcp, self.o_yp, self.TPT)
        self.phase_final(self.xcs, self.o_ys, c.TS)
        self.P.emit()
        return self.nc


def _pT(a, n):
    sh = a.shape[:-1]
    return np.ascontiguousarray(np.swapaxes(a.reshape(sh + (n, 128)), -1, -2))


def _SL(a):
    sh = a.shape[:-2]
    b = a.reshape(sh + (32, 2, 64))
    nd = len(sh)
    b = np.transpose(b, tuple(range(nd)) + (nd + 1, nd + 2, nd))
    return np.ascontiguousarray(b.reshape(sh + (128, 32)))


def _unSL(a):
    sh = a.shape[:-2]
    b = a.reshape(sh + (2, 64, 32))
    nd = len(sh)
    b = np.transpose(b, tuple(range(nd)) + (nd + 2, nd, nd + 1))
    return np.ascontiguousarray(b.reshape(sh + (64, 64)))


def _consts(TS):
    f = np.float32
    c = {}
    c["c_ident"] = np.eye(128, dtype=f)
    c["c_ones"] = np.ones((128, 128), f)
    s = np.arange(64)[:, None]; t = np.arange(64)[None, :]
    c["c_maskL"] = np.where(s <= t, 0.0, NEG).astype(f)
    c["c_maskU"] = np.where(s >= t, 0.0, NEG).astype(f)
    sel = np.zeros((4, 4, 128), f)
    for h in range(4):
        sel[h, h, :] = 1.0
    c["c_sel"] = sel
    R = np.zeros((64, 64), f)
    for base in (0, 32):
        for j in range(16):
            R[base + j, base + j + 16] = -1.0
            R[base + j + 16, base + j] = 1.0
    R128 = np.zeros((128, 128), f)
    R128[:64, :64] = R; R128[64:, 64:] = R
    c["c_ropeR"] = np.ascontiguousarray(R128.T)
    inv = (np.float32(10000.0) ** (-np.arange(16, dtype=f) / np.float32(16))).astype(f)
    tt = np.arange(TS)
    rows = (tt // GRID_W).astype(f); cols = (tt % GRID_W).astype(f)
    C = np.zeros((128, TS), f); S = np.zeros((128, TS), f)
    for p in range(128):
        q = p % 64
        pos = rows if q < 32 else cols
        ang = (pos * inv[q % 16]).astype(f)
        C[p] = np.cos(ang); S[p] = np.sin(ang)
    c["c_ropeC"] = C; c["c_ropeS"] = S
    return c


def _weights(I):
    f = np.float32
    L = DEPTH
    w = {}
    w["w_ada"] = I["w_ada"]; w["b_adaT"] = _pT(I["b_ada"], 96)
    w["nmixT"] = _pT(I["norm_mix"], 16); w["nffnT"] = _pT(I["norm_ffn"], 16)
    w["w_in"] = I["w_in"]; w["mlifb"] = np.ascontiguousarray(np.swapaxes(I["ml_if_bias"].reshape(L, 4, 4), 1, 2))
    w["mlnormT"] = _pT(I["ml_norm"], 8); w["kvnorm"] = np.ascontiguousarray(I["mla_kv_norm"].reshape(L, 1, 512))
    w["w_kvb"] = I["mla_w_kvb"]
    w["s5_are"] = np.ascontiguousarray(I["s5_a_re"].reshape(L, 2, 1, 4096)); w["s5_aim"] = np.ascontiguousarray(I["s5_a_im"].reshape(L, 2, 1, 4096))
    ldt = np.repeat(I["s5_log_dt"][..., None], 64, axis=-1)
    w["s5_ldt"] = np.ascontiguousarray(ldt.reshape(L, 2, 1, 4096))
    w["s5_areS"] = _SL(I["s5_a_re"]); w["s5_aimS"] = _SL(I["s5_a_im"]); w["s5_ldtS"] = _SL(ldt)
    for nm, src in (("s5_breT", "s5_b_re"), ("s5_bimT", "s5_b_im")):
        b = I[src].reshape(L, 2, 32, 2, 64, 16)
        o = np.zeros((L, 2, 2, 16, 32, 2, 64), f)
        for g2 in range(2):
            o[:, :, g2, :, :, g2, :] = np.transpose(b[:, :, :, g2, :, :], (0, 1, 4, 2, 3))
        w[nm] = o.reshape(L, 2, 32, 4096)
    for nm, src in (("s5_creT", "s5_c_re"), ("s5_cimT", "s5_c_im")):
        cc = I[src].reshape(L, 2, 32, 2, 16, 64)
        o = np.zeros((L, 2, 2, 64, 32, 2, 16), f)
        for g2 in range(2):
            o[:, :, g2, :, :, g2, :] = np.transpose(cc[:, :, :, g2, :, :], (0, 1, 4, 2, 3))
        w[nm] = o.reshape(L, 2, 128, 32, 32)
    w["s5_dT"] = _pT(I["s5_d"], 8); w["w_glu"] = I["s5_w_glu"]
    w["df_lam"] = np.ascontiguousarray(I["df_lambda"].reshape(L, 1, 256)); w["dfnormT"] = _pT(I["df_norm"], 1)
    w["w_br"] = I["w_branch"]; w["w_o"] = I["w_o"]; w["w_f1"] = I["w_ffn_in"]; w["w_f2"] = I["w_ffn_out"]
    w["fnorm"] = np.ascontiguousarray(I["final_norm"].reshape(1, D))
    return w


def run(cfg, inputs, trace=False):
    I = {k: np.asarray(v) for k, v in inputs.items()}
    f = np.float32
    L = DEPTH
    NB = I["x_prompt"].shape[0]
    ncore = NB // cfg.NPR
    nsamp = I["x_sample"].shape[0]
    assert ncore == 2 * nsamp
    b = Builder(cfg)
    nc = b.build()
    shared = _weights(I)
    shared.update(_consts(cfg.TS))
    maps = []
    for c in range(ncore):
        s = c // 2
        m = dict(shared)
        m["xs"] = np.ascontiguousarray(I["x_sample"][s])
        m["xp"] = np.ascontiguousarray(I["x_prompt"][c * cfg.NPR:(c + 1) * cfg.NPR].reshape(cfg.NPR * cfg.TP, D))
        cond = np.stack([I["c_ctx"], I["c"][s]], 0)
        m["condT"] = np.ascontiguousarray(np.transpose(cond.T.reshape(16, 128, 2), (1, 0, 2)))
        m["ckv_c"] = np.ascontiguousarray(I["cache_mla_ckv"][s]); m["kr_c"] = np.ascontiguousarray(I["cache_mla_krope"][s])
        m["dk_c"] = np.ascontiguousarray(I["cache_diff_k"][s].reshape(L, cfg.PAST, 1024))
        m["dv_c"] = np.ascontiguousarray(I["cache_diff_v"][s].reshape(L, cfg.PAST, 1024))
        m["mlc0T"] = np.ascontiguousarray(np.swapaxes(I["state_mlstm_c"][s], -1, -2))
        m["mln0"] = np.ascontiguousarray(np.swapaxes(I["state_mlstm_n"][s].reshape(L, 2, 4, 2, 128), -1, -2)); m["mlm0"] = np.ascontiguousarray(I["state_mlstm_m"][s][..., None])
        m["s5h0r"] = _SL(I["state_s5_re"][s]); m["s5h0i"] = _SL(I["state_s5_im"][s])
        for k in list(m.keys()):
            if k not in b.din:
                raise KeyError(k)
            m[k] = np.ascontiguousarray(m[k], dtype=f)
            assert list(m[k].shape) == b.din[k][0], (k, m[k].shape, b.din[k][0])
        maps.append(m)
    res = run_bass_kernel_spmd(nc, maps, core_ids=list(range(ncore)), trace=trace)
    R = res.results
    TP, NPR, TS = cfg.TP, cfg.NPR, cfg.TS
    yp = np.concatenate([R[c]["o_yp"].reshape(NPR, TP, D) for c in range(ncore)], 0)
    hs = TS // 2
    ys = np.stack([np.concatenate([R[2 * s]["o_ys"][:hs], R[2 * s + 1]["o_ys"][hs:]], 0) for s in range(nsamp)], 0)

    def tokout(name, tail):
        return np.concatenate([np.transpose(R[c][name].reshape((L, NPR, TP) + tail), (1, 0, 2) + tuple(range(3, 3 + len(tail))))
                               for c in range(ncore)], 0)
    ckv = tokout("o_ckv", (512,)); kr = tokout("o_kr", (64,))
    dk = tokout("o_dk", (8, 128)); dv = tokout("o_dv", (8, 128))
    mlc = np.concatenate([np.swapaxes(R[c]["o_mlc"], -1, -2) for c in range(ncore)], 0)
    mln = np.concatenate([np.swapaxes(R[c]["o_mln"], -1, -2).reshape(NPR, L, 2, 4, 256) for c in range(ncore)], 0)
    mlm = np.concatenate([R[c]["o_mlm"][..., 0] for c in range(ncore)], 0)
    s5r = np.concatenate([_unSL(R[c]["o_s5r"]) for c in range(ncore)], 0)
    s5i = np.concatenate([_unSL(R[c]["o_s5i"]) for c in range(ncore)], 0)
    outs = (yp, ys, ckv, kr, dk, dv, mlc, mln, mlm, s5r, s5i)
    outs = tuple(np.ascontiguousarray(o, dtype=f) for o in outs)
    if trace:
        return outs, res
    return outs


def kernel(**inputs):
    return run(Cfg(), inputs)
```
